# Optimizing a Trainium2 kernel written in Bass

```python
import jax, jax.numpy as jnp
from jax import lax
import numpy as np

D_MODEL = 1024
BATCH = 8
SEQ = 2048
DEPTH = 2

GRID_W = 64
CTX_LEN = 256
D_FOURIER = 256
FOURIER_GROUPS = 4
D_CONV = 256
CONV_WIDTH = 31
N_NA_HEADS = 8
NA_HEAD_DIM = 64
D_NA = N_NA_HEADS * NA_HEAD_DIM
WIN_ROWS = 8
WIN_COLS = 16
QKV_START = D_FOURIER + 2 * D_CONV
KV_START = QKV_START + D_NA
D_IN = QKV_START + 3 * D_NA
D_MIX = D_FOURIER + D_CONV + D_NA
D_FF = 4 * D_MODEL
N_MOD = 6
RMS_EPS = 1e-6
LN_EPS = 1e-5

kernel_name = "hymba_style_fourier_conformer_natten_dit"


def rms_norm(x, g):
    xf = x.astype(jnp.float32)
    y = xf * lax.rsqrt(jnp.mean(xf * xf, axis=-1, keepdims=True) + RMS_EPS)
    return (y * g.astype(jnp.float32)).astype(x.dtype)


def layer_norm(x, g, b):
    xf = x.astype(jnp.float32)
    mu = jnp.mean(xf, axis=-1, keepdims=True)
    var = jnp.mean(jnp.square(xf - mu), axis=-1, keepdims=True)
    y = (xf - mu) * lax.rsqrt(var + LN_EPS)
    return (y * g.astype(jnp.float32) + b.astype(jnp.float32)).astype(x.dtype)


def modulate(x, shift, scale):
    return x * (1 + scale) + shift


def split_heads(u):
    b, l, _ = u.shape
    qkv = u[..., QKV_START:].reshape(b, l, 3, N_NA_HEADS, NA_HEAD_DIM)
    return u[..., :D_FOURIER], u[..., D_FOURIER:QKV_START], qkv[:, :, 0], qkv[:, :, 1], qkv[:, :, 2]


def fourier_mix(u, w):
    b, l, _ = u.shape
    ug = u.reshape(b, l, FOURIER_GROUPS, D_FOURIER // FOURIER_GROUPS).astype(jnp.float32)
    f = jnp.fft.fftn(ug, axes=(1, 3), norm="ortho").real
    return f.reshape(b, l, D_FOURIER).astype(u.dtype) @ w


def conv_module(u, dw_w, dw_b, ln_g, ln_b, pw_w, pw_b):
    a, gt = jnp.split(u, 2, axis=-1)
    v = a * jax.nn.sigmoid(gt)
    pad = CONV_WIDTH // 2
    v = lax.conv_general_dilated(v, dw_w[:, None, :].astype(v.dtype), window_strides=(1,), padding=[(pad, pad)],
                                 dimension_numbers=("NWC", "WIO", "NWC"), feature_group_count=D_CONV) + dw_b
    v = jax.nn.silu(layer_norm(v, ln_g, ln_b))
    return v @ pw_w + pw_b


def context_attention(q, k, v):
    s = jnp.einsum("bqhd,bkhd->bhqk", q, k).astype(jnp.float32) * (NA_HEAD_DIM ** -0.5)
    p = jax.nn.softmax(s, axis=-1).astype(v.dtype)
    o = jnp.einsum("bhqk,bkhd->bqhd", p, v)
    return o.reshape(q.shape[0], q.shape[1], D_NA)


def neighbourhood_attention(q, k, v, k_ctx, v_ctx, rpb):
    b, l, h, dh = q.shape
    rows = l // GRID_W
    kr = min(WIN_ROWS, rows)
    r = np.arange(rows)
    row_start = np.clip(r - kr // 2, 0, rows - kr)
    key_rows = row_start[:, None] + np.arange(kr)[None, :]
    col = np.arange(GRID_W)
    col_start = np.clip(col - WIN_COLS // 2, 0, GRID_W - WIN_COLS)
    col_valid = (col[None, :] >= col_start[:, None]) & (col[None, :] < col_start[:, None] + WIN_COLS)
    row_off = key_rows - r[:, None] + (WIN_ROWS - 1)
    col_off = np.clip(col[None, :] - col[:, None] + (WIN_COLS - 1), 0, 2 * WIN_COLS - 2)
    bias = rpb[:, row_off[:, :, None, None], col_off[None, None]].astype(jnp.float32)
    bias = jnp.where(jnp.asarray(col_valid)[None, None, None], bias, -jnp.inf)
    bias = bias.transpose(0, 1, 3, 2, 4).reshape(h, rows, GRID_W, kr * GRID_W)
    qg = q.reshape(b, rows, GRID_W, h, dh)
    kg = k.reshape(b, rows, GRID_W, h, dh)[:, key_rows].reshape(b, rows, kr * GRID_W, h, dh)
    vg = v.reshape(b, rows, GRID_W, h, dh)[:, key_rows].reshape(b, rows, kr * GRID_W, h, dh)
    scale = NA_HEAD_DIM ** -0.5
    s_loc = jnp.einsum("brqhd,brkhd->bhrqk", qg, kg).astype(jnp.float32) * scale + bias[None]
    s_ctx = jnp.einsum("brqhd,bchd->bhrqc", qg, k_ctx).astype(jnp.float32) * scale
    p = jax.nn.softmax(jnp.concatenate([s_loc, s_ctx], axis=-1), axis=-1).astype(v.dtype)
    n_loc = kr * GRID_W
    o = (jnp.einsum("bhrqk,brkhd->brqhd", p[..., :n_loc], vg)
         + jnp.einsum("bhrqc,bchd->brqhd", p[..., n_loc:], v_ctx))
    return o.reshape(b, l, D_NA)


def setup_inputs(seed: int = 0) -> dict:
    key = jax.random.key(seed)
    ks = jax.random.split(key, 24)

    def nrm(k, shape, scale):
        return jax.random.normal(k, shape, jnp.float32) * scale

    return {
        "x": nrm(ks[0], (BATCH, SEQ, D_MODEL), 1.0),
        "c": nrm(ks[1], (BATCH, D_MODEL), 1.0),
        "ctx": nrm(ks[2], (BATCH, CTX_LEN, D_MODEL), 1.0),
        "c_ctx": nrm(ks[3], (D_MODEL,), 1.0),
        "ada_w": nrm(ks[4], (DEPTH, D_MODEL, N_MOD * D_MODEL), D_MODEL ** -0.5),
        "ada_b": nrm(ks[5], (DEPTH, N_MOD * D_MODEL), 0.02),
        "norm1_g": 1.0 + nrm(ks[6], (DEPTH, D_MODEL), 0.05),
        "norm2_g": 1.0 + nrm(ks[7], (DEPTH, D_MODEL), 0.05),
        "w_in": nrm(ks[8], (DEPTH, D_MODEL, D_IN), D_MODEL ** -0.5),
        "w_fourier": nrm(ks[9], (DEPTH, D_FOURIER, D_FOURIER), D_FOURIER ** -0.5),
        "conv_dw_w": nrm(ks[10], (DEPTH, CONV_WIDTH, D_CONV), CONV_WIDTH ** -0.5),
        "conv_dw_b": nrm(ks[11], (DEPTH, D_CONV), 0.02),
        "conv_norm_g": 1.0 + nrm(ks[12], (DEPTH, D_CONV), 0.05),
        "conv_norm_b": nrm(ks[13], (DEPTH, D_CONV), 0.02),
        "conv_pw_w": nrm(ks[14], (DEPTH, D_CONV, D_CONV), D_CONV ** -0.5),
        "conv_pw_b": nrm(ks[15], (DEPTH, D_CONV), 0.02),
        "na_rpb": nrm(ks[16], (DEPTH, N_NA_HEADS, 2 * WIN_ROWS - 1, 2 * WIN_COLS - 1), 0.1),
        "w_out": nrm(ks[17], (DEPTH, D_MIX, D_MODEL), D_MIX ** -0.5),
        "mlp_w1": nrm(ks[18], (DEPTH, D_MODEL, D_FF), D_MODEL ** -0.5),
        "mlp_w2": nrm(ks[19], (DEPTH, D_FF, D_MODEL), D_FF ** -0.5),
        "final_norm_g": 1.0 + nrm(ks[20], (D_MODEL,), 0.05),
    }


def reference(x, c, ctx, c_ctx, ada_w, ada_b, norm1_g, norm2_g, w_in, w_fourier, conv_dw_w, conv_dw_b,
              conv_norm_g, conv_norm_b, conv_pw_w, conv_pw_b, na_rpb, w_out, mlp_w1, mlp_w2, final_norm_g):
    h_lat, h_ctx = x, ctx
    b = x.shape[0]
    n_ctx = ctx.shape[1]
    for i in range(DEPTH):
        last = i == DEPTH - 1
        conv_p = (conv_dw_w[i], conv_dw_b[i], conv_norm_g[i], conv_norm_b[i], conv_pw_w[i], conv_pw_b[i])
        mod = (jax.nn.silu(c) @ ada_w[i] + ada_b[i])[:, None, :]
        sh1, sc1, g1, sh2, sc2, g2 = jnp.split(mod, N_MOD, axis=-1)
        n_ctx_mod = (2 if last else N_MOD) * D_MODEL
        mod_c = (jax.nn.silu(c_ctx) @ ada_w[i][:, :n_ctx_mod] + ada_b[i][:n_ctx_mod])[None, None, :]
        mods_c = jnp.split(mod_c, n_ctx_mod // D_MODEL, axis=-1)

        hn_ctx = modulate(rms_norm(h_ctx, norm1_g[i]), mods_c[0], mods_c[1])
        if last:
            kv = (hn_ctx @ w_in[i][:, KV_START:]).reshape(b, n_ctx, 2, N_NA_HEADS, NA_HEAD_DIM)
            k_ctx, v_ctx = kv[:, :, 0], kv[:, :, 1]
        else:
            uf_c, uc_c, q_ctx, k_ctx, v_ctx = split_heads(hn_ctx @ w_in[i])
            mix_ctx = jnp.concatenate([fourier_mix(uf_c, w_fourier[i]), conv_module(uc_c, *conv_p),
                                       context_attention(q_ctx, k_ctx, v_ctx)], axis=-1)

        hn_lat = modulate(rms_norm(h_lat, norm1_g[i]), sh1, sc1)
        uf, uc, q, k, v = split_heads(hn_lat @ w_in[i])
        mix_lat = jnp.concatenate([fourier_mix(uf, w_fourier[i]), conv_module(uc, *conv_p),
                                   neighbourhood_attention(q, k, v, k_ctx, v_ctx, na_rpb[i])], axis=-1)
        h_lat = h_lat + g1 * (mix_lat @ w_out[i])

        hn = modulate(rms_norm(h_lat, norm2_g[i]), sh2, sc2)
        h_lat = h_lat + g2 * (jnp.square(jax.nn.relu(hn @ mlp_w1[i])) @ mlp_w2[i])

        if not last:
            h_ctx = h_ctx + mods_c[2] * (mix_ctx @ w_out[i])
            hn_c2 = modulate(rms_norm(h_ctx, norm2_g[i]), mods_c[3], mods_c[4])
            h_ctx = h_ctx + mods_c[5] * (jnp.square(jax.nn.relu(hn_c2 @ mlp_w1[i])) @ mlp_w2[i])
    return rms_norm(h_lat, final_norm_g)
```

```python
import math
from contextlib import ExitStack

import numpy as np
import ml_dtypes
import concourse.bass as bass
import concourse.mybir as mybir
from concourse.bass_utils import run_bass_kernel_spmd

F32 = mybir.dt.float32
BF16 = mybir.dt.bfloat16
AF = mybir.ActivationFunctionType
ALU = mybir.AluOpType
NPBF = ml_dtypes.bfloat16

D = 1024
T = 2048
CT = 256
DEPTH = 2
DIN = 2304
DFF = 4096
NCORES = 8
NEG = -30000.0
NBLK = 21
EOFF = {"int": 0, 0: 5, 1: 9, 14: 13, 15: 17}
RMS_EPS = 1e-6
LN_EPS = 1e-5
CELL = 512
_CACHE = {}
DBG = {"pairs": 4, "units": None, "parts": 5, "pv": 0}


class Builder:
    def __init__(self, nc):
        self.nc = nc
        self.ops = []
        self.cw = {}
        self.cr = {}

    @staticmethod
    def _cells(ap):
        sz = 4 if ap.dtype == F32 else 2
        pairs = ap.ap
        pstride, pcount = pairs[0]
        off = ap.offset
        p0 = off // pstride
        f0 = off % pstride
        ext = 0
        for step, cnt in pairs[1:]:
            ext += (cnt - 1) * step
        lo = f0 * sz
        hi = (f0 + ext + 1) * sz
        name = ap.name
        cell = 2048 if name.startswith("pp") else CELL
        out = []
        for q in range(p0 // 32, (p0 + pcount - 1) // 32 + 1):
            for ci in range(lo // cell, (hi - 1) // cell + 1):
                out.append((name, q, ci))
        return out

    def add(self, eng, fn, reads=(), writes=(), dma=False):
        oid = len(self.ops)
        deps = set()
        rc = [c for ap in reads for c in self._cells(ap)]
        wc = [c for ap in writes for c in self._cells(ap)]
        wc += [c for c in rc if c[0].startswith("pp")]
        for c in rc:
            w = self.cw.get(c)
            if w is not None:
                deps.add(w)
        for c in wc:
            w = self.cw.get(c)
            if w is not None:
                deps.add(w)
            r = self.cr.get(c)
            if r:
                deps.update(r.values())
        key = ("d", oid) if dma else eng
        for c in wc:
            self.cw[c] = oid
            self.cr[c] = {}
        for c in rc:
            self.cr.setdefault(c, {})[key] = oid
        self.ops.append(dict(eng=eng, fn=fn, deps=deps, dma=dma, sig=False, cnt=0))
        return oid

    def mm(self, out, lhsT, rhs, start, stop):
        self.add("pe", lambda e: e.matmul(out, lhsT, rhs, start=start, stop=stop), reads=[lhsT, rhs], writes=[out])

    def act(self, out, in_, func, bias=None, scale=None):
        reads = [in_]
        kw = {}
        if bias is not None:
            kw["bias"] = bias
            if not isinstance(bias, float):
                reads.append(bias)
        if scale is not None:
            kw["scale"] = scale
            if not isinstance(scale, float):
                reads.append(scale)
        self.add("act", lambda e: e.activation(out=out, in_=in_, func=func, **kw), reads=reads, writes=[out])

    def tt(self, eng, out, in0, in1, op):
        self.add(eng, lambda e: e.tensor_tensor(out=out, in0=in0, in1=in1, op=op), reads=[in0, in1], writes=[out])

    def ts(self, eng, out, in0, s1, op0, s2=None, op1=None):
        reads = [in0]
        if not isinstance(s1, float):
            reads.append(s1)
        if s2 is not None and not isinstance(s2, float):
            reads.append(s2)
        if op1 is None:
            self.add(eng, lambda e: e.tensor_scalar(out=out, in0=in0, scalar1=s1, scalar2=None, op0=op0), reads=reads, writes=[out])
        else:
            self.add(eng, lambda e: e.tensor_scalar(out=out, in0=in0, scalar1=s1, scalar2=s2, op0=op0, op1=op1), reads=reads, writes=[out])

    def stt(self, out, in0, scalar, in1, op0, op1):
        reads = [in0, in1]
        if not isinstance(scalar, float):
            reads.append(scalar)
        self.add("dve", lambda e: e.scalar_tensor_tensor(out=out, in0=in0, scalar=scalar, in1=in1, op0=op0, op1=op1),
                 reads=reads, writes=[out])

    def copy(self, eng, out, in_):
        if eng == "act":
            self.add("act", lambda e: e.copy(out=out, in_=in_), reads=[in_], writes=[out])
        else:
            self.add(eng, lambda e: e.tensor_copy(out=out, in_=in_), reads=[in_], writes=[out])

    def recip(self, out, in_):
        self.add("dve", lambda e: e.reciprocal(out=out, in_=in_), reads=[in_], writes=[out])

    def memset(self, eng, ap, val):
        self.add(eng, lambda e: e.memset(ap, val), writes=[ap])

    def dma(self, q, out, in_, reads=(), writes=()):
        self.add(q, lambda e: e.dma_start(out=out, in_=in_), reads=reads, writes=writes, dma=True)

    def emit(self, es):
        nc = self.nc
        ops = self.ops
        engs = ["pe", "act", "dve", "pool", "sp"]
        for op in ops:
            best = {}
            keep = set()
            for d in op["deps"]:
                a = ops[d]
                if a["dma"]:
                    keep.add(d)
                    continue
                if a["eng"] == "pe" and op["eng"] == "pe" and not op["dma"]:
                    continue
                if d > best.get(a["eng"], -1):
                    best[a["eng"]] = d
            for d in best.values():
                ops[d]["sig"] = True
                keep.add(d)
            op["deps"] = keep
        cnt = {e: 0 for e in engs}
        for op in ops:
            if not op["dma"] and op["sig"]:
                cnt[op["eng"]] += 1
                op["cnt"] = cnt[op["eng"]]
        NDS = 8
        esem = {e: es.enter_context(nc.semaphore("s_" + e)) for e in engs}
        dsem = {q: [es.enter_context(nc.semaphore(f"d_{q}{i}")) for i in range(NDS)] for q in ("sp", "pool", "act")}
        dk = {q: 0 for q in dsem}
        for op in ops:
            if op["dma"]:
                q = op["eng"]
                k = dk[q]
                dk[q] += 1
                op["dsem"] = dsem[q][k % NDS]
                op["dval"] = 16 * (k // NDS + 1)
                op["dprev"] = 16 * (k // NDS)
        streams = {e: [op for op in ops if op["eng"] == e] for e in engs}
        self.trace = []
        semname = {id(v): k for k, v in esem.items()}
        for q in dsem:
            for i, sm in enumerate(dsem[q]):
                semname[id(sm)] = f"d_{q}{i}"

        def run(e, name):
            waited = {}

            def wait(sem, val):
                key = id(sem)
                if waited.get(key, 0) >= val:
                    return
                waited[key] = val
                self.trace.append((name, "wait", semname.get(key, "?"), val))
                e.wait_ge(sem, val)

            for op in streams[name]:
                for d in sorted(op["deps"]):
                    a = ops[d]
                    if a["dma"]:
                        wait(a["dsem"], a["dval"])
                    else:
                        if a["eng"] == "pe" and name == "pe" and not op["dma"]:
                            continue
                        wait(esem[a["eng"]], a["cnt"])
                if op["dma"]:
                    if op["dprev"] > 0:
                        wait(op["dsem"], op["dprev"])
                    ins = op["fn"](e)
                    ins.then_inc(op["dsem"], 16)
                    self.trace.append((name, "dma", semname[id(op["dsem"])], op["dval"]))
                else:
                    ins = op["fn"](e)
                    self.trace.append((name, "op", type(ins).__name__, op["cnt"] if op["sig"] else 0))
                    if op["sig"]:
                        ins.then_inc(esem[name], 1)
            if name in dsem:
                for i in range(NDS):
                    k = dk[name]
                    n_i = (k - i + NDS - 1) // NDS if k > i else 0
                    if n_i > 0:
                        wait(dsem[name][i], 16 * n_i)

        with nc.Block() as block:
            @block.tensor
            def _(e):
                run(e, "pe")

            @block.scalar
            def _(e):
                run(e, "act")

            @block.vector
            def _(e):
                run(e, "dve")

            @block.gpsimd
            def _(e):
                run(e, "pool")

            @block.sync
            def _(e):
                run(e, "sp")


def tile_chunks(m):
    if m <= 1:
        return list(range(0, 4)), m
    if m >= 14:
        return list(range(12, 16)), m
    return list(range(m - 2, m + 3)), "int"


def build_program(stop_after=None, dbg_names=()):
    nc = bass.Bass("TRN2", target_bir_lowering=False)

    def din(name, shape, dt=F32):
        return nc.dram_tensor(name, list(shape), dt, kind="ExternalInput").ap()

    xT = din("xT", [D, T])
    cxT = din("cxT", [D, CT])
    cvec_d = din("cvec", [128, 16])
    ada_w = din("ada_w", [DEPTH, D, 6 * D])
    adab_d = din("adab", [128, DEPTH * 96])
    ng_d = din("ng", [128, DEPTH * 2 * 16])
    fng_d = din("fng", [128, 8])
    cp_d = din("cp", [128, DEPTH * 2 * 35])
    w_in = din("w_in", [DEPTH, D, DIN])
    w_inFT = din("w_inFT", [DEPTH, 256, D])
    w_f = din("w_f", [DEPTH, 256, 256])
    w_pw = din("w_pw", [DEPTH, 256, 256])
    w_out = din("w_out", [DEPTH, D, D])
    w1 = din("w1", [DEPTH, D, DFF])
    w2 = din("w2", [DEPTH, DFF, D])
    csblk_d = din("csblk", [256, 512])
    dft_d = din("dft", [8, 128, 16 * 2 * 256], BF16)
    dft256_d = din("dft256", [128, 2 * 2 * 256], BF16)
    ident_d = din("ident", [128, 128], BF16)
    bias_d = din("biasT", [DEPTH, 8, 128, NBLK * 128])
    outT = nc.dram_tensor("outT", [D, T], F32, kind="ExternalOutput").ap()
    dbg_out = {}

    es = ExitStack()
    with es:
        def sb(name, n, dt):
            return es.enter_context(nc.sbuf_tensor(name, [128, n], dt))

        H_t = sb("H", 8 * T, F32)
        Hc_t = sb("Hc", 8 * CT, F32)
        hn_t = sb("hn", 8 * T, BF16)
        hnc_t = sb("hnc", 8 * CT, BF16)
        mixc_t = sb("mixc", 8 * CT, BF16)
        AR = sb("arena", 40960, BF16)
        ident = sb("identS", 128, BF16)
        ones_rms = sb("ones_rms", 128, BF16)
        ones_ln = sb("ones_ln", 128, BF16)
        csblk = sb("csblkS", 2 * 512, BF16)
        dft256 = sb("dft256S", 2 * 2 * 256, BF16)
        wf_t = sb("wfS", 2 * 256, BF16)
        pw_t = sb("pwS", 2 * 256, BF16)
        cvec = sb("cvecS", 16, F32)
        csil = sb("csil", 16, BF16)
        adab = sb("adabS", DEPTH * 96, F32)
        ngs = sb("ngS", DEPTH * 32, F32)
        fng = sb("fngS", 8, F32)
        cps = sb("cpS", DEPTH * 70, F32)
        mod_t = sb("mod", DEPTH * 96, F32)
        coef_t = sb("coef", DEPTH * 96, F32)
        pp = [es.enter_context(nc.psum_tensor(f"pp{i}", [128, 1024], F32)) for i in range(4)]

        B = Builder(nc)

        def bank(i):
            i = i % 8
            return pp[i // 2][:, (i % 2) * 512:(i % 2) * 512 + 512]

        bank_ctr = [0]

        def nbank():
            b = bank(bank_ctr[0])
            bank_ctr[0] += 1
            return b

        H = H_t[:].rearrange("p (c t) -> p c t", c=8)
        Hc = Hc_t[:].rearrange("p (c t) -> p c t", c=8)
        hn = hn_t[:].rearrange("p (c t) -> p c t", c=8)
        hnc = hnc_t[:].rearrange("p (c t) -> p c t", c=8)
        mixc = mixc_t[:].rearrange("p (c t) -> p c t", c=8)

        def av(off_kib, n):
            o = int(round(off_kib * 512))
            return AR[:, o:o + n]

        mix = av(0, 8 * T).rearrange("p (c t) -> p c t", c=8)
        WA = av(32, 4096)
        WB = av(40, 4096)
        sqb = [av(48, 512), av(49, 512)]
        rstd_b = av(50, 1024).bitcast(F32)
        t32 = [av(52, 1024).bitcast(F32), av(54, 1024).bitcast(F32)]
        SCR = 56.0

        class Stream:
            pass

        LAT = Stream()
        LAT.T, LAT.H, LAT.hn, LAT.mix, LAT.col, LAT.name = T, H, hn, mix, 0, "lat"
        CTX = Stream()
        CTX.T, CTX.H, CTX.hn, CTX.mix, CTX.col, CTX.name = CT, Hc, hnc, mixc, 1, "ctx"

        def blocks(S, n=512):
            return [(o, min(n, S.T - o)) for o in range(0, S.T, n)]

        def coef(l, k, c, col):
            o = l * 96 + k * 16 + c * 2 + col
            return coef_t[:, o:o + 1]

        B.dma("sp", ident[:], ident_d, writes=[ident[:]])
        B.dma("sp", cvec[:], cvec_d, writes=[cvec[:]])
        B.dma("sp", adab[:], adab_d, writes=[adab[:]])
        B.dma("sp", ngs[:], ng_d, writes=[ngs[:]])
        B.dma("sp", fng[:], fng_d, writes=[fng[:]])
        B.dma("sp", cps[:], cp_d, writes=[cps[:]])
        B.dma("sp", dft256[:], dft256_d, writes=[dft256[:]])
        B.dma("pool", csblk[:].rearrange("p (c n) -> p c n", c=2), csblk_d.rearrange("(c p) n -> p c n", p=128),
              writes=[csblk[:]])
        B.memset("pool", ones_rms[:], 1.0 / D)
        B.memset("pool", ones_ln[:], 1.0 / 256)
        for c in range(8):
            B.dma("sp", H[:, c, :], xT.rearrange("(c p) t -> p c t", p=128)[:, c, :], writes=[H[:, c, :]])
        B.dma("sp", Hc, cxT.rearrange("(c p) t -> p c t", p=128), writes=[Hc_t[:]])
        B.act(csil[:], cvec[:], AF.Silu)

        def ada(l):
            csv = csil[:].rearrange("p (c k) -> p c k", c=8)
            for s in range(12):
                slot = (WA, WB)[s % 2]
                sv = slot.rearrange("p (c n) -> p c n", c=8)
                B.dma("pool", sv, ada_w[l].rearrange("(c p) n -> p c n", p=128)[:, :, s * 512:(s + 1) * 512], writes=[slot])
                ps = nbank()
                for jj in range(4):
                    for dc in range(8):
                        B.mm(ps[:, 2 * jj:2 * jj + 2], sv[:, dc, jj * 128:(jj + 1) * 128], csv[:, dc, :], dc == 0, dc == 7)
                o = l * 96 + s * 8
                B.tt("dve", mod_t[:, o:o + 8], ps[:, 0:8], adab[:, o:o + 8], ALU.add)
            m3 = mod_t[:, l * 96:(l + 1) * 96].rearrange("p (k x) -> p k x", k=6)
            c3 = coef_t[:, l * 96:(l + 1) * 96].rearrange("p (k x) -> p k x", k=6)
            n3 = ngs[:, l * 32:(l + 1) * 32].rearrange("p (k x) -> p k x", k=2)
            B.stt(c3[:, 0, :], m3[:, 1, :], 1.0, n3[:, 0, :], ALU.add, ALU.mult)
            B.copy("dve", c3[:, 1, :], m3[:, 0, :])
            B.copy("dve", c3[:, 2, :], m3[:, 2, :])
            B.stt(c3[:, 3, :], m3[:, 4, :], 1.0, n3[:, 1, :], ALU.add, ALU.mult)
            B.copy("dve", c3[:, 4, :], m3[:, 3, :])
            B.copy("dve", c3[:, 5, :], m3[:, 5, :])

        def norm(S, l, kA, kB, final=False):
            for (o, n) in blocks(S):
                ps = nbank()
                for c in range(8):
                    sq = sqb[c % 2]
                    B.act(sq[:, :n], S.H[:, c, o:o + n], AF.Square)
                    B.mm(ps[:, :n], ones_rms[:], sq[:, :n], c == 0, c == 7)
                B.act(rstd_b[:, :n], ps[:, :n], AF.Sqrt, bias=RMS_EPS_AP[0], scale=1.0)
                B.recip(rstd_b[:, :n], rstd_b[:, :n])
                for c in range(8):
                    t = t32[c % 2]
                    B.tt("dve", t[:, :n], S.H[:, c, o:o + n], rstd_b[:, :n], ALU.mult)
                    if final:
                        B.act(S.H[:, c, o:o + n], t[:, :n], AF.Identity, scale=fng[:, c:c + 1])
                    else:
                        B.act(S.hn[:, c, o:o + n], t[:, :n], AF.Identity, bias=coef(l, kB, c, S.col), scale=coef(l, kA, c, S.col))

        eps_t = sb("epsS", 2, F32)
        B.memset("pool", eps_t[:, 0:1], RMS_EPS)
        B.memset("pool", eps_t[:, 1:2], LN_EPS)
        RMS_EPS_AP = [eps_t[:, 0:1]]
        LN_EPS_AP = eps_t[:, 1:2]

        evac_rr = [0]

        def evac(out, in_):
            eng = ("act", "dve")[evac_rr[0] % 2]
            evac_rr[0] += 1
            B.copy(eng, out, in_)

        def proj_cm(S, wv, col0, evac_fn, nblk=512):
            for (o, n) in blocks(S, nblk):
                ps = nbank()
                for dc in range(8):
                    B.mm(ps[:, :n], wv[:, dc, col0:col0 + 128], S.hn[:, dc, o:o + n], dc == 0, dc == 7)
                evac_fn(ps[:, :n], o, n)

        def attention(l, last):
            QTA = av(SCR + 0, T)
            QTB = av(SCR + 4, T)
            KT = av(SCR + 8, T)
            Vp = av(SCR + 12, 16 * 256).rearrange("p (t x) -> p t x", t=16)
            KcT = av(SCR + 20, CT)
            Vc = av(SCR + 20.5, 2 * 256).rearrange("p (t x) -> p t x", t=2)
            QcA = av(48, CT)
            QcB = av(48.5, CT)
            rec = [av(52, 256).bitcast(F32), av(52.5, 256).bitcast(F32)]
            Eb = [av(0, NBLK * 128), av(5.25, NBLK * 128)]
            PT = [av(10.5 + 1.75 * i, 896) for i in range(3)]
            B.memset("pool", Vp[:, :, 64:128], 1.0)
            B.memset("pool", Vc[:, :, 64:128], 1.0)
            B.memset("pool", QTA[64:128, :], 0.0)
            B.memset("pool", QTB[0:64, :], 0.0)
            B.memset("pool", QcA[64:128, :], 0.0)
            B.memset("pool", QcB[0:64, :], 0.0)
            Sps = [pp[0], pp[1]]
            Obanks = [pp[2][:, 0:512], pp[2][:, 512:1024]]
            Mbanks = [pp[3][:, 0:512], pp[3][:, 512:1024]]
            mctr = [0]

            def nM():
                b = Mbanks[mctr[0] % 2]
                mctr[0] += 1
                return b
            uctr = [0]

            for p in range(DBG["pairs"]):
                wv = WB.rearrange("p (c n) -> p c n", c=8)[:, :, 0:384]
                c0 = 768 + p * 384
                B.dma("pool", wv, w_in[l].rearrange("(c p) n -> p c n", p=128)[:, :, c0:c0 + 384], writes=[WB])
                for hh in range(2):
                    B.dma("pool", Eb[hh], bias_d[l, 2 * p + hh], writes=[Eb[hh]])
                    B.act(Eb[hh], Eb[hh], AF.Exp)

                def ev_kc(ps, o, n):
                    evac(KcT[:, o:o + n], ps)
                proj_cm_fixed(CTX, wv, 128, ev_kc, nM)
                if not last:
                    def ev_qc(ps, o, n):
                        B.copy("act", QcA[0:64, o:o + n], ps[0:64, :])
                        B.copy("dve", QcB[64:128, o:o + n], ps[64:128, :])
                    proj_cm_fixed(CTX, wv, 0, ev_qc, nM)
                for t in range(2):
                    mb = nM()
                    for dc in range(8):
                        B.mm(mb[:, 0:128], hnc[:, dc, t * 128:(t + 1) * 128], wv[:, dc, 256:384], dc == 0, dc == 7)
                    B.copy("dve", Vc[:, t, :].rearrange("p (a b) -> p a b", a=2)[:, :, 0:64],
                           mb[:, 0:128].rearrange("p (a b) -> p a b", a=2))

                def ev_q(ps, o, n):
                    B.copy("act", QTA[0:64, o:o + n], ps[0:64, :])
                    B.copy("dve", QTB[64:128, o:o + n], ps[64:128, :])

                def ev_k(ps, o, n):
                    evac(KT[:, o:o + n], ps)
                proj_cm_fixed(LAT, wv, 0, ev_q, nM)
                proj_cm_fixed(LAT, wv, 128, ev_k, nM)
                for t4 in range(4):
                    mb = nM()
                    for tt_ in range(4):
                        t = t4 * 4 + tt_
                        for dc in range(8):
                            B.mm(mb[:, tt_ * 128:(tt_ + 1) * 128], hn[:, dc, t * 128:(t + 1) * 128], wv[:, dc, 256:384], dc == 0, dc == 7)
                    for tt_ in range(4):
                        t = t4 * 4 + tt_
                        B.copy(("dve", "act")[tt_ % 2], Vp[:, t, :].rearrange("p (a b) -> p a b", a=2)[:, :, 0:64],
                               mb[:, tt_ * 128:(tt_ + 1) * 128].rearrange("p (a b) -> p a b", a=2))

                units = []
                if not last:
                    for hh in range(2):
                        for m in range(2):
                            units.append(("ctx", hh, m))
                for hh in range(2):
                    for m in range(16):
                        units.append(("lat", hh, m))
                if DBG["units"] is not None:
                    units = units[:DBG["units"]]

                def unit_S(u):
                    kind, hh, m = units[u]
                    k = uctr[0] + u
                    Sp = Sps[k % 2]
                    if kind == "lat":
                        chunks, _ = tile_chunks(m)
                        q = (QTA, QTB)[hh][:, m * 128:(m + 1) * 128]
                        for i, j in enumerate(chunks):
                            B.mm(Sp[:, i * 128:(i + 1) * 128], KT[:, j * 128:(j + 1) * 128], q, True, True)
                        nl = len(chunks)
                    else:
                        q = (QcA, QcB)[hh][:, m * 128:(m + 1) * 128]
                        nl = 0
                    for cc in range(2):
                        B.mm(Sp[:, (nl + cc) * 128:(nl + cc + 1) * 128], KcT[:, cc * 128:(cc + 1) * 128], q, True, True)

                def unit_rest(u):
                    kind, hh, m = units[u]
                    k = uctr[0] + u
                    Sp = Sps[k % 2]
                    P = PT[k % 3]
                    if kind == "lat":
                        chunks, typ = tile_chunks(m)
                    else:
                        chunks, typ = [], None
                    nl = len(chunks)
                    ns = nl + 2
                    B.act(P[:, 0:ns * 128], Sp[:, 0:ns * 128], AF.Exp, scale=0.125)
                    if nl:
                        eo = EOFF[typ] * 128
                        B.tt("dve", P[:, 0:nl * 128], P[:, 0:nl * 128], Eb[hh][:, eo:eo + nl * 128], ALU.mult)
                    if DBG["parts"] < 3:
                        return
                    Op = Obanks[k % 2][:, 0:128]
                    vo = 64 * hh
                    for i, j in enumerate(chunks):
                        B.mm(Op, Vp[:, j, vo:vo + 128], P[:, i * 128:(i + 1) * 128], i == 0, False)
                    for cc in range(2):
                        B.mm(Op, Vc[:, cc, vo:vo + 128], P[:, (nl + cc) * 128:(nl + cc + 1) * 128], (nl == 0 and cc == 0), cc == 1)
                    if DBG["parts"] < 4:
                        B.copy("act", rec[0][:, :], Op)
                        return
                    r = rec[k % 2]
                    olo, ohi = (0, 64) if hh == 0 else (64, 128)
                    dlo, dhi = (64, 128) if hh == 0 else (0, 64)
                    B.recip(r[dlo:dhi, :], Op[dlo:dhi, :])
                    dst = (mix if kind == "lat" else mixc)[olo:ohi, 4 + p, m * 128:(m + 1) * 128]
                    B.tt("dve", dst, Op[olo:ohi, :], r[dlo:dhi, :], ALU.mult)

                LOOK = 1
                for u in range(min(LOOK, len(units))):
                    unit_S(u)
                for u in range(len(units)):
                    if u + LOOK < len(units):
                        unit_S(u + LOOK)
                    unit_rest(u)
                uctr[0] += len(units)

        def proj_cm_fixed(S, wv, col0, evac_fn, nb):
            for (o, n) in blocks(S):
                psb = nb()
                for dc in range(8):
                    B.mm(psb[:, :n], wv[:, dc, col0:col0 + 128], S.hn[:, dc, o:o + n], dc == 0, dc == 7)
                evac_fn(psb[:, :n], o, n)

        def conv_phase(l, streams):
            wv = WA.rearrange("p (c n) -> p c n", c=8)
            B.dma("pool", wv, w_in[l].rearrange("(c p) n -> p c n", p=128)[:, :, 256:768], writes=[WA])
            B.dma("pool", pw_t[:].rearrange("p (c n) -> p c n", c=2), w_pw[l].rearrange("(c p) n -> p c n", p=128), writes=[pw_t[:]])
            pwv = pw_t[:].rearrange("p (c n) -> p c n", c=2)
            vpad = av(SCR, 2 * (T + 30)).rearrange("p (c t) -> p c t", c=2)
            vpadc = av(0, 2 * (CT + 30)).rearrange("p (c t) -> p c t", c=2)
            doff = SCR + (2 * (T + 30) * 2) / 1024.0
            doff = math.ceil(doff * 16) / 16.0
            Dg = av(doff, 62 * 128).rearrange("p (c k n) -> p c k n", c=2, k=31)
            cpl = cps[:, l * 70:(l + 1) * 70].rearrange("p (c k) -> p c k", c=2)
            sig = av(2, 512)
            ybf = [av(3, 512), av(4, 512)]
            ysq = [av(5, 512), av(6, 512)]
            y32 = [t32[0], t32[1]]
            z32 = rstd_b
            sil = [sqb[0], sqb[1]]
            for ch in range(2):
                for k in range(31):
                    B.ts("dve", Dg[:, ch, k, :], ident[:], cpl[:, ch, k:k + 1], ALU.mult)
            for S in streams:
                vp = vpad if S is LAT else vpadc
                B.memset("pool", vp[:, :, 0:15], 0.0)
                B.memset("pool", vp[:, :, 15 + S.T:30 + S.T], 0.0)
                for (o, n) in blocks(S):
                    for ch in range(2):
                        pa = nbank()
                        pg = nbank()
                        for dc in range(8):
                            B.mm(pa[:, :n], wv[:, dc, ch * 128:(ch + 1) * 128], S.hn[:, dc, o:o + n], dc == 0, dc == 7)
                        for dc in range(8):
                            B.mm(pg[:, :n], wv[:, dc, 256 + ch * 128:256 + (ch + 1) * 128], S.hn[:, dc, o:o + n], dc == 0, dc == 7)
                        B.act(sig[:, :n], pg[:, :n], AF.Sigmoid)
                        B.tt("dve", vp[:, ch, 15 + o:15 + o + n], pa[:, :n], sig[:, :n], ALU.mult)
                for (o, n) in blocks(S):
                    pm = nbank()
                    pq = nbank()
                    for ch in range(2):
                        pc = nbank()
                        for k in range(31):
                            B.mm(pc[:, :n], Dg[:, ch, k, :], vp[:, ch, o + k:o + k + n], k == 0, k == 30)
                        B.act(ybf[ch][:, :n], pc[:, :n], AF.Identity, bias=cpl[:, ch, 31:32], scale=1.0)
                        B.act(ysq[ch][:, :n], pc[:, :n], AF.Square, bias=cpl[:, ch, 31:32], scale=1.0)
                        B.ts("dve", y32[ch][:, :n], pc[:, :n], cpl[:, ch, 31:32], ALU.add)
                    for ch in range(2):
                        B.mm(pm[:, :n], ones_ln[:], ybf[ch][:, :n], ch == 0, ch == 1)
                    for ch in range(2):
                        B.mm(pq[:, :n], ones_ln[:], ysq[ch][:, :n], ch == 0, ch == 1)
                    B.act(z32[:, :n], pm[:, :n], AF.Square)
                    B.tt("dve", z32[:, :n], pq[:, :n], z32[:, :n], ALU.subtract)
                    B.act(z32[:, :n], z32[:, :n], AF.Sqrt, bias=LN_EPS_AP, scale=1.0)
                    B.recip(z32[:, :n], z32[:, :n])
                    for ch in range(2):
                        B.tt("dve", y32[ch][:, :n], y32[ch][:, :n], pm[:, :n], ALU.subtract)
                        B.tt("dve", y32[ch][:, :n], y32[ch][:, :n], z32[:, :n], ALU.mult)
                        B.act(sil[ch][:, :n], y32[ch][:, :n], AF.Silu, bias=cpl[:, ch, 33:34], scale=cpl[:, ch, 32:33])
                    for oc in range(2):
                        po = nbank()
                        for ch in range(2):
                            B.mm(po[:, :n], pwv[:, ch, oc * 128:(oc + 1) * 128], sil[ch][:, :n], ch == 0, ch == 1)
                        B.act(S.mix[:, 2 + oc, o:o + n], po[:, :n], AF.Identity, bias=cpl[:, oc, 34:35], scale=1.0)

        def fourier_phase(l, streams):
            ftv = WB[:, 0:2048].rearrange("p (c n) -> p c n", c=2)
            B.dma("pool", ftv, w_inFT[l].rearrange("(c p) n -> p c n", p=128), writes=[WB[:, 0:2048]])
            B.dma("pool", wf_t[:].rearrange("p (c n) -> p c n", c=2), w_f[l].rearrange("(c p) n -> p c n", p=128), writes=[wf_t[:]])
            wfv = wf_t[:].rearrange("p (c n) -> p c n", c=2)
            csv = csblk[:].rearrange("p (c n) -> p c n", c=2)
            Wp = WA.rearrange("p (c n) -> p c n", c=8)
            for dc in range(8):
                ps = nbank()
                for cc in range(2):
                    B.mm(ps, ftv[:, cc, dc * 128:(dc + 1) * 128], csv[:, cc, :], cc == 0, cc == 1)
                evac(Wp[:, dc, :], ps)
            UU = av(SCR, 16 * 512).rearrange("p (t n) -> p t n", t=16)
            UUc = av(SCR + 16, 2 * 512).rearrange("p (t n) -> p t n", t=2)
            FT = av(SCR + 18, 2 * 256).rearrange("p (c n) -> p c n", c=2)
            for S in streams:
                U = UU if S is LAT else UUc
                nt = S.T // 128
                for t in range(nt):
                    ps = nbank()
                    for dc in range(8):
                        B.mm(ps, S.hn[:, dc, t * 128:(t + 1) * 128], Wp[:, dc, :], dc == 0, dc == 7)
                    evac(U[:, t, :], ps)
            for S in streams:
                U = UU if S is LAT else UUc
                nt = S.T // 128
                njb = S.T // 256
                for jb in range(njb):
                    if S is LAT:
                        dbuf = hn_t[:, (jb % 2) * 8192:(jb % 2 + 1) * 8192]
                        B.dma("sp", dbuf, dft_d[jb], writes=[dbuf])
                        dv = dbuf.rearrange("p (k s j) -> p k s j", k=16, s=2)
                    else:
                        dv = dft256[:].rearrange("p (k s j) -> p k s j", k=2, s=2)
                    for cg in range(2):
                        ps = nbank()
                        first = True
                        for kc in range(nt):
                            for s in range(2):
                                B.mm(ps[:, 0:256], U[:, kc, s * 256 + cg * 128:s * 256 + (cg + 1) * 128], dv[:, kc, s, :],
                                     first, (kc == nt - 1 and s == 1))
                                first = False
                        evac(FT[:, cg, :], ps[:, 0:256])
                    for oc in range(2):
                        ps = nbank()
                        for cg in range(2):
                            B.mm(ps[:, 0:256], wfv[:, cg, oc * 128:(oc + 1) * 128], FT[:, cg, :], cg == 0, cg == 1)
                        evac(S.mix[:, oc, jb * 256:(jb + 1) * 256], ps[:, 0:256])

        def outproj(l, streams):
            wo = AR[:, 32 * 512:48 * 512].rearrange("p (c n) -> p c n", c=8)
            B.dma("pool", wo[:, 0:4, :], w_out[l].rearrange("(c p) n -> p c n", p=128)[:, 0:4, :], writes=[WA])
            B.dma("pool", wo[:, 4:8, :], w_out[l].rearrange("(c p) n -> p c n", p=128)[:, 4:8, :], writes=[WB])
            for S in streams:
                for (o, n) in blocks(S):
                    for oc in range(8):
                        ps = nbank()
                        for kc in range(8):
                            B.mm(ps[:, :n], wo[:, kc, oc * 128:(oc + 1) * 128], S.mix[:, kc, o:o + n], kc == 0, kc == 7)
                        B.stt(S.H[:, oc, o:o + n], ps[:, :n], coef(l, 2, oc, S.col), S.H[:, oc, o:o + n], ALU.mult, ALU.add)

        def mlp(l, streams, after_first_norm):
            slot_off = [8, 16, 24, 32, 40, 56, 64, 72]
            sets = [[5, 6, 7, 0], [1, 2, 3, 4]]
            hid = [av(0, 2048).rearrange("p (f t) -> p f t", f=8), av(4, 2048).rearrange("p (f t) -> p f t", f=8)]
            rl = [sqb[0][:, 0:256], sqb[1][:, 0:256]]
            hctr = 0
            for fb in range(4):
                st = sets[fb % 2]
                w1s = [av(slot_off[st[0]], 4096).rearrange("p (c n) -> p c n", c=8),
                       av(slot_off[st[1]], 4096).rearrange("p (c n) -> p c n", c=8)]
                w2s = [av(slot_off[st[2]], 4096).rearrange("p (c n) -> p c n", c=4),
                       av(slot_off[st[3]], 4096).rearrange("p (c n) -> p c n", c=4)]
                for hhalf in range(2):
                    c0 = fb * 1024 + hhalf * 512
                    B.dma("pool", w1s[hhalf], w1[l].rearrange("(c p) n -> p c n", p=128)[:, :, c0:c0 + 512],
                          writes=[av(slot_off[st[hhalf]], 4096)])
                for hhalf in range(2):
                    r0 = fb * 8 + hhalf * 4
                    B.dma("pool", w2s[hhalf], w2[l].rearrange("(c p) n -> p c n", p=128)[:, r0:r0 + 4, :],
                          writes=[av(slot_off[st[2 + hhalf]], 4096)])
                if fb == 0 and after_first_norm is not None:
                    after_first_norm()
                for S in streams:
                    for (o, n) in blocks(S, 256):
                        hb = hid[hctr % 2]
                        hctr += 1
                        for fc in range(8):
                            ps = nbank()
                            wv = w1s[fc // 4]
                            cc = (fc % 4) * 128
                            for dc in range(8):
                                B.mm(ps[:, :n], wv[:, dc, cc:cc + 128], S.hn[:, dc, o:o + n], dc == 0, dc == 7)
                            r = rl[fc % 2]
                            B.act(r[:, :n], ps[:, :n], AF.Relu)
                            B.tt("dve", hb[:, fc, :n], r[:, :n], r[:, :n], ALU.mult)
                        for oc in range(8):
                            ps = nbank()
                            for fc in range(8):
                                B.mm(ps[:, :n], w2s[fc // 4][:, fc % 4, oc * 128:(oc + 1) * 128], hb[:, fc, :n], fc == 0, fc == 7)
                            B.stt(S.H[:, oc, o:o + n], ps[:, :n], coef(l, 5, oc, S.col), S.H[:, oc, o:o + n], ALU.mult, ALU.add)

        def dump(name, ap_sb, shape):
            d = nc.dram_tensor("dbg_" + name, list(shape), ap_sb.dtype, kind="ExternalOutput").ap()
            B.dma("sp", d, ap_sb, reads=[ap_sb])
            dbg_out[name] = True

        done = False
        for l in range(DEPTH):
            last = l == DEPTH - 1
            streams = [LAT] if last else [CTX, LAT]
            ada(l)
            if stop_after == f"ada{l}":
                dump("mod", mod_t[:], [128, DEPTH * 96])
                dump("coef", coef_t[:], [128, DEPTH * 96])
                done = True
                break
            norm(CTX, l, 0, 1)
            norm(LAT, l, 0, 1)
            if stop_after == f"norm{l}":
                dump("hn", hn_t[:], [128, 8 * T])
                dump("hnc", hnc_t[:], [128, 8 * CT])
                done = True
                break
            attention(l, last)
            if stop_after == f"attn{l}":
                dump("mix", AR[:, 0:8 * T], [128, 8 * T])
                dump("mixc", mixc_t[:], [128, 8 * CT])
                done = True
                break
            conv_phase(l, streams)
            if stop_after == f"conv{l}":
                dump("mix", AR[:, 0:8 * T], [128, 8 * T])
                dump("mixc", mixc_t[:], [128, 8 * CT])
                done = True
                break
            fourier_phase(l, streams)
            if stop_after == f"four{l}":
                dump("mix", AR[:, 0:8 * T], [128, 8 * T])
                dump("mixc", mixc_t[:], [128, 8 * CT])
                done = True
                break
            outproj(l, streams)
            if stop_after == f"oproj{l}":
                dump("H", H_t[:], [128, 8 * T])
                dump("Hc", Hc_t[:], [128, 8 * CT])
                done = True
                break
            for S in streams:
                norm(S, l, 3, 4)
            mlp(l, streams, None)
            if stop_after == f"mlp{l}":
                dump("H", H_t[:], [128, 8 * T])
                dump("Hc", Hc_t[:], [128, 8 * CT])
                done = True
                break
        if not done:
            norm(LAT, 0, 0, 0, final=True)
        for c in range(8):
            B.dma("sp", outT.rearrange("(c p) t -> p c t", p=128)[:, c, :], H[:, c, :], reads=[H[:, c, :]])
        B.emit(es)
    _CACHE['trace'] = B.trace
    return nc, sorted(dbg_out.keys()), len(B.ops)


def _bias_tables(rpb):
    kc = np.arange(64)
    qc = np.arange(64)
    cs = np.clip(qc - 8, 0, 48)
    colvalid = (kc[:, None] >= cs[None, :]) & (kc[:, None] < cs[None, :] + 16)
    coloff = np.clip(kc[:, None] - qc[None, :] + 15, 0, 30)
    blocks = []
    specs = [(5, j) for j in range(3, 8)] + [(0, j) for j in range(4)] + [(1, j) for j in range(4)] + \
            [(14, j) for j in range(12, 16)] + [(15, j) for j in range(12, 16)]
    out = np.full((DEPTH, 8, 128, NBLK, 128), NEG, np.float32)
    for bi, (m, j) in enumerate(specs):
        for a in range(2):
            for b in range(2):
                krow = 2 * j + a
                qrow = 2 * m + b
                rs = min(max(qrow - 4, 0), 24)
                if not (rs <= krow < rs + 8):
                    continue
                dr = krow - qrow + 7
                vals = rpb[:, :, dr, :][:, :, coloff]
                vals = np.where(colvalid[None, None], vals, np.float32(NEG))
                out[:, :, a * 64:(a + 1) * 64, bi, b * 64:(b + 1) * 64] = vals
    return np.ascontiguousarray(out.reshape(DEPTH, 8, 128, NBLK * 128))


def _dft_tables():
    k = np.arange(T, dtype=np.int64)
    ang = 2.0 * np.pi * ((k[:, None] * k[None, :]) % T).astype(np.float64) / T
    Cm = (np.cos(ang) / math.sqrt(T)).astype(np.float32)
    Sm = (-np.sin(ang) / math.sqrt(T)).astype(np.float32)
    CS = np.stack([Cm, Sm], axis=0)
    CS = CS.reshape(2, 16, 128, 8, 256)
    dft = np.ascontiguousarray(CS.transpose(3, 2, 1, 0, 4)).reshape(8, 128, 16 * 2 * 256).astype(NPBF)
    k2 = np.arange(CT, dtype=np.int64)
    ang2 = 2.0 * np.pi * ((k2[:, None] * k2[None, :]) % CT).astype(np.float64) / CT
    C2 = (np.cos(ang2) / math.sqrt(CT)).astype(np.float32)
    S2 = (-np.sin(ang2) / math.sqrt(CT)).astype(np.float32)
    CS2 = np.stack([C2, S2], axis=0).reshape(2, 2, 128, 256)
    dft256 = np.ascontiguousarray(CS2.transpose(2, 1, 0, 3)).reshape(128, 2 * 2 * 256).astype(NPBF)
    a = np.arange(64)
    ang3 = 2.0 * np.pi * ((a[:, None] * a[None, :]) % 64) / 64.0
    cb = np.zeros((256, 256), np.float32)
    sbk = np.zeros((256, 256), np.float32)
    for g in range(4):
        cb[g * 64:(g + 1) * 64, g * 64:(g + 1) * 64] = np.cos(ang3) / 8.0
        sbk[g * 64:(g + 1) * 64, g * 64:(g + 1) * 64] = np.sin(ang3) / 8.0
    csblk = np.ascontiguousarray(np.concatenate([cb, sbk], axis=1)).astype(np.float32)
    return dft, dft256, csblk


def host_prep(inp):
    f = lambda a: np.ascontiguousarray(np.asarray(a, dtype=np.float32))
    x, c, ctx, c_ctx = f(inp["x"]), f(inp["c"]), f(inp["ctx"]), f(inp["c_ctx"])
    shared = {}
    shared["ada_w"] = f(inp["ada_w"])
    ab = f(inp["ada_b"]).reshape(DEPTH, 48, 128).transpose(2, 0, 1)
    shared["adab"] = np.ascontiguousarray(np.repeat(ab[:, :, :, None], 2, axis=3)).reshape(128, DEPTH * 96)
    n1 = f(inp["norm1_g"]).reshape(DEPTH, 8, 128).transpose(2, 0, 1)
    n2 = f(inp["norm2_g"]).reshape(DEPTH, 8, 128).transpose(2, 0, 1)
    ng = np.stack([n1, n2], axis=2)
    shared["ng"] = np.ascontiguousarray(np.repeat(ng[..., None], 2, axis=4)).reshape(128, DEPTH * 32)
    shared["fng"] = np.ascontiguousarray(f(inp["final_norm_g"]).reshape(8, 128).T)
    dw = f(inp["conv_dw_w"]).reshape(DEPTH, 31, 2, 128).transpose(3, 0, 2, 1)
    def pv(name):
        return f(inp[name]).reshape(DEPTH, 2, 128).transpose(2, 0, 1)[..., None]
    cp = np.concatenate([dw, pv("conv_dw_b"), pv("conv_norm_g"), pv("conv_norm_b"), pv("conv_pw_b")], axis=3)
    shared["cp"] = np.ascontiguousarray(cp).reshape(128, DEPTH * 70)
    w_in = f(inp["w_in"])
    order = list(range(0, 768))
    for p in range(4):
        for part in range(3):
            base = 768 + part * 512 + p * 128
            order += list(range(base, base + 128))
    shared["w_in"] = np.ascontiguousarray(w_in[:, :, order])
    shared["w_inFT"] = np.ascontiguousarray(w_in[:, :, :256].transpose(0, 2, 1))
    shared["w_f"] = f(inp["w_fourier"])
    shared["w_pw"] = f(inp["conv_pw_w"])
    shared["w_out"] = f(inp["w_out"])
    shared["w1"] = f(inp["mlp_w1"])
    shared["w2"] = f(inp["mlp_w2"])
    dft, dft256, csblk = _dft_tables()
    shared["dft"] = dft
    shared["dft256"] = dft256
    shared["csblk"] = csblk
    shared["ident"] = np.eye(128, dtype=np.float32).astype(NPBF)
    shared["biasT"] = _bias_tables(f(inp["na_rpb"]))
    maps = []
    cc = c_ctx.reshape(8, 128).T
    for b in range(NCORES):
        m = dict(shared)
        m["xT"] = np.ascontiguousarray(x[b].T)
        m["cxT"] = np.ascontiguousarray(ctx[b].T)
        cb = c[b].reshape(8, 128).T
        m["cvec"] = np.ascontiguousarray(np.stack([cb, cc], axis=2)).reshape(128, 16)
        maps.append(m)
    return maps


def kernel(**inputs):
    maps = host_prep(inputs)
    if "nc" not in _CACHE:
        _CACHE["nc"] = build_program()[0]
    nc = _CACHE["nc"]
    res = run_bass_kernel_spmd(nc, maps, core_ids=list(range(NCORES)))
    out = np.stack([np.ascontiguousarray(res.results[b]["outT"].T) for b in range(NCORES)], axis=0)
    return out.astype(np.float32)
```

```python
import math
from contextlib import ExitStack

import numpy as np
import ml_dtypes
import concourse.bass as bass
import concourse.mybir as mybir
from concourse.bass_utils import run_bass_kernel_spmd

F32 = mybir.dt.float32
BF16 = mybir.dt.bfloat16
AF = mybir.ActivationFunctionType
ALU = mybir.AluOpType
NPBF = ml_dtypes.bfloat16

D = 1024
T = 2048
CT = 256
DEPTH = 2
DIN = 2304
DFF = 4096
NCORES = 8
NEG = -30000.0
NBLK = 21
EOFF = {"int": 0, 0: 5, 1: 9, 14: 13, 15: 17}
RMS_EPS = 1e-6
LN_EPS = 1e-5
CELL = 512
_CACHE = {}
DBG = {"pairs": 4, "units": None, "parts": 5, "pv": 0}


class Builder:
    def __init__(self, nc):
        self.nc = nc
        self.ops = []
        self.cw = {}
        self.cr = {}

    @staticmethod
    def _cells(ap):
        sz = 4 if ap.dtype == F32 else 2
        pairs = ap.ap
        pstride, pcount = pairs[0]
        off = ap.offset
        p0 = off // pstride
        f0 = off % pstride
        ext = 0
        for step, cnt in pairs[1:]:
            ext += (cnt - 1) * step
        lo = f0 * sz
        hi = (f0 + ext + 1) * sz
        name = ap.name
        cell = 2048 if name.startswith("pp") else CELL
        out = []
        for q in range(p0 // 32, (p0 + pcount - 1) // 32 + 1):
            for ci in range(lo // cell, (hi - 1) // cell + 1):
                out.append((name, q, ci))
        return out

    def add(self, eng, fn, reads=(), writes=(), dma=False):
        oid = len(self.ops)
        deps = set()
        rc = [c for ap in reads for c in self._cells(ap)]
        wc = [c for ap in writes for c in self._cells(ap)]
        wc += [c for c in rc if c[0].startswith("pp")]
        for c in rc:
            w = self.cw.get(c)
            if w is not None:
                deps.add(w)
        for c in wc:
            w = self.cw.get(c)
            if w is not None:
                deps.add(w)
            r = self.cr.get(c)
            if r:
                deps.update(r.values())
        key = ("d", oid) if dma else eng
        for c in wc:
            self.cw[c] = oid
            self.cr[c] = {}
        for c in rc:
            self.cr.setdefault(c, {})[key] = oid
        self.ops.append(dict(eng=eng, fn=fn, deps=deps, dma=dma, sig=False, cnt=0))
        return oid

    def mm(self, out, lhsT, rhs, start, stop):
        self.add("pe", lambda e: e.matmul(out, lhsT, rhs, start=start, stop=stop), reads=[lhsT, rhs], writes=[out])

    def act(self, out, in_, func, bias=None, scale=None):
        reads = [in_]
        kw = {}
        if bias is not None:
            kw["bias"] = bias
            if not isinstance(bias, float):
                reads.append(bias)
        if scale is not None:
            kw["scale"] = scale
            if not isinstance(scale, float):
                reads.append(scale)
        self.add("act", lambda e: e.activation(out=out, in_=in_, func=func, **kw), reads=reads, writes=[out])

    def tt(self, eng, out, in0, in1, op):
        self.add(eng, lambda e: e.tensor_tensor(out=out, in0=in0, in1=in1, op=op), reads=[in0, in1], writes=[out])

    def ts(self, eng, out, in0, s1, op0, s2=None, op1=None):
        reads = [in0]
        if not isinstance(s1, float):
            reads.append(s1)
        if s2 is not None and not isinstance(s2, float):
            reads.append(s2)
        if op1 is None:
            self.add(eng, lambda e: e.tensor_scalar(out=out, in0=in0, scalar1=s1, scalar2=None, op0=op0), reads=reads, writes=[out])
        else:
            self.add(eng, lambda e: e.tensor_scalar(out=out, in0=in0, scalar1=s1, scalar2=s2, op0=op0, op1=op1), reads=reads, writes=[out])

    def stt(self, out, in0, scalar, in1, op0, op1):
        reads = [in0, in1]
        if not isinstance(scalar, float):
            reads.append(scalar)
        self.add("dve", lambda e: e.scalar_tensor_tensor(out=out, in0=in0, scalar=scalar, in1=in1, op0=op0, op1=op1),
                 reads=reads, writes=[out])

    def copy(self, eng, out, in_):
        if eng == "act":
            self.add("act", lambda e: e.copy(out=out, in_=in_), reads=[in_], writes=[out])
        else:
            self.add(eng, lambda e: e.tensor_copy(out=out, in_=in_), reads=[in_], writes=[out])

    def recip(self, out, in_):
        self.add("dve", lambda e: e.reciprocal(out=out, in_=in_), reads=[in_], writes=[out])

    def memset(self, eng, ap, val):
        self.add(eng, lambda e: e.memset(ap, val), writes=[ap])

    def dma(self, q, out, in_, reads=(), writes=()):
        self.add(q, lambda e: e.dma_start(out=out, in_=in_), reads=reads, writes=writes, dma=True)

    def emit(self, es):
        nc = self.nc
        ops = self.ops
        engs = ["pe", "act", "dve", "pool", "sp"]
        for op in ops:
            best = {}
            keep = set()
            for d in op["deps"]:
                a = ops[d]
                if a["dma"]:
                    keep.add(d)
                    continue
                if a["eng"] == "pe" and op["eng"] == "pe" and not op["dma"]:
                    continue
                if d > best.get(a["eng"], -1):
                    best[a["eng"]] = d
            for d in best.values():
                ops[d]["sig"] = True
                keep.add(d)
            op["deps"] = keep
        cnt = {e: 0 for e in engs}
        for op in ops:
            if not op["dma"] and op["sig"]:
                cnt[op["eng"]] += 1
                op["cnt"] = cnt[op["eng"]]
        NDS = 8
        esem = {e: es.enter_context(nc.semaphore("s_" + e)) for e in engs}
        dsem = {q: [es.enter_context(nc.semaphore(f"d_{q}{i}")) for i in range(NDS)] for q in ("sp", "pool", "act")}
        dk = {q: 0 for q in dsem}
        for op in ops:
            if op["dma"]:
                q = op["eng"]
                k = dk[q]
                dk[q] += 1
                op["dsem"] = dsem[q][k % NDS]
                op["dval"] = 16 * (k // NDS + 1)
                op["dprev"] = 16 * (k // NDS)
        streams = {e: [op for op in ops if op["eng"] == e] for e in engs}
        self.trace = []
        semname = {id(v): k for k, v in esem.items()}
        for q in dsem:
            for i, sm in enumerate(dsem[q]):
                semname[id(sm)] = f"d_{q}{i}"

        def run(e, name):
            waited = {}

            def wait(sem, val):
                key = id(sem)
                if waited.get(key, 0) >= val:
                    return
                waited[key] = val
                self.trace.append((name, "wait", semname.get(key, "?"), val))
                e.wait_ge(sem, val)

            for op in streams[name]:
                for d in sorted(op["deps"]):
                    a = ops[d]
                    if a["dma"]:
                        wait(a["dsem"], a["dval"])
                    else:
                        if a["eng"] == "pe" and name == "pe" and not op["dma"]:
                            continue
                        wait(esem[a["eng"]], a["cnt"])
                if op["dma"]:
                    if op["dprev"] > 0:
                        wait(op["dsem"], op["dprev"])
                    ins = op["fn"](e)
                    ins.then_inc(op["dsem"], 16)
                    self.trace.append((name, "dma", semname[id(op["dsem"])], op["dval"]))
                else:
                    ins = op["fn"](e)
                    self.trace.append((name, "op", type(ins).__name__, op["cnt"] if op["sig"] else 0))
                    if op["sig"]:
                        ins.then_inc(esem[name], 1)
            if name in dsem:
                for i in range(NDS):
                    k = dk[name]
                    n_i = (k - i + NDS - 1) // NDS if k > i else 0
                    if n_i > 0:
                        wait(dsem[name][i], 16 * n_i)

        with nc.Block() as block:
            @block.tensor
            def _(e):
                run(e, "pe")

            @block.scalar
            def _(e):
                run(e, "act")

            @block.vector
            def _(e):
                run(e, "dve")

            @block.gpsimd
            def _(e):
                run(e, "pool")

            @block.sync
            def _(e):
                run(e, "sp")


def tile_chunks(m):
    if m <= 1:
        return list(range(0, 4)), m
    if m >= 14:
        return list(range(12, 16)), m
    return list(range(m - 2, m + 3)), "int"


def build_program(stop_after=None, dbg_names=()):
    nc = bass.Bass("TRN2", target_bir_lowering=False)

    def din(name, shape, dt=F32):
        return nc.dram_tensor(name, list(shape), dt, kind="ExternalInput").ap()

    xT = din("xT", [D, T])
    cxT = din("cxT", [D, CT])
    cvec_d = din("cvec", [128, 16])
    ada_w = din("ada_w", [DEPTH, D, 6 * D])
    adab_d = din("adab", [128, DEPTH * 96])
    ng_d = din("ng", [128, DEPTH * 2 * 16])
    fng_d = din("fng", [128, 8])
    cp_d = din("cp", [128, DEPTH * 2 * 35])
    w_in = din("w_in", [DEPTH, D, DIN])
    w_inFT = din("w_inFT", [DEPTH, 256, D])
    w_f = din("w_f", [DEPTH, 256, 256])
    w_pw = din("w_pw", [DEPTH, 256, 256])
    w_out = din("w_out", [DEPTH, D, D])
    w1 = din("w1", [DEPTH, D, DFF])
    w2 = din("w2", [DEPTH, DFF, D])
    csblk_d = din("csblk", [256, 512])
    dft_d = din("dft", [8, 128, 16 * 2 * 256], BF16)
    dft256_d = din("dft256", [128, 2 * 2 * 256], BF16)
    ident_d = din("ident", [128, 128], BF16)
    bias_d = din("biasT", [DEPTH, 8, 128, NBLK * 128])
    outT = nc.dram_tensor("outT", [D, T], F32, kind="ExternalOutput").ap()
    dbg_out = {}

    es = ExitStack()
    with es:
        def sb(name, n, dt):
            return es.enter_context(nc.sbuf_tensor(name, [128, n], dt))

        H_t = sb("H", 8 * T, F32)
        Hc_t = sb("Hc", 8 * CT, F32)
        hn_t = sb("hn", 8 * T, BF16)
        hnc_t = sb("hnc", 8 * CT, BF16)
        mixc_t = sb("mixc", 8 * CT, BF16)
        AR = sb("arena", 40960, BF16)
        ident = sb("identS", 128, BF16)
        ones_rms = sb("ones_rms", 128, BF16)
        ones_ln = sb("ones_ln", 128, BF16)
        csblk = sb("csblkS", 2 * 512, BF16)
        dft256 = sb("dft256S", 2 * 2 * 256, BF16)
        wf_t = sb("wfS", 2 * 256, BF16)
        pw_t = sb("pwS", 2 * 256, BF16)
        cvec = sb("cvecS", 16, F32)
        csil = sb("csil", 16, BF16)
        adab = sb("adabS", DEPTH * 96, F32)
        ngs = sb("ngS", DEPTH * 32, F32)
        fng = sb("fngS", 8, F32)
        cps = sb("cpS", DEPTH * 70, F32)
        mod_t = sb("mod", DEPTH * 96, F32)
        coef_t = sb("coef", DEPTH * 96, F32)
        pp = [es.enter_context(nc.psum_tensor(f"pp{i}", [128, 1024], F32)) for i in range(4)]

        B = Builder(nc)

        def bank(i):
            i = i % 8
            return pp[i // 2][:, (i % 2) * 512:(i % 2) * 512 + 512]

        bank_ctr = [0]

        def nbank():
            b = bank(bank_ctr[0])
            bank_ctr[0] += 1
            return b

        H = H_t[:].rearrange("p (c t) -> p c t", c=8)
        Hc = Hc_t[:].rearrange("p (c t) -> p c t", c=8)
        hn = hn_t[:].rearrange("p (c t) -> p c t", c=8)
        hnc = hnc_t[:].rearrange("p (c t) -> p c t", c=8)
        mixc = mixc_t[:].rearrange("p (c t) -> p c t", c=8)

        def av(off_kib, n):
            o = int(round(off_kib * 512))
            return AR[:, o:o + n]

        mix = av(0, 8 * T).rearrange("p (c t) -> p c t", c=8)
        WA = av(32, 4096)
        WB = av(40, 4096)
        sqb = [av(48, 512), av(49, 512)]
        rstd_b = av(50, 1024).bitcast(F32)
        t32 = [av(52, 1024).bitcast(F32), av(54, 1024).bitcast(F32)]
        SCR = 56.0

        class Stream:
            pass

        LAT = Stream()
        LAT.T, LAT.H, LAT.hn, LAT.mix, LAT.col, LAT.name = T, H, hn, mix, 0, "lat"
        CTX = Stream()
        CTX.T, CTX.H, CTX.hn, CTX.mix, CTX.col, CTX.name = CT, Hc, hnc, mixc, 1, "ctx"

        def blocks(S, n=512):
            return [(o, min(n, S.T - o)) for o in range(0, S.T, n)]

        def coef(l, k, c, col):
            o = l * 96 + k * 16 + c * 2 + col
            return coef_t[:, o:o + 1]

        B.dma("sp", ident[:], ident_d, writes=[ident[:]])
        B.dma("sp", cvec[:], cvec_d, writes=[cvec[:]])
        B.dma("sp", adab[:], adab_d, writes=[adab[:]])
        B.dma("sp", ngs[:], ng_d, writes=[ngs[:]])
        B.dma("sp", fng[:], fng_d, writes=[fng[:]])
        B.dma("sp", cps[:], cp_d, writes=[cps[:]])
        B.dma("sp", dft256[:], dft256_d, writes=[dft256[:]])
        B.dma("pool", csblk[:].rearrange("p (c n) -> p c n", c=2), csblk_d.rearrange("(c p) n -> p c n", p=128),
              writes=[csblk[:]])
        B.memset("pool", ones_rms[:], 1.0 / D)
        B.memset("pool", ones_ln[:], 1.0 / 256)
        for c in range(8):
            B.dma("sp", H[:, c, :], xT.rearrange("(c p) t -> p c t", p=128)[:, c, :], writes=[H[:, c, :]])
        B.dma("sp", Hc, cxT.rearrange("(c p) t -> p c t", p=128), writes=[Hc_t[:]])
        B.act(csil[:], cvec[:], AF.Silu)

        def ada(l):
            csv = csil[:].rearrange("p (c k) -> p c k", c=8)
            for s in range(12):
                slot = (WA, WB)[s % 2]
                sv = slot.rearrange("p (c n) -> p c n", c=8)
                B.dma("pool", sv, ada_w[l].rearrange("(c p) n -> p c n", p=128)[:, :, s * 512:(s + 1) * 512], writes=[slot])
                ps = nbank()
                for jj in range(4):
                    for dc in range(8):
                        B.mm(ps[:, 2 * jj:2 * jj + 2], sv[:, dc, jj * 128:(jj + 1) * 128], csv[:, dc, :], dc == 0, dc == 7)
                o = l * 96 + s * 8
                B.tt("dve", mod_t[:, o:o + 8], ps[:, 0:8], adab[:, o:o + 8], ALU.add)
            m3 = mod_t[:, l * 96:(l + 1) * 96].rearrange("p (k x) -> p k x", k=6)
            c3 = coef_t[:, l * 96:(l + 1) * 96].rearrange("p (k x) -> p k x", k=6)
            n3 = ngs[:, l * 32:(l + 1) * 32].rearrange("p (k x) -> p k x", k=2)
            B.stt(c3[:, 0, :], m3[:, 1, :], 1.0, n3[:, 0, :], ALU.add, ALU.mult)
            B.copy("dve", c3[:, 1, :], m3[:, 0, :])
            B.copy("dve", c3[:, 2, :], m3[:, 2, :])
            B.stt(c3[:, 3, :], m3[:, 4, :], 1.0, n3[:, 1, :], ALU.add, ALU.mult)
            B.copy("dve", c3[:, 4, :], m3[:, 3, :])
            B.copy("dve", c3[:, 5, :], m3[:, 5, :])

        def norm(S, l, kA, kB, final=False):
            for (o, n) in blocks(S):
                ps = nbank()
                for c in range(8):
                    sq = sqb[c % 2]
                    B.act(sq[:, :n], S.H[:, c, o:o + n], AF.Square)
                    B.mm(ps[:, :n], ones_rms[:], sq[:, :n], c == 0, c == 7)
                B.act(rstd_b[:, :n], ps[:, :n], AF.Sqrt, bias=RMS_EPS_AP[0], scale=1.0)
                B.recip(rstd_b[:, :n], rstd_b[:, :n])
                for c in range(8):
                    t = t32[c % 2]
                    B.tt("dve", t[:, :n], S.H[:, c, o:o + n], rstd_b[:, :n], ALU.mult)
                    if final:
                        B.act(S.H[:, c, o:o + n], t[:, :n], AF.Identity, scale=fng[:, c:c + 1])
                    else:
                        B.act(S.hn[:, c, o:o + n], t[:, :n], AF.Identity, bias=coef(l, kB, c, S.col), scale=coef(l, kA, c, S.col))

        eps_t = sb("epsS", 2, F32)
        B.memset("pool", eps_t[:, 0:1], RMS_EPS)
        B.memset("pool", eps_t[:, 1:2], LN_EPS)
        RMS_EPS_AP = [eps_t[:, 0:1]]
        LN_EPS_AP = eps_t[:, 1:2]

        evac_rr = [0]

        def evac(out, in_):
            eng = ("act", "dve")[evac_rr[0] % 2]
            evac_rr[0] += 1
            B.copy(eng, out, in_)

        def proj_cm(S, wv, col0, evac_fn, nblk=512):
            for (o, n) in blocks(S, nblk):
                ps = nbank()
                for dc in range(8):
                    B.mm(ps[:, :n], wv[:, dc, col0:col0 + 128], S.hn[:, dc, o:o + n], dc == 0, dc == 7)
                evac_fn(ps[:, :n], o, n)

        def attention(l, last):
            QTA = av(SCR + 0, T)
            QTB = av(SCR + 4, T)
            KT = av(SCR + 8, T)
            Vp = av(SCR + 12, 16 * 256).rearrange("p (t x) -> p t x", t=16)
            KcT = av(SCR + 20, CT)
            Vc = av(SCR + 20.5, 2 * 256).rearrange("p (t x) -> p t x", t=2)
            QcA = av(48, CT)
            QcB = av(48.5, CT)
            rec = [av(52, 256).bitcast(F32), av(52.5, 256).bitcast(F32)]
            Eb = [av(0, NBLK * 128), av(5.25, NBLK * 128)]
            PT = [av(10.5 + 1.75 * i, 896) for i in range(3)]
            B.memset("pool", Vp[:, :, 64:128], 1.0)
            B.memset("pool", Vc[:, :, 64:128], 1.0)
            B.memset("pool", QTA[64:128, :], 0.0)
            B.memset("pool", QTB[0:64, :], 0.0)
            B.memset("pool", QcA[64:128, :], 0.0)
            B.memset("pool", QcB[0:64, :], 0.0)
            Sps = [pp[0], pp[1], pp[2]]
            Obank = pp[3][:, 0:512]
            Mbanks = [pp[0][:, 0:512], pp[0][:, 512:1024], pp[1][:, 0:512], pp[1][:, 512:1024],
                      pp[2][:, 0:512], pp[2][:, 512:1024], pp[3][:, 512:1024]]
            mctr = [0]

            def nM():
                b = Mbanks[mctr[0] % len(Mbanks)]
                mctr[0] += 1
                return b
            uctr = [0]

            for p in range(DBG["pairs"]):
                wv = WB.rearrange("p (c n) -> p c n", c=8)[:, :, 0:384]
                c0 = 768 + p * 384
                B.dma("pool", wv, w_in[l].rearrange("(c p) n -> p c n", p=128)[:, :, c0:c0 + 384], writes=[WB])
                for hh in range(2):
                    B.dma("pool", Eb[hh], bias_d[l, 2 * p + hh], writes=[Eb[hh]])
                    B.act(Eb[hh], Eb[hh], AF.Exp)

                def ev_kc(ps, o, n):
                    evac(KcT[:, o:o + n], ps)
                proj_cm_fixed(CTX, wv, 128, ev_kc, nM)
                if not last:
                    def ev_qc(ps, o, n):
                        B.copy("act", QcA[0:64, o:o + n], ps[0:64, :])
                        B.copy("dve", QcB[64:128, o:o + n], ps[64:128, :])
                    proj_cm_fixed(CTX, wv, 0, ev_qc, nM)
                for t in range(2):
                    mb = nM()
                    for dc in range(8):
                        B.mm(mb[:, 0:128], hnc[:, dc, t * 128:(t + 1) * 128], wv[:, dc, 256:384], dc == 0, dc == 7)
                    B.copy("dve", Vc[:, t, :].rearrange("p (a b) -> p a b", a=2)[:, :, 0:64],
                           mb[:, 0:128].rearrange("p (a b) -> p a b", a=2))

                def ev_q(ps, o, n):
                    B.copy("act", QTA[0:64, o:o + n], ps[0:64, :])
                    B.copy("dve", QTB[64:128, o:o + n], ps[64:128, :])

                def ev_k(ps, o, n):
                    evac(KT[:, o:o + n], ps)
                proj_cm_fixed(LAT, wv, 0, ev_q, nM)
                proj_cm_fixed(LAT, wv, 128, ev_k, nM)
                for t4 in range(4):
                    mb = nM()
                    for tt_ in range(4):
                        t = t4 * 4 + tt_
                        for dc in range(8):
                            B.mm(mb[:, tt_ * 128:(tt_ + 1) * 128], hn[:, dc, t * 128:(t + 1) * 128], wv[:, dc, 256:384], dc == 0, dc == 7)
                    for tt_ in range(4):
                        t = t4 * 4 + tt_
                        B.copy(("dve", "act")[tt_ % 2], Vp[:, t, :].rearrange("p (a b) -> p a b", a=2)[:, :, 0:64],
                               mb[:, tt_ * 128:(tt_ + 1) * 128].rearrange("p (a b) -> p a b", a=2))

                units = []
                if not last:
                    for hh in range(2):
                        for m in range(2):
                            units.append(("ctx", hh, m))
                for hh in range(2):
                    for m in range(16):
                        units.append(("lat", hh, m))
                if DBG["units"] is not None:
                    units = units[:DBG["units"]]

                def unit_S(u):
                    kind, hh, m = units[u]
                    k = uctr[0] + u
                    Sp = Sps[k % 3]
                    if kind == "lat":
                        chunks, _ = tile_chunks(m)
                        q = (QTA, QTB)[hh][:, m * 128:(m + 1) * 128]
                        for i, j in enumerate(chunks):
                            B.mm(Sp[:, i * 128:(i + 1) * 128], KT[:, j * 128:(j + 1) * 128], q, True, True)
                        nl = len(chunks)
                    else:
                        q = (QcA, QcB)[hh][:, m * 128:(m + 1) * 128]
                        nl = 0
                    for cc in range(2):
                        B.mm(Sp[:, (nl + cc) * 128:(nl + cc + 1) * 128], KcT[:, cc * 128:(cc + 1) * 128], q, True, True)

                def unit_rest(u):
                    kind, hh, m = units[u]
                    k = uctr[0] + u
                    Sp = Sps[k % 3]
                    P = PT[k % 3]
                    if kind == "lat":
                        chunks, typ = tile_chunks(m)
                    else:
                        chunks, typ = [], None
                    nl = len(chunks)
                    ns = nl + 2
                    B.act(P[:, 0:ns * 128], Sp[:, 0:ns * 128], AF.Exp, scale=0.125)
                    if nl:
                        eo = EOFF[typ] * 128
                        B.tt("pool", P[:, 0:nl * 128], P[:, 0:nl * 128], Eb[hh][:, eo:eo + nl * 128], ALU.mult)
                    if DBG["parts"] < 3:
                        return
                    Op = Obank[:, (k % 4) * 128:(k % 4 + 1) * 128]
                    vo = 64 * hh
                    for i, j in enumerate(chunks):
                        B.mm(Op, Vp[:, j, vo:vo + 128], P[:, i * 128:(i + 1) * 128], i == 0, False)
                    for cc in range(2):
                        B.mm(Op, Vc[:, cc, vo:vo + 128], P[:, (nl + cc) * 128:(nl + cc + 1) * 128], (nl == 0 and cc == 0), cc == 1)
                    if DBG["parts"] < 4:
                        B.copy("act", rec[0][:, :], Op)
                        return
                    r = rec[k % 2]
                    olo, ohi = (0, 64) if hh == 0 else (64, 128)
                    dlo, dhi = (64, 128) if hh == 0 else (0, 64)
                    B.recip(r[dlo:dhi, :], Op[dlo:dhi, :])
                    dst = (mix if kind == "lat" else mixc)[olo:ohi, 4 + p, m * 128:(m + 1) * 128]
                    B.tt("dve", dst, Op[olo:ohi, :], r[dlo:dhi, :], ALU.mult)

                LOOK = 2
                for u in range(min(LOOK, len(units))):
                    unit_S(u)
                for u in range(len(units)):
                    if u + LOOK < len(units):
                        unit_S(u + LOOK)
                    unit_rest(u)
                uctr[0] += len(units)

        def proj_cm_fixed(S, wv, col0, evac_fn, nb):
            for (o, n) in blocks(S):
                psb = nb()
                for dc in range(8):
                    B.mm(psb[:, :n], wv[:, dc, col0:col0 + 128], S.hn[:, dc, o:o + n], dc == 0, dc == 7)
                evac_fn(psb[:, :n], o, n)

        def conv_phase(l, streams):
            wv = WA.rearrange("p (c n) -> p c n", c=8)
            B.dma("pool", wv, w_in[l].rearrange("(c p) n -> p c n", p=128)[:, :, 256:768], writes=[WA])
            B.dma("pool", pw_t[:].rearrange("p (c n) -> p c n", c=2), w_pw[l].rearrange("(c p) n -> p c n", p=128), writes=[pw_t[:]])
            pwv = pw_t[:].rearrange("p (c n) -> p c n", c=2)
            vpad = av(SCR, 2 * (T + 30)).rearrange("p (c t) -> p c t", c=2)
            vpadc = av(0, 2 * (CT + 30)).rearrange("p (c t) -> p c t", c=2)
            doff = SCR + (2 * (T + 30) * 2) / 1024.0
            doff = math.ceil(doff * 16) / 16.0
            Dg = av(doff, 62 * 128).rearrange("p (c k n) -> p c k n", c=2, k=31)
            cpl = cps[:, l * 70:(l + 1) * 70].rearrange("p (c k) -> p c k", c=2)
            sig = av(2, 512)
            ybf = [av(3, 512), av(4, 512)]
            ysq = [av(5, 512), av(6, 512)]
            y32 = [t32[0], t32[1]]
            z32 = rstd_b
            sil = [sqb[0], sqb[1]]
            for ch in range(2):
                for k in range(31):
                    B.ts("dve", Dg[:, ch, k, :], ident[:], cpl[:, ch, k:k + 1], ALU.mult)
            for S in streams:
                vp = vpad if S is LAT else vpadc
                B.memset("pool", vp[:, :, 0:15], 0.0)
                B.memset("pool", vp[:, :, 15 + S.T:30 + S.T], 0.0)
                for (o, n) in blocks(S):
                    for ch in range(2):
                        pa = nbank()
                        pg = nbank()
                        for dc in range(8):
                            B.mm(pa[:, :n], wv[:, dc, ch * 128:(ch + 1) * 128], S.hn[:, dc, o:o + n], dc == 0, dc == 7)
                        for dc in range(8):
                            B.mm(pg[:, :n], wv[:, dc, 256 + ch * 128:256 + (ch + 1) * 128], S.hn[:, dc, o:o + n], dc == 0, dc == 7)
                        B.act(sig[:, :n], pg[:, :n], AF.Sigmoid)
                        B.tt("dve", vp[:, ch, 15 + o:15 + o + n], pa[:, :n], sig[:, :n], ALU.mult)
                for (o, n) in blocks(S):
                    pm = nbank()
                    pq = nbank()
                    for ch in range(2):
                        pc = nbank()
                        for k in range(31):
                            B.mm(pc[:, :n], Dg[:, ch, k, :], vp[:, ch, o + k:o + k + n], k == 0, k == 30)
                        B.act(ybf[ch][:, :n], pc[:, :n], AF.Identity, bias=cpl[:, ch, 31:32], scale=1.0)
                        B.act(ysq[ch][:, :n], pc[:, :n], AF.Square, bias=cpl[:, ch, 31:32], scale=1.0)
                        B.ts("dve", y32[ch][:, :n], pc[:, :n], cpl[:, ch, 31:32], ALU.add)
                    for ch in range(2):
                        B.mm(pm[:, :n], ones_ln[:], ybf[ch][:, :n], ch == 0, ch == 1)
                    for ch in range(2):
                        B.mm(pq[:, :n], ones_ln[:], ysq[ch][:, :n], ch == 0, ch == 1)
                    B.act(z32[:, :n], pm[:, :n], AF.Square)
                    B.tt("dve", z32[:, :n], pq[:, :n], z32[:, :n], ALU.subtract)
                    B.act(z32[:, :n], z32[:, :n], AF.Sqrt, bias=LN_EPS_AP, scale=1.0)
                    B.recip(z32[:, :n], z32[:, :n])
                    for ch in range(2):
                        B.tt("dve", y32[ch][:, :n], y32[ch][:, :n], pm[:, :n], ALU.subtract)
                        B.tt("dve", y32[ch][:, :n], y32[ch][:, :n], z32[:, :n], ALU.mult)
                        B.act(sil[ch][:, :n], y32[ch][:, :n], AF.Silu, bias=cpl[:, ch, 33:34], scale=cpl[:, ch, 32:33])
                    for oc in range(2):
                        po = nbank()
                        for ch in range(2):
                            B.mm(po[:, :n], pwv[:, ch, oc * 128:(oc + 1) * 128], sil[ch][:, :n], ch == 0, ch == 1)
                        B.act(S.mix[:, 2 + oc, o:o + n], po[:, :n], AF.Identity, bias=cpl[:, oc, 34:35], scale=1.0)

        def fourier_phase(l, streams):
            ftv = WB[:, 0:2048].rearrange("p (c n) -> p c n", c=2)
            B.dma("pool", ftv, w_inFT[l].rearrange("(c p) n -> p c n", p=128), writes=[WB[:, 0:2048]])
            B.dma("pool", wf_t[:].rearrange("p (c n) -> p c n", c=2), w_f[l].rearrange("(c p) n -> p c n", p=128), writes=[wf_t[:]])
            wfv = wf_t[:].rearrange("p (c n) -> p c n", c=2)
            csv = csblk[:].rearrange("p (c n) -> p c n", c=2)
            Wp = WA.rearrange("p (c n) -> p c n", c=8)
            for dc in range(8):
                ps = nbank()
                for cc in range(2):
                    B.mm(ps, ftv[:, cc, dc * 128:(dc + 1) * 128], csv[:, cc, :], cc == 0, cc == 1)
                evac(Wp[:, dc, :], ps)
            UU = av(SCR, 16 * 512).rearrange("p (t n) -> p t n", t=16)
            UUc = av(SCR + 16, 2 * 512).rearrange("p (t n) -> p t n", t=2)
            FT = av(SCR + 18, 2 * 256).rearrange("p (c n) -> p c n", c=2)
            for S in streams:
                U = UU if S is LAT else UUc
                nt = S.T // 128
                for t in range(nt):
                    ps = nbank()
                    for dc in range(8):
                        B.mm(ps, S.hn[:, dc, t * 128:(t + 1) * 128], Wp[:, dc, :], dc == 0, dc == 7)
                    evac(U[:, t, :], ps)
            for S in streams:
                U = UU if S is LAT else UUc
                nt = S.T // 128
                njb = S.T // 256
                for jb in range(njb):
                    if S is LAT:
                        dbuf = hn_t[:, (jb % 2) * 8192:(jb % 2 + 1) * 8192]
                        B.dma("sp", dbuf, dft_d[jb], writes=[dbuf])
                        dv = dbuf.rearrange("p (k s j) -> p k s j", k=16, s=2)
                    else:
                        dv = dft256[:].rearrange("p (k s j) -> p k s j", k=2, s=2)
                    for cg in range(2):
                        ps = nbank()
                        first = True
                        for kc in range(nt):
                            for s in range(2):
                                B.mm(ps[:, 0:256], U[:, kc, s * 256 + cg * 128:s * 256 + (cg + 1) * 128], dv[:, kc, s, :],
                                     first, (kc == nt - 1 and s == 1))
                                first = False
                        evac(FT[:, cg, :], ps[:, 0:256])
                    for oc in range(2):
                        ps = nbank()
                        for cg in range(2):
                            B.mm(ps[:, 0:256], wfv[:, cg, oc * 128:(oc + 1) * 128], FT[:, cg, :], cg == 0, cg == 1)
                        evac(S.mix[:, oc, jb * 256:(jb + 1) * 256], ps[:, 0:256])

        def outproj(l, streams):
            wo = AR[:, 32 * 512:48 * 512].rearrange("p (c n) -> p c n", c=8)
            B.dma("pool", wo[:, 0:4, :], w_out[l].rearrange("(c p) n -> p c n", p=128)[:, 0:4, :], writes=[WA])
            B.dma("pool", wo[:, 4:8, :], w_out[l].rearrange("(c p) n -> p c n", p=128)[:, 4:8, :], writes=[WB])
            for S in streams:
                for (o, n) in blocks(S):
                    for oc in range(8):
                        ps = nbank()
                        for kc in range(8):
                            B.mm(ps[:, :n], wo[:, kc, oc * 128:(oc + 1) * 128], S.mix[:, kc, o:o + n], kc == 0, kc == 7)
                        B.stt(S.H[:, oc, o:o + n], ps[:, :n], coef(l, 2, oc, S.col), S.H[:, oc, o:o + n], ALU.mult, ALU.add)

        def mlp(l, streams, after_first_norm):
            slot_off = [8, 16, 24, 32, 40, 56, 64, 72]
            sets = [[5, 6, 7, 0], [1, 2, 3, 4]]
            hid = [av(0, 2048).rearrange("p (f t) -> p f t", f=8), av(4, 2048).rearrange("p (f t) -> p f t", f=8)]
            rl = [sqb[0][:, 0:256], sqb[1][:, 0:256]]
            hctr = 0
            for fb in range(4):
                st = sets[fb % 2]
                w1s = [av(slot_off[st[0]], 4096).rearrange("p (c n) -> p c n", c=8),
                       av(slot_off[st[1]], 4096).rearrange("p (c n) -> p c n", c=8)]
                w2s = [av(slot_off[st[2]], 4096).rearrange("p (c n) -> p c n", c=4),
                       av(slot_off[st[3]], 4096).rearrange("p (c n) -> p c n", c=4)]
                for hhalf in range(2):
                    c0 = fb * 1024 + hhalf * 512
                    B.dma("pool", w1s[hhalf], w1[l].rearrange("(c p) n -> p c n", p=128)[:, :, c0:c0 + 512],
                          writes=[av(slot_off[st[hhalf]], 4096)])
                for hhalf in range(2):
                    r0 = fb * 8 + hhalf * 4
                    B.dma("pool", w2s[hhalf], w2[l].rearrange("(c p) n -> p c n", p=128)[:, r0:r0 + 4, :],
                          writes=[av(slot_off[st[2 + hhalf]], 4096)])
                if fb == 0 and after_first_norm is not None:
                    after_first_norm()
                for S in streams:
                    for (o, n) in blocks(S, 256):
                        hb = hid[hctr % 2]
                        hctr += 1
                        for fc in range(8):
                            ps = nbank()
                            wv = w1s[fc // 4]
                            cc = (fc % 4) * 128
                            for dc in range(8):
                                B.mm(ps[:, :n], wv[:, dc, cc:cc + 128], S.hn[:, dc, o:o + n], dc == 0, dc == 7)
                            r = rl[fc % 2]
                            B.act(r[:, :n], ps[:, :n], AF.Relu)
                            B.tt("dve", hb[:, fc, :n], r[:, :n], r[:, :n], ALU.mult)
                        for oc in range(8):
                            ps = nbank()
                            for fc in range(8):
                                B.mm(ps[:, :n], w2s[fc // 4][:, fc % 4, oc * 128:(oc + 1) * 128], hb[:, fc, :n], fc == 0, fc == 7)
                            B.stt(S.H[:, oc, o:o + n], ps[:, :n], coef(l, 5, oc, S.col), S.H[:, oc, o:o + n], ALU.mult, ALU.add)

        def dump(name, ap_sb, shape):
            d = nc.dram_tensor("dbg_" + name, list(shape), ap_sb.dtype, kind="ExternalOutput").ap()
            B.dma("sp", d, ap_sb, reads=[ap_sb])
            dbg_out[name] = True

        done = False
        for l in range(DEPTH):
            last = l == DEPTH - 1
            streams = [LAT] if last else [CTX, LAT]
            ada(l)
            if stop_after == f"ada{l}":
                dump("mod", mod_t[:], [128, DEPTH * 96])
                dump("coef", coef_t[:], [128, DEPTH * 96])
                done = True
                break
            norm(CTX, l, 0, 1)
            norm(LAT, l, 0, 1)
            if stop_after == f"norm{l}":
                dump("hn", hn_t[:], [128, 8 * T])
                dump("hnc", hnc_t[:], [128, 8 * CT])
                done = True
                break
            attention(l, last)
            if stop_after == f"attn{l}":
                dump("mix", AR[:, 0:8 * T], [128, 8 * T])
                dump("mixc", mixc_t[:], [128, 8 * CT])
                done = True
                break
            conv_phase(l, streams)
            if stop_after == f"conv{l}":
                dump("mix", AR[:, 0:8 * T], [128, 8 * T])
                dump("mixc", mixc_t[:], [128, 8 * CT])
                done = True
                break
            fourier_phase(l, streams)
            if stop_after == f"four{l}":
                dump("mix", AR[:, 0:8 * T], [128, 8 * T])
                dump("mixc", mixc_t[:], [128, 8 * CT])
                done = True
                break
            outproj(l, streams)
            if stop_after == f"oproj{l}":
                dump("H", H_t[:], [128, 8 * T])
                dump("Hc", Hc_t[:], [128, 8 * CT])
                done = True
                break
            for S in streams:
                norm(S, l, 3, 4)
            mlp(l, streams, None)
            if stop_after == f"mlp{l}":
                dump("H", H_t[:], [128, 8 * T])
                dump("Hc", Hc_t[:], [128, 8 * CT])
                done = True
                break
        if not done:
            norm(LAT, 0, 0, 0, final=True)
        for c in range(8):
            B.dma("sp", outT.rearrange("(c p) t -> p c t", p=128)[:, c, :], H[:, c, :], reads=[H[:, c, :]])
        B.emit(es)
    _CACHE['trace'] = B.trace
    return nc, sorted(dbg_out.keys()), len(B.ops)


def _bias_tables(rpb):
    kc = np.arange(64)
    qc = np.arange(64)
    cs = np.clip(qc - 8, 0, 48)
    colvalid = (kc[:, None] >= cs[None, :]) & (kc[:, None] < cs[None, :] + 16)
    coloff = np.clip(kc[:, None] - qc[None, :] + 15, 0, 30)
    blocks = []
    specs = [(5, j) for j in range(3, 8)] + [(0, j) for j in range(4)] + [(1, j) for j in range(4)] + \
            [(14, j) for j in range(12, 16)] + [(15, j) for j in range(12, 16)]
    out = np.full((DEPTH, 8, 128, NBLK, 128), NEG, np.float32)
    for bi, (m, j) in enumerate(specs):
        for a in range(2):
            for b in range(2):
                krow = 2 * j + a
                qrow = 2 * m + b
                rs = min(max(qrow - 4, 0), 24)
                if not (rs <= krow < rs + 8):
                    continue
                dr = krow - qrow + 7
                vals = rpb[:, :, dr, :][:, :, coloff]
                vals = np.where(colvalid[None, None], vals, np.float32(NEG))
                out[:, :, a * 64:(a + 1) * 64, bi, b * 64:(b + 1) * 64] = vals
    return np.ascontiguousarray(out.reshape(DEPTH, 8, 128, NBLK * 128))


def _dft_tables():
    k = np.arange(T, dtype=np.int64)
    ang = 2.0 * np.pi * ((k[:, None] * k[None, :]) % T).astype(np.float64) / T
    Cm = (np.cos(ang) / math.sqrt(T)).astype(np.float32)
    Sm = (-np.sin(ang) / math.sqrt(T)).astype(np.float32)
    CS = np.stack([Cm, Sm], axis=0)
    CS = CS.reshape(2, 16, 128, 8, 256)
    dft = np.ascontiguousarray(CS.transpose(3, 2, 1, 0, 4)).reshape(8, 128, 16 * 2 * 256).astype(NPBF)
    k2 = np.arange(CT, dtype=np.int64)
    ang2 = 2.0 * np.pi * ((k2[:, None] * k2[None, :]) % CT).astype(np.float64) / CT
    C2 = (np.cos(ang2) / math.sqrt(CT)).astype(np.float32)
    S2 = (-np.sin(ang2) / math.sqrt(CT)).astype(np.float32)
    CS2 = np.stack([C2, S2], axis=0).reshape(2, 2, 128, 256)
    dft256 = np.ascontiguousarray(CS2.transpose(2, 1, 0, 3)).reshape(128, 2 * 2 * 256).astype(NPBF)
    a = np.arange(64)
    ang3 = 2.0 * np.pi * ((a[:, None] * a[None, :]) % 64) / 64.0
    cb = np.zeros((256, 256), np.float32)
    sbk = np.zeros((256, 256), np.float32)
    for g in range(4):
        cb[g * 64:(g + 1) * 64, g * 64:(g + 1) * 64] = np.cos(ang3) / 8.0
        sbk[g * 64:(g + 1) * 64, g * 64:(g + 1) * 64] = np.sin(ang3) / 8.0
    csblk = np.ascontiguousarray(np.concatenate([cb, sbk], axis=1)).astype(np.float32)
    return dft, dft256, csblk


def host_prep(inp):
    f = lambda a: np.ascontiguousarray(np.asarray(a, dtype=np.float32))
    x, c, ctx, c_ctx = f(inp["x"]), f(inp["c"]), f(inp["ctx"]), f(inp["c_ctx"])
    shared = {}
    shared["ada_w"] = f(inp["ada_w"])
    ab = f(inp["ada_b"]).reshape(DEPTH, 48, 128).transpose(2, 0, 1)
    shared["adab"] = np.ascontiguousarray(np.repeat(ab[:, :, :, None], 2, axis=3)).reshape(128, DEPTH * 96)
    n1 = f(inp["norm1_g"]).reshape(DEPTH, 8, 128).transpose(2, 0, 1)
    n2 = f(inp["norm2_g"]).reshape(DEPTH, 8, 128).transpose(2, 0, 1)
    ng = np.stack([n1, n2], axis=2)
    shared["ng"] = np.ascontiguousarray(np.repeat(ng[..., None], 2, axis=4)).reshape(128, DEPTH * 32)
    shared["fng"] = np.ascontiguousarray(f(inp["final_norm_g"]).reshape(8, 128).T)
    dw = f(inp["conv_dw_w"]).reshape(DEPTH, 31, 2, 128).transpose(3, 0, 2, 1)
    def pv(name):
        return f(inp[name]).reshape(DEPTH, 2, 128).transpose(2, 0, 1)[..., None]
    cp = np.concatenate([dw, pv("conv_dw_b"), pv("conv_norm_g"), pv("conv_norm_b"), pv("conv_pw_b")], axis=3)
    shared["cp"] = np.ascontiguousarray(cp).reshape(128, DEPTH * 70)
    w_in = f(inp["w_in"])
    order = list(range(0, 768))
    for p in range(4):
        for part in range(3):
            base = 768 + part * 512 + p * 128
            order += list(range(base, base + 128))
    shared["w_in"] = np.ascontiguousarray(w_in[:, :, order])
    shared["w_inFT"] = np.ascontiguousarray(w_in[:, :, :256].transpose(0, 2, 1))
    shared["w_f"] = f(inp["w_fourier"])
    shared["w_pw"] = f(inp["conv_pw_w"])
    shared["w_out"] = f(inp["w_out"])
    shared["w1"] = f(inp["mlp_w1"])
    shared["w2"] = f(inp["mlp_w2"])
    dft, dft256, csblk = _dft_tables()
    shared["dft"] = dft
    shared["dft256"] = dft256
    shared["csblk"] = csblk
    shared["ident"] = np.eye(128, dtype=np.float32).astype(NPBF)
    shared["biasT"] = _bias_tables(f(inp["na_rpb"]))
    maps = []
    cc = c_ctx.reshape(8, 128).T
    for b in range(NCORES):
        m = dict(shared)
        m["xT"] = np.ascontiguousarray(x[b].T)
        m["cxT"] = np.ascontiguousarray(ctx[b].T)
        cb = c[b].reshape(8, 128).T
        m["cvec"] = np.ascontiguousarray(np.stack([cb, cc], axis=2)).reshape(128, 16)
        maps.append(m)
    return maps


def kernel(**inputs):
    maps = host_prep(inputs)
    if "nc" not in _CACHE:
        _CACHE["nc"] = build_program()[0]
    nc = _CACHE["nc"]
    res = run_bass_kernel_spmd(nc, maps, core_ids=list(range(NCORES)))
    out = np.stack([np.ascontiguousarray(res.results[b]["outT"].T) for b in range(NCORES)], axis=0)
    return out.astype(np.float32)
```

```python
import math
from contextlib import ExitStack

import numpy as np
import ml_dtypes
import concourse.bass as bass
import concourse.mybir as mybir
from concourse.bass_utils import run_bass_kernel_spmd

F32 = mybir.dt.float32
BF16 = mybir.dt.bfloat16
AF = mybir.ActivationFunctionType
ALU = mybir.AluOpType
NPBF = ml_dtypes.bfloat16

D = 1024
T = 2048
CT = 256
DEPTH = 2
DIN = 2304
DFF = 4096
NCORES = 8
NEG = -30000.0
NBLK = 21
EOFF = {"int": 0, 0: 5, 1: 9, 14: 13, 15: 17}
RMS_EPS = 1e-6
LN_EPS = 1e-5
CELL = 512
_CACHE = {}
DBG = {"pairs": 4, "units": None, "parts": 5, "pv": 0}


class Builder:
    def __init__(self, nc):
        self.nc = nc
        self.ops = []
        self.cw = {}
        self.cr = {}

    @staticmethod
    def _cells(ap):
        sz = 4 if ap.dtype == F32 else 2
        pairs = ap.ap
        pstride, pcount = pairs[0]
        off = ap.offset
        p0 = off // pstride
        f0 = off % pstride
        ext = 0
        for step, cnt in pairs[1:]:
            ext += (cnt - 1) * step
        lo = f0 * sz
        hi = (f0 + ext + 1) * sz
        name = ap.name
        cell = 2048 if name.startswith("pp") else CELL
        out = []
        for q in range(p0 // 32, (p0 + pcount - 1) // 32 + 1):
            for ci in range(lo // cell, (hi - 1) // cell + 1):
                out.append((name, q, ci))
        return out

    def add(self, eng, fn, reads=(), writes=(), dma=False):
        oid = len(self.ops)
        deps = set()
        rc = [c for ap in reads for c in self._cells(ap)]
        wc = [c for ap in writes for c in self._cells(ap)]
        wc += [c for c in rc if c[0].startswith("pp")]
        for c in rc:
            w = self.cw.get(c)
            if w is not None:
                deps.add(w)
        for c in wc:
            w = self.cw.get(c)
            if w is not None:
                deps.add(w)
            r = self.cr.get(c)
            if r:
                deps.update(r.values())
        key = ("d", oid) if dma else eng
        for c in wc:
            self.cw[c] = oid
            self.cr[c] = {}
        for c in rc:
            self.cr.setdefault(c, {})[key] = oid
        self.ops.append(dict(eng=eng, fn=fn, deps=deps, dma=dma, sig=False, cnt=0))
        return oid

    def mm(self, out, lhsT, rhs, start, stop):
        self.add("pe", lambda e: e.matmul(out, lhsT, rhs, start=start, stop=stop), reads=[lhsT, rhs], writes=[out])

    def act(self, out, in_, func, bias=None, scale=None):
        reads = [in_]
        kw = {}
        if bias is not None:
            kw["bias"] = bias
            if not isinstance(bias, float):
                reads.append(bias)
        if scale is not None:
            kw["scale"] = scale
            if not isinstance(scale, float):
                reads.append(scale)
        self.add("act", lambda e: e.activation(out=out, in_=in_, func=func, **kw), reads=reads, writes=[out])

    def tt(self, eng, out, in0, in1, op):
        self.add(eng, lambda e: e.tensor_tensor(out=out, in0=in0, in1=in1, op=op), reads=[in0, in1], writes=[out])

    def ts(self, eng, out, in0, s1, op0, s2=None, op1=None):
        reads = [in0]
        if not isinstance(s1, float):
            reads.append(s1)
        if s2 is not None and not isinstance(s2, float):
            reads.append(s2)
        if op1 is None:
            self.add(eng, lambda e: e.tensor_scalar(out=out, in0=in0, scalar1=s1, scalar2=None, op0=op0), reads=reads, writes=[out])
        else:
            self.add(eng, lambda e: e.tensor_scalar(out=out, in0=in0, scalar1=s1, scalar2=s2, op0=op0, op1=op1), reads=reads, writes=[out])

    def stt(self, out, in0, scalar, in1, op0, op1):
        reads = [in0, in1]
        if not isinstance(scalar, float):
            reads.append(scalar)
        self.add("dve", lambda e: e.scalar_tensor_tensor(out=out, in0=in0, scalar=scalar, in1=in1, op0=op0, op1=op1),
                 reads=reads, writes=[out])

    def copy(self, eng, out, in_):
        if eng == "act":
            self.add("act", lambda e: e.copy(out=out, in_=in_), reads=[in_], writes=[out])
        else:
            self.add(eng, lambda e: e.tensor_copy(out=out, in_=in_), reads=[in_], writes=[out])

    def recip(self, out, in_):
        self.add("dve", lambda e: e.reciprocal(out=out, in_=in_), reads=[in_], writes=[out])

    def memset(self, eng, ap, val):
        self.add(eng, lambda e: e.memset(ap, val), writes=[ap])

    def dma(self, q, out, in_, reads=(), writes=()):
        self.add(q, lambda e: e.dma_start(out=out, in_=in_), reads=reads, writes=writes, dma=True)

    def emit(self, es):
        nc = self.nc
        ops = self.ops
        engs = ["pe", "act", "dve", "pool", "sp"]
        for op in ops:
            best = {}
            keep = set()
            for d in op["deps"]:
                a = ops[d]
                if a["dma"]:
                    keep.add(d)
                    continue
                if a["eng"] == "pe" and op["eng"] == "pe" and not op["dma"]:
                    continue
                if d > best.get(a["eng"], -1):
                    best[a["eng"]] = d
            for d in best.values():
                ops[d]["sig"] = True
                keep.add(d)
            op["deps"] = keep
        cnt = {e: 0 for e in engs}
        for op in ops:
            if not op["dma"] and op["sig"]:
                cnt[op["eng"]] += 1
                op["cnt"] = cnt[op["eng"]]
        NDS = 8
        esem = {e: es.enter_context(nc.semaphore("s_" + e)) for e in engs}
        dsem = {q: [es.enter_context(nc.semaphore(f"d_{q}{i}")) for i in range(NDS)] for q in ("sp", "pool", "act")}
        dk = {q: 0 for q in dsem}
        for op in ops:
            if op["dma"]:
                q = op["eng"]
                k = dk[q]
                dk[q] += 1
                op["dsem"] = dsem[q][k % NDS]
                op["dval"] = 16 * (k // NDS + 1)
                op["dprev"] = 16 * (k // NDS)
        streams = {e: [op for op in ops if op["eng"] == e] for e in engs}
        self.trace = []
        semname = {id(v): k for k, v in esem.items()}
        for q in dsem:
            for i, sm in enumerate(dsem[q]):
                semname[id(sm)] = f"d_{q}{i}"

        def run(e, name):
            waited = {}

            def wait(sem, val):
                key = id(sem)
                if waited.get(key, 0) >= val:
                    return
                waited[key] = val
                self.trace.append((name, "wait", semname.get(key, "?"), val))
                e.wait_ge(sem, val)

            for op in streams[name]:
                for d in sorted(op["deps"]):
                    a = ops[d]
                    if a["dma"]:
                        wait(a["dsem"], a["dval"])
                    else:
                        if a["eng"] == "pe" and name == "pe" and not op["dma"]:
                            continue
                        wait(esem[a["eng"]], a["cnt"])
                if op["dma"]:
                    if op["dprev"] > 0:
                        wait(op["dsem"], op["dprev"])
                    ins = op["fn"](e)
                    ins.then_inc(op["dsem"], 16)
                    self.trace.append((name, "dma", semname[id(op["dsem"])], op["dval"]))
                else:
                    ins = op["fn"](e)
                    self.trace.append((name, "op", type(ins).__name__, op["cnt"] if op["sig"] else 0))
                    if op["sig"]:
                        ins.then_inc(esem[name], 1)
            if name in dsem:
                for i in range(NDS):
                    k = dk[name]
                    n_i = (k - i + NDS - 1) // NDS if k > i else 0
                    if n_i > 0:
                        wait(dsem[name][i], 16 * n_i)

        with nc.Block() as block:
            @block.tensor
            def _(e):
                run(e, "pe")

            @block.scalar
            def _(e):
                run(e, "act")

            @block.vector
            def _(e):
                run(e, "dve")

            @block.gpsimd
            def _(e):
                run(e, "pool")

            @block.sync
            def _(e):
                run(e, "sp")


def tile_chunks(m):
    if m <= 1:
        return list(range(0, 4)), m
    if m >= 14:
        return list(range(12, 16)), m
    return list(range(m - 2, m + 3)), "int"


def build_program(stop_after=None, dbg_names=()):
    nc = bass.Bass("TRN2", target_bir_lowering=False)

    def din(name, shape, dt=F32):
        return nc.dram_tensor(name, list(shape), dt, kind="ExternalInput").ap()

    xT = din("xT", [D, T])
    cxT = din("cxT", [D, CT])
    cvec_d = din("cvec", [128, 16])
    ada_w = din("ada_w", [DEPTH, D, 6 * D])
    adab_d = din("adab", [128, DEPTH * 96])
    ng_d = din("ng", [128, DEPTH * 2 * 16])
    fng_d = din("fng", [128, 8])
    cp_d = din("cp", [128, DEPTH * 2 * 35])
    w_in = din("w_in", [DEPTH, D, DIN])
    w_inFT = din("w_inFT", [DEPTH, 256, D])
    w_f = din("w_f", [DEPTH, 256, 256])
    w_pw = din("w_pw", [DEPTH, 256, 256])
    w_out = din("w_out", [DEPTH, D, D])
    w1 = din("w1", [DEPTH, D, DFF])
    w2 = din("w2", [DEPTH, DFF, D])
    csblk_d = din("csblk", [256, 512])
    dft_d = din("dft", [8, 128, 16 * 2 * 256], BF16)
    dft256_d = din("dft256", [128, 2 * 2 * 256], BF16)
    ident_d = din("ident", [128, 128], BF16)
    bias_d = din("biasT", [DEPTH, 8, 128, NBLK * 128])
    outT = nc.dram_tensor("outT", [D, T], F32, kind="ExternalOutput").ap()
    dbg_out = {}

    es = ExitStack()
    with es:
        def sb(name, n, dt):
            return es.enter_context(nc.sbuf_tensor(name, [128, n], dt))

        H_t = sb("H", 8 * T, F32)
        Hc_t = sb("Hc", 8 * CT, F32)
        hn_t = sb("hn", 8 * T, BF16)
        hnc_t = sb("hnc", 8 * CT, BF16)
        mixc_t = sb("mixc", 8 * CT, BF16)
        AR = sb("arena", 40960, BF16)
        ident = sb("identS", 128, BF16)
        ones_rms = sb("ones_rms", 128, BF16)
        ones_ln = sb("ones_ln", 128, BF16)
        csblk = sb("csblkS", 2 * 512, BF16)
        dft256 = sb("dft256S", 2 * 2 * 256, BF16)
        wf_t = sb("wfS", 2 * 256, BF16)
        pw_t = sb("pwS", 2 * 256, BF16)
        cvec = sb("cvecS", 16, F32)
        csil = sb("csil", 16, BF16)
        adab = sb("adabS", DEPTH * 96, F32)
        ngs = sb("ngS", DEPTH * 32, F32)
        fng = sb("fngS", 8, F32)
        cps = sb("cpS", DEPTH * 70, F32)
        mod_t = sb("mod", DEPTH * 96, F32)
        coef_t = sb("coef", DEPTH * 96, F32)
        pp = [es.enter_context(nc.psum_tensor(f"pp{i}", [128, 1024], F32)) for i in range(4)]

        B = Builder(nc)

        def bank(i):
            i = i % 8
            return pp[i // 2][:, (i % 2) * 512:(i % 2) * 512 + 512]

        bank_ctr = [0]

        def nbank():
            b = bank(bank_ctr[0])
            bank_ctr[0] += 1
            return b

        H = H_t[:].rearrange("p (c t) -> p c t", c=8)
        Hc = Hc_t[:].rearrange("p (c t) -> p c t", c=8)
        hn = hn_t[:].rearrange("p (c t) -> p c t", c=8)
        hnc = hnc_t[:].rearrange("p (c t) -> p c t", c=8)
        mixc = mixc_t[:].rearrange("p (c t) -> p c t", c=8)

        def av(off_kib, n):
            o = int(round(off_kib * 512))
            return AR[:, o:o + n]

        mix = av(0, 8 * T).rearrange("p (c t) -> p c t", c=8)
        WA = av(32, 4096)
        WB = av(40, 4096)
        sqb = [av(48, 512), av(49, 512)]
        rstd_b = av(50, 1024).bitcast(F32)
        t32 = [av(52, 1024).bitcast(F32), av(54, 1024).bitcast(F32)]
        SCR = 56.0

        class Stream:
            pass

        LAT = Stream()
        LAT.T, LAT.H, LAT.hn, LAT.mix, LAT.col, LAT.name = T, H, hn, mix, 0, "lat"
        CTX = Stream()
        CTX.T, CTX.H, CTX.hn, CTX.mix, CTX.col, CTX.name = CT, Hc, hnc, mixc, 1, "ctx"

        def blocks(S, n=512):
            return [(o, min(n, S.T - o)) for o in range(0, S.T, n)]

        def coef(l, k, c, col):
            o = l * 96 + k * 16 + c * 2 + col
            return coef_t[:, o:o + 1]

        B.dma("sp", ident[:], ident_d, writes=[ident[:]])
        B.dma("sp", cvec[:], cvec_d, writes=[cvec[:]])
        B.dma("sp", adab[:], adab_d, writes=[adab[:]])
        B.dma("sp", ngs[:], ng_d, writes=[ngs[:]])
        B.dma("sp", fng[:], fng_d, writes=[fng[:]])
        B.dma("sp", cps[:], cp_d, writes=[cps[:]])
        B.dma("sp", dft256[:], dft256_d, writes=[dft256[:]])
        B.dma("pool", csblk[:].rearrange("p (c n) -> p c n", c=2), csblk_d.rearrange("(c p) n -> p c n", p=128),
              writes=[csblk[:]])
        B.memset("pool", ones_rms[:], 1.0 / D)
        B.memset("pool", ones_ln[:], 1.0 / 256)
        for c in range(8):
            B.dma("sp", H[:, c, :], xT.rearrange("(c p) t -> p c t", p=128)[:, c, :], writes=[H[:, c, :]])
        B.dma("sp", Hc, cxT.rearrange("(c p) t -> p c t", p=128), writes=[Hc_t[:]])
        B.act(csil[:], cvec[:], AF.Silu)

        def ada_slice(l, sidx, slot, ps):
            csv = csil[:].rearrange("p (c k) -> p c k", c=8)
            sv = slot.rearrange("p (c n) -> p c n", c=8)
            B.dma("pool", sv, ada_w[l].rearrange("(c p) n -> p c n", p=128)[:, :, sidx * 512:(sidx + 1) * 512], writes=[slot])
            for jj in range(4):
                for dc in range(8):
                    B.mm(ps[:, 2 * jj:2 * jj + 2], sv[:, dc, jj * 128:(jj + 1) * 128], csv[:, dc, :], dc == 0, dc == 7)
            o = l * 96 + sidx * 8
            B.tt("dve", mod_t[:, o:o + 8], ps[:, 0:8], adab[:, o:o + 8], ALU.add)

        def ada_coefs(l, ks):
            m3 = mod_t[:, l * 96:(l + 1) * 96].rearrange("p (k x) -> p k x", k=6)
            c3 = coef_t[:, l * 96:(l + 1) * 96].rearrange("p (k x) -> p k x", k=6)
            n3 = ngs[:, l * 32:(l + 1) * 32].rearrange("p (k x) -> p k x", k=2)
            for k in ks:
                if k == 0:
                    B.stt(c3[:, 0, :], m3[:, 1, :], 1.0, n3[:, 0, :], ALU.add, ALU.mult)
                elif k == 1:
                    B.copy("dve", c3[:, 1, :], m3[:, 0, :])
                elif k == 2:
                    B.copy("dve", c3[:, 2, :], m3[:, 2, :])
                elif k == 3:
                    B.stt(c3[:, 3, :], m3[:, 4, :], 1.0, n3[:, 1, :], ALU.add, ALU.mult)
                elif k == 4:
                    B.copy("dve", c3[:, 4, :], m3[:, 3, :])
                elif k == 5:
                    B.copy("dve", c3[:, 5, :], m3[:, 5, :])

        def norm(S, l, kA, kB, final=False):
            for (o, n) in blocks(S):
                ps = nbank()
                for c in range(8):
                    sq = sqb[c % 2]
                    B.act(sq[:, :n], S.H[:, c, o:o + n], AF.Square)
                    B.mm(ps[:, :n], ones_rms[:], sq[:, :n], c == 0, c == 7)
                B.act(rstd_b[:, :n], ps[:, :n], AF.Sqrt, bias=RMS_EPS_AP[0], scale=1.0)
                B.recip(rstd_b[:, :n], rstd_b[:, :n])
                for c in range(8):
                    t = t32[c % 2]
                    B.tt("dve", t[:, :n], S.H[:, c, o:o + n], rstd_b[:, :n], ALU.mult)
                    if final:
                        B.act(S.H[:, c, o:o + n], t[:, :n], AF.Identity, scale=fng[:, c:c + 1])
                    else:
                        B.act(S.hn[:, c, o:o + n], t[:, :n], AF.Identity, bias=coef(l, kB, c, S.col), scale=coef(l, kA, c, S.col))

        eps_t = sb("epsS", 2, F32)
        B.memset("pool", eps_t[:, 0:1], RMS_EPS)
        B.memset("pool", eps_t[:, 1:2], LN_EPS)
        RMS_EPS_AP = [eps_t[:, 0:1]]
        LN_EPS_AP = eps_t[:, 1:2]

        evac_rr = [0]

        def evac(out, in_):
            eng = ("act", "dve")[evac_rr[0] % 2]
            evac_rr[0] += 1
            B.copy(eng, out, in_)

        def proj_cm(S, wv, col0, evac_fn, nblk=512):
            for (o, n) in blocks(S, nblk):
                ps = nbank()
                for dc in range(8):
                    B.mm(ps[:, :n], wv[:, dc, col0:col0 + 128], S.hn[:, dc, o:o + n], dc == 0, dc == 7)
                evac_fn(ps[:, :n], o, n)

        def attention(l, last, bg=()):
            QTA = av(SCR + 0, T)
            QTB = av(SCR + 4, T)
            KT = av(SCR + 8, T)
            Vp = av(SCR + 12, 16 * 256).rearrange("p (t x) -> p t x", t=16)
            KcT = av(SCR + 20, CT)
            Vc = av(SCR + 20.5, 2 * 256).rearrange("p (t x) -> p t x", t=2)
            QcA = av(48, CT)
            QcB = av(48.5, CT)
            rec = [av(52, 256).bitcast(F32), av(52.5, 256).bitcast(F32)]
            Eb = [av(0, NBLK * 128), av(5.25, NBLK * 128)]
            PT = [av(10.5 + 1.75 * i, 896) for i in range(3)]
            B.memset("pool", Vp[:, :, 64:128], 1.0)
            B.memset("pool", Vc[:, :, 64:128], 1.0)
            B.memset("pool", QTA[64:128, :], 0.0)
            B.memset("pool", QTB[0:64, :], 0.0)
            B.memset("pool", QcA[64:128, :], 0.0)
            B.memset("pool", QcB[0:64, :], 0.0)
            Sps = [pp[0], pp[1], pp[2]]
            Obank = pp[3][:, 0:512]
            Mbanks = [pp[0][:, 0:512], pp[0][:, 512:1024], pp[1][:, 0:512], pp[1][:, 512:1024],
                      pp[2][:, 0:512], pp[2][:, 512:1024], pp[3][:, 512:1024]]
            mctr = [0]

            def nM():
                b = Mbanks[mctr[0] % len(Mbanks)]
                mctr[0] += 1
                return b
            uctr = [0]
            bgq = list(bg)

            for p in range(DBG["pairs"]):
                wv = WB.rearrange("p (c n) -> p c n", c=8)[:, :, 0:384]
                c0 = 768 + p * 384
                B.dma("pool", wv, w_in[l].rearrange("(c p) n -> p c n", p=128)[:, :, c0:c0 + 384], writes=[WB])
                for hh in range(2):
                    B.dma("pool", Eb[hh], bias_d[l, 2 * p + hh], writes=[Eb[hh]])
                    B.act(Eb[hh], Eb[hh], AF.Exp)

                def ev_kc(ps, o, n):
                    evac(KcT[:, o:o + n], ps)
                proj_cm_fixed(CTX, wv, 128, ev_kc, nM)
                if not last:
                    def ev_qc(ps, o, n):
                        B.copy("act", QcA[0:64, o:o + n], ps[0:64, :])
                        B.copy("dve", QcB[64:128, o:o + n], ps[64:128, :])
                    proj_cm_fixed(CTX, wv, 0, ev_qc, nM)
                for t in range(2):
                    mb = nM()
                    for dc in range(8):
                        B.mm(mb[:, 0:128], hnc[:, dc, t * 128:(t + 1) * 128], wv[:, dc, 256:384], dc == 0, dc == 7)
                    B.copy("dve", Vc[:, t, :].rearrange("p (a b) -> p a b", a=2)[:, :, 0:64],
                           mb[:, 0:128].rearrange("p (a b) -> p a b", a=2))

                def ev_q(ps, o, n):
                    B.copy("act", QTA[0:64, o:o + n], ps[0:64, :])
                    B.copy("dve", QTB[64:128, o:o + n], ps[64:128, :])

                def ev_k(ps, o, n):
                    evac(KT[:, o:o + n], ps)
                proj_cm_fixed(LAT, wv, 0, ev_q, nM)
                proj_cm_fixed(LAT, wv, 128, ev_k, nM)
                for t4 in range(4):
                    mb = nM()
                    for tt_ in range(4):
                        t = t4 * 4 + tt_
                        for dc in range(8):
                            B.mm(mb[:, tt_ * 128:(tt_ + 1) * 128], hn[:, dc, t * 128:(t + 1) * 128], wv[:, dc, 256:384], dc == 0, dc == 7)
                    for tt_ in range(4):
                        t = t4 * 4 + tt_
                        B.copy(("dve", "act")[tt_ % 2], Vp[:, t, :].rearrange("p (a b) -> p a b", a=2)[:, :, 0:64],
                               mb[:, tt_ * 128:(tt_ + 1) * 128].rearrange("p (a b) -> p a b", a=2))

                units = []
                if not last:
                    for hh in range(2):
                        for m in range(2):
                            units.append(("ctx", hh, m))
                for hh in range(2):
                    for m in range(16):
                        units.append(("lat", hh, m))
                if DBG["units"] is not None:
                    units = units[:DBG["units"]]

                def unit_S(u):
                    kind, hh, m = units[u]
                    k = uctr[0] + u
                    Sp = Sps[k % 3]
                    if kind == "lat":
                        chunks, _ = tile_chunks(m)
                        q = (QTA, QTB)[hh][:, m * 128:(m + 1) * 128]
                        for i, j in enumerate(chunks):
                            B.mm(Sp[:, i * 128:(i + 1) * 128], KT[:, j * 128:(j + 1) * 128], q, True, True)
                        nl = len(chunks)
                    else:
                        q = (QcA, QcB)[hh][:, m * 128:(m + 1) * 128]
                        nl = 0
                    for cc in range(2):
                        B.mm(Sp[:, (nl + cc) * 128:(nl + cc + 1) * 128], KcT[:, cc * 128:(cc + 1) * 128], q, True, True)

                def unit_rest(u):
                    kind, hh, m = units[u]
                    k = uctr[0] + u
                    Sp = Sps[k % 3]
                    P = PT[k % 3]
                    if kind == "lat":
                        chunks, typ = tile_chunks(m)
                    else:
                        chunks, typ = [], None
                    nl = len(chunks)
                    ns = nl + 2
                    B.act(P[:, 0:ns * 128], Sp[:, 0:ns * 128], AF.Exp, scale=0.125)
                    if nl:
                        eo = EOFF[typ] * 128
                        B.tt("dve", P[:, 0:nl * 128], P[:, 0:nl * 128], Eb[hh][:, eo:eo + nl * 128], ALU.mult)
                    if DBG["parts"] < 3:
                        return
                    Op = Obank[:, (k % 4) * 128:(k % 4 + 1) * 128]
                    vo = 64 * hh
                    for i, j in enumerate(chunks):
                        B.mm(Op, Vp[:, j, vo:vo + 128], P[:, i * 128:(i + 1) * 128], i == 0, False)
                    for cc in range(2):
                        B.mm(Op, Vc[:, cc, vo:vo + 128], P[:, (nl + cc) * 128:(nl + cc + 1) * 128], (nl == 0 and cc == 0), cc == 1)
                    if DBG["parts"] < 4:
                        B.copy("act", rec[0][:, :], Op)
                        return
                    r = rec[k % 2]
                    olo, ohi = (0, 64) if hh == 0 else (64, 128)
                    dlo, dhi = (64, 128) if hh == 0 else (0, 64)
                    B.recip(r[dlo:dhi, :], Op[dlo:dhi, :])
                    dst = (mix if kind == "lat" else mixc)[olo:ohi, 4 + p, m * 128:(m + 1) * 128]
                    B.tt("dve", dst, Op[olo:ohi, :], r[dlo:dhi, :], ALU.mult)

                LOOK = 2
                for u in range(min(LOOK, len(units))):
                    unit_S(u)
                for u in range(len(units)):
                    if u + LOOK < len(units):
                        unit_S(u + LOOK)
                    unit_rest(u)
                    if bgq and (uctr[0] + u) % 6 == 5:
                        bgq.pop(0)(WA, pp[3][:, 512:1024])
                uctr[0] += len(units)
            while bgq:
                bgq.pop(0)(WA, pp[3][:, 512:1024])

        def proj_cm_fixed(S, wv, col0, evac_fn, nb):
            for (o, n) in blocks(S):
                psb = nb()
                for dc in range(8):
                    B.mm(psb[:, :n], wv[:, dc, col0:col0 + 128], S.hn[:, dc, o:o + n], dc == 0, dc == 7)
                evac_fn(psb[:, :n], o, n)

        def conv_phase(l, streams):
            wv = WA.rearrange("p (c n) -> p c n", c=8)
            B.dma("pool", wv, w_in[l].rearrange("(c p) n -> p c n", p=128)[:, :, 256:768], writes=[WA])
            B.dma("pool", pw_t[:].rearrange("p (c n) -> p c n", c=2), w_pw[l].rearrange("(c p) n -> p c n", p=128), writes=[pw_t[:]])
            pwv = pw_t[:].rearrange("p (c n) -> p c n", c=2)
            vpad = av(SCR, 2 * (T + 30)).rearrange("p (c t) -> p c t", c=2)
            vpadc = av(0, 2 * (CT + 30)).rearrange("p (c t) -> p c t", c=2)
            doff = SCR + (2 * (T + 30) * 2) / 1024.0
            doff = math.ceil(doff * 16) / 16.0
            Dg = av(doff, 62 * 128).rearrange("p (c k n) -> p c k n", c=2, k=31)
            cpl = cps[:, l * 70:(l + 1) * 70].rearrange("p (c k) -> p c k", c=2)
            sig = av(2, 512)
            ybf = [av(3, 512), av(4, 512)]
            ysq = [av(5, 512), av(6, 512)]
            y32 = [t32[0], t32[1]]
            z32 = rstd_b
            sil = [sqb[0], sqb[1]]
            for ch in range(2):
                for k in range(31):
                    B.ts("dve", Dg[:, ch, k, :], ident[:], cpl[:, ch, k:k + 1], ALU.mult)
            for S in streams:
                vp = vpad if S is LAT else vpadc
                B.memset("pool", vp[:, :, 0:15], 0.0)
                B.memset("pool", vp[:, :, 15 + S.T:30 + S.T], 0.0)
                for (o, n) in blocks(S):
                    for ch in range(2):
                        pa = nbank()
                        pg = nbank()
                        for dc in range(8):
                            B.mm(pa[:, :n], wv[:, dc, ch * 128:(ch + 1) * 128], S.hn[:, dc, o:o + n], dc == 0, dc == 7)
                        for dc in range(8):
                            B.mm(pg[:, :n], wv[:, dc, 256 + ch * 128:256 + (ch + 1) * 128], S.hn[:, dc, o:o + n], dc == 0, dc == 7)
                        B.act(sig[:, :n], pg[:, :n], AF.Sigmoid)
                        B.tt("dve", vp[:, ch, 15 + o:15 + o + n], pa[:, :n], sig[:, :n], ALU.mult)
                for (o, n) in blocks(S):
                    pm = nbank()
                    pq = nbank()
                    for ch in range(2):
                        pc = nbank()
                        for k in range(31):
                            B.mm(pc[:, :n], Dg[:, ch, k, :], vp[:, ch, o + k:o + k + n], k == 0, k == 30)
                        B.act(ybf[ch][:, :n], pc[:, :n], AF.Identity, bias=cpl[:, ch, 31:32], scale=1.0)
                        B.act(ysq[ch][:, :n], pc[:, :n], AF.Square, bias=cpl[:, ch, 31:32], scale=1.0)
                        B.ts("dve", y32[ch][:, :n], pc[:, :n], cpl[:, ch, 31:32], ALU.add)
                    for ch in range(2):
                        B.mm(pm[:, :n], ones_ln[:], ybf[ch][:, :n], ch == 0, ch == 1)
                    for ch in range(2):
                        B.mm(pq[:, :n], ones_ln[:], ysq[ch][:, :n], ch == 0, ch == 1)
                    B.act(z32[:, :n], pm[:, :n], AF.Square)
                    B.tt("dve", z32[:, :n], pq[:, :n], z32[:, :n], ALU.subtract)
                    B.act(z32[:, :n], z32[:, :n], AF.Sqrt, bias=LN_EPS_AP, scale=1.0)
                    B.recip(z32[:, :n], z32[:, :n])
                    for ch in range(2):
                        B.tt("dve", y32[ch][:, :n], y32[ch][:, :n], pm[:, :n], ALU.subtract)
                        B.tt("dve", y32[ch][:, :n], y32[ch][:, :n], z32[:, :n], ALU.mult)
                        B.act(sil[ch][:, :n], y32[ch][:, :n], AF.Silu, bias=cpl[:, ch, 33:34], scale=cpl[:, ch, 32:33])
                    for oc in range(2):
                        po = nbank()
                        for ch in range(2):
                            B.mm(po[:, :n], pwv[:, ch, oc * 128:(oc + 1) * 128], sil[ch][:, :n], ch == 0, ch == 1)
                        B.act(S.mix[:, 2 + oc, o:o + n], po[:, :n], AF.Identity, bias=cpl[:, oc, 34:35], scale=1.0)

        def fourier_phase(l, streams):
            ftv = WB[:, 0:2048].rearrange("p (c n) -> p c n", c=2)
            B.dma("pool", ftv, w_inFT[l].rearrange("(c p) n -> p c n", p=128), writes=[WB[:, 0:2048]])
            B.dma("pool", wf_t[:].rearrange("p (c n) -> p c n", c=2), w_f[l].rearrange("(c p) n -> p c n", p=128), writes=[wf_t[:]])
            wfv = wf_t[:].rearrange("p (c n) -> p c n", c=2)
            csv = csblk[:].rearrange("p (c n) -> p c n", c=2)
            Wp = WA.rearrange("p (c n) -> p c n", c=8)
            for dc in range(8):
                ps = nbank()
                for cc in range(2):
                    B.mm(ps, ftv[:, cc, dc * 128:(dc + 1) * 128], csv[:, cc, :], cc == 0, cc == 1)
                evac(Wp[:, dc, :], ps)
            UU = av(SCR, 16 * 512).rearrange("p (t n) -> p t n", t=16)
            UUc = av(SCR + 16, 2 * 512).rearrange("p (t n) -> p t n", t=2)
            FT = av(SCR + 18, 2 * 256).rearrange("p (c n) -> p c n", c=2)
            for S in streams:
                U = UU if S is LAT else UUc
                nt = S.T // 128
                for t in range(nt):
                    ps = nbank()
                    for dc in range(8):
                        B.mm(ps, S.hn[:, dc, t * 128:(t + 1) * 128], Wp[:, dc, :], dc == 0, dc == 7)
                    evac(U[:, t, :], ps)
            for S in streams:
                U = UU if S is LAT else UUc
                nt = S.T // 128
                njb = S.T // 256
                for jb in range(njb):
                    if S is LAT:
                        dbuf = hn_t[:, (jb % 2) * 8192:(jb % 2 + 1) * 8192]
                        B.dma("sp", dbuf, dft_d[jb], writes=[dbuf])
                        dv = dbuf.rearrange("p (k s j) -> p k s j", k=16, s=2)
                    else:
                        dv = dft256[:].rearrange("p (k s j) -> p k s j", k=2, s=2)
                    for cg in range(2):
                        ps = nbank()
                        first = True
                        for kc in range(nt):
                            for s in range(2):
                                B.mm(ps[:, 0:256], U[:, kc, s * 256 + cg * 128:s * 256 + (cg + 1) * 128], dv[:, kc, s, :],
                                     first, (kc == nt - 1 and s == 1))
                                first = False
                        evac(FT[:, cg, :], ps[:, 0:256])
                    for oc in range(2):
                        ps = nbank()
                        for cg in range(2):
                            B.mm(ps[:, 0:256], wfv[:, cg, oc * 128:(oc + 1) * 128], FT[:, cg, :], cg == 0, cg == 1)
                        evac(S.mix[:, oc, jb * 256:(jb + 1) * 256], ps[:, 0:256])

        def outproj(l, streams):
            wo = AR[:, 32 * 512:48 * 512].rearrange("p (c n) -> p c n", c=8)
            B.dma("pool", wo[:, 0:4, :], w_out[l].rearrange("(c p) n -> p c n", p=128)[:, 0:4, :], writes=[WA])
            B.dma("pool", wo[:, 4:8, :], w_out[l].rearrange("(c p) n -> p c n", p=128)[:, 4:8, :], writes=[WB])
            for S in streams:
                for (o, n) in blocks(S):
                    for oc in range(8):
                        ps = nbank()
                        for kc in range(8):
                            B.mm(ps[:, :n], wo[:, kc, oc * 128:(oc + 1) * 128], S.mix[:, kc, o:o + n], kc == 0, kc == 7)
                        B.stt(S.H[:, oc, o:o + n], ps[:, :n], coef(l, 2, oc, S.col), S.H[:, oc, o:o + n], ALU.mult, ALU.add)

        def mlp(l, streams, after_first_norm):
            slot_off = [8, 16, 24, 32, 40, 56, 64, 72]
            sets = [[5, 6, 7, 0], [1, 2, 3, 4]]
            hid = [av(0, 2048).rearrange("p (f t) -> p f t", f=8), av(4, 2048).rearrange("p (f t) -> p f t", f=8)]
            rl = [sqb[0][:, 0:256], sqb[1][:, 0:256]]
            hctr = 0
            for fb in range(4):
                st = sets[fb % 2]
                w1s = [av(slot_off[st[0]], 4096).rearrange("p (c n) -> p c n", c=8),
                       av(slot_off[st[1]], 4096).rearrange("p (c n) -> p c n", c=8)]
                w2s = [av(slot_off[st[2]], 4096).rearrange("p (c n) -> p c n", c=4),
                       av(slot_off[st[3]], 4096).rearrange("p (c n) -> p c n", c=4)]
                for hhalf in range(2):
                    c0 = fb * 1024 + hhalf * 512
                    B.dma("pool", w1s[hhalf], w1[l].rearrange("(c p) n -> p c n", p=128)[:, :, c0:c0 + 512],
                          writes=[av(slot_off[st[hhalf]], 4096)])
                for hhalf in range(2):
                    r0 = fb * 8 + hhalf * 4
                    B.dma("pool", w2s[hhalf], w2[l].rearrange("(c p) n -> p c n", p=128)[:, r0:r0 + 4, :],
                          writes=[av(slot_off[st[2 + hhalf]], 4096)])
                if fb == 0 and after_first_norm is not None:
                    after_first_norm()
                for S in streams:
                    for (o, n) in blocks(S, 256):
                        hb = hid[hctr % 2]
                        hctr += 1
                        for fc in range(8):
                            ps = nbank()
                            wv = w1s[fc // 4]
                            cc = (fc % 4) * 128
                            for dc in range(8):
                                B.mm(ps[:, :n], wv[:, dc, cc:cc + 128], S.hn[:, dc, o:o + n], dc == 0, dc == 7)
                            r = rl[fc % 2]
                            B.act(r[:, :n], ps[:, :n], AF.Relu)
                            B.tt("dve", hb[:, fc, :n], r[:, :n], r[:, :n], ALU.mult)
                        for oc in range(8):
                            ps = nbank()
                            for fc in range(8):
                                B.mm(ps[:, :n], w2s[fc // 4][:, fc % 4, oc * 128:(oc + 1) * 128], hb[:, fc, :n], fc == 0, fc == 7)
                            B.stt(S.H[:, oc, o:o + n], ps[:, :n], coef(l, 5, oc, S.col), S.H[:, oc, o:o + n], ALU.mult, ALU.add)

        def dump(name, ap_sb, shape):
            d = nc.dram_tensor("dbg_" + name, list(shape), ap_sb.dtype, kind="ExternalOutput").ap()
            B.dma("sp", d, ap_sb, reads=[ap_sb])
            dbg_out[name] = True

        done = False
        for l in range(DEPTH):
            last = l == DEPTH - 1
            streams = [LAT] if last else [CTX, LAT]
            bg = []
            if l == 0:
                for sidx in range(4):
                    ada_slice(0, sidx, (WA, WB)[sidx % 2], nbank())
                ada_coefs(0, [0, 1])
                for sidx in range(4, 12):
                    bg.append(lambda slot, ps, sidx=sidx: ada_slice(0, sidx, slot, ps))
                bg.append(lambda slot, ps: ada_coefs(0, [2, 3, 4, 5]))
                for sidx in range(12):
                    bg.append(lambda slot, ps, sidx=sidx: ada_slice(1, sidx, slot, ps))
                bg.append(lambda slot, ps: ada_coefs(1, [0, 1, 2, 3, 4, 5]))
            if stop_after == f"ada{l}":
                dump("mod", mod_t[:], [128, DEPTH * 96])
                dump("coef", coef_t[:], [128, DEPTH * 96])
                done = True
                break
            norm(CTX, l, 0, 1)
            norm(LAT, l, 0, 1)
            if stop_after == f"norm{l}":
                dump("hn", hn_t[:], [128, 8 * T])
                dump("hnc", hnc_t[:], [128, 8 * CT])
                done = True
                break
            attention(l, last, bg)
            if stop_after == f"attn{l}":
                dump("mix", AR[:, 0:8 * T], [128, 8 * T])
                dump("mixc", mixc_t[:], [128, 8 * CT])
                done = True
                break
            conv_phase(l, streams)
            if stop_after == f"conv{l}":
                dump("mix", AR[:, 0:8 * T], [128, 8 * T])
                dump("mixc", mixc_t[:], [128, 8 * CT])
                done = True
                break
            fourier_phase(l, streams)
            if stop_after == f"four{l}":
                dump("mix", AR[:, 0:8 * T], [128, 8 * T])
                dump("mixc", mixc_t[:], [128, 8 * CT])
                done = True
                break
            outproj(l, streams)
            if stop_after == f"oproj{l}":
                dump("H", H_t[:], [128, 8 * T])
                dump("Hc", Hc_t[:], [128, 8 * CT])
                done = True
                break
            for S in streams:
                norm(S, l, 3, 4)
            mlp(l, streams, None)
            if stop_after == f"mlp{l}":
                dump("H", H_t[:], [128, 8 * T])
                dump("Hc", Hc_t[:], [128, 8 * CT])
                done = True
                break
        if not done:
            norm(LAT, 0, 0, 0, final=True)
        for c in range(8):
            B.dma("sp", outT.rearrange("(c p) t -> p c t", p=128)[:, c, :], H[:, c, :], reads=[H[:, c, :]])
        B.emit(es)
    _CACHE['trace'] = B.trace
    return nc, sorted(dbg_out.keys()), len(B.ops)


def _bias_tables(rpb):
    kc = np.arange(64)
    qc = np.arange(64)
    cs = np.clip(qc - 8, 0, 48)
    colvalid = (kc[:, None] >= cs[None, :]) & (kc[:, None] < cs[None, :] + 16)
    coloff = np.clip(kc[:, None] - qc[None, :] + 15, 0, 30)
    blocks = []
    specs = [(5, j) for j in range(3, 8)] + [(0, j) for j in range(4)] + [(1, j) for j in range(4)] + \
            [(14, j) for j in range(12, 16)] + [(15, j) for j in range(12, 16)]
    out = np.full((DEPTH, 8, 128, NBLK, 128), NEG, np.float32)
    for bi, (m, j) in enumerate(specs):
        for a in range(2):
            for b in range(2):
                krow = 2 * j + a
                qrow = 2 * m + b
                rs = min(max(qrow - 4, 0), 24)
                if not (rs <= krow < rs + 8):
                    continue
                dr = krow - qrow + 7
                vals = rpb[:, :, dr, :][:, :, coloff]
                vals = np.where(colvalid[None, None], vals, np.float32(NEG))
                out[:, :, a * 64:(a + 1) * 64, bi, b * 64:(b + 1) * 64] = vals
    return np.ascontiguousarray(out.reshape(DEPTH, 8, 128, NBLK * 128))


def _dft_tables():
    k = np.arange(T, dtype=np.int64)
    ang = 2.0 * np.pi * ((k[:, None] * k[None, :]) % T).astype(np.float64) / T
    Cm = (np.cos(ang) / math.sqrt(T)).astype(np.float32)
    Sm = (-np.sin(ang) / math.sqrt(T)).astype(np.float32)
    CS = np.stack([Cm, Sm], axis=0)
    CS = CS.reshape(2, 16, 128, 8, 256)
    dft = np.ascontiguousarray(CS.transpose(3, 2, 1, 0, 4)).reshape(8, 128, 16 * 2 * 256).astype(NPBF)
    k2 = np.arange(CT, dtype=np.int64)
    ang2 = 2.0 * np.pi * ((k2[:, None] * k2[None, :]) % CT).astype(np.float64) / CT
    C2 = (np.cos(ang2) / math.sqrt(CT)).astype(np.float32)
    S2 = (-np.sin(ang2) / math.sqrt(CT)).astype(np.float32)
    CS2 = np.stack([C2, S2], axis=0).reshape(2, 2, 128, 256)
    dft256 = np.ascontiguousarray(CS2.transpose(2, 1, 0, 3)).reshape(128, 2 * 2 * 256).astype(NPBF)
    a = np.arange(64)
    ang3 = 2.0 * np.pi * ((a[:, None] * a[None, :]) % 64) / 64.0
    cb = np.zeros((256, 256), np.float32)
    sbk = np.zeros((256, 256), np.float32)
    for g in range(4):
        cb[g * 64:(g + 1) * 64, g * 64:(g + 1) * 64] = np.cos(ang3) / 8.0
        sbk[g * 64:(g + 1) * 64, g * 64:(g + 1) * 64] = np.sin(ang3) / 8.0
    csblk = np.ascontiguousarray(np.concatenate([cb, sbk], axis=1)).astype(np.float32)
    return dft, dft256, csblk


def host_prep(inp):
    f = lambda a: np.ascontiguousarray(np.asarray(a, dtype=np.float32))
    x, c, ctx, c_ctx = f(inp["x"]), f(inp["c"]), f(inp["ctx"]), f(inp["c_ctx"])
    shared = {}
    shared["ada_w"] = f(inp["ada_w"])
    ab = f(inp["ada_b"]).reshape(DEPTH, 48, 128).transpose(2, 0, 1)
    shared["adab"] = np.ascontiguousarray(np.repeat(ab[:, :, :, None], 2, axis=3)).reshape(128, DEPTH * 96)
    n1 = f(inp["norm1_g"]).reshape(DEPTH, 8, 128).transpose(2, 0, 1)
    n2 = f(inp["norm2_g"]).reshape(DEPTH, 8, 128).transpose(2, 0, 1)
    ng = np.stack([n1, n2], axis=2)
    shared["ng"] = np.ascontiguousarray(np.repeat(ng[..., None], 2, axis=4)).reshape(128, DEPTH * 32)
    shared["fng"] = np.ascontiguousarray(f(inp["final_norm_g"]).reshape(8, 128).T)
    dw = f(inp["conv_dw_w"]).reshape(DEPTH, 31, 2, 128).transpose(3, 0, 2, 1)
    def pv(name):
        return f(inp[name]).reshape(DEPTH, 2, 128).transpose(2, 0, 1)[..., None]
    cp = np.concatenate([dw, pv("conv_dw_b"), pv("conv_norm_g"), pv("conv_norm_b"), pv("conv_pw_b")], axis=3)
    shared["cp"] = np.ascontiguousarray(cp).reshape(128, DEPTH * 70)
    w_in = f(inp["w_in"])
    order = list(range(0, 768))
    for p in range(4):
        for part in range(3):
            base = 768 + part * 512 + p * 128
            order += list(range(base, base + 128))
    shared["w_in"] = np.ascontiguousarray(w_in[:, :, order])
    shared["w_inFT"] = np.ascontiguousarray(w_in[:, :, :256].transpose(0, 2, 1))
    shared["w_f"] = f(inp["w_fourier"])
    shared["w_pw"] = f(inp["conv_pw_w"])
    shared["w_out"] = f(inp["w_out"])
    shared["w1"] = f(inp["mlp_w1"])
    shared["w2"] = f(inp["mlp_w2"])
    dft, dft256, csblk = _dft_tables()
    shared["dft"] = dft
    shared["dft256"] = dft256
    shared["csblk"] = csblk
    shared["ident"] = np.eye(128, dtype=np.float32).astype(NPBF)
    shared["biasT"] = _bias_tables(f(inp["na_rpb"]))
    maps = []
    cc = c_ctx.reshape(8, 128).T
    for b in range(NCORES):
        m = dict(shared)
        m["xT"] = np.ascontiguousarray(x[b].T)
        m["cxT"] = np.ascontiguousarray(ctx[b].T)
        cb = c[b].reshape(8, 128).T
        m["cvec"] = np.ascontiguousarray(np.stack([cb, cc], axis=2)).reshape(128, 16)
        maps.append(m)
    return maps


def kernel(**inputs):
    maps = host_prep(inputs)
    if "nc" not in _CACHE:
        _CACHE["nc"] = build_program()[0]
    nc = _CACHE["nc"]
    res = run_bass_kernel_spmd(nc, maps, core_ids=list(range(NCORES)))
    out = np.stack([np.ascontiguousarray(res.results[b]["outT"].T) for b in range(NCORES)], axis=0)
    return out.astype(np.float32)
```

```python
import math
from contextlib import ExitStack

import numpy as np
import ml_dtypes
import concourse.bass as bass
import concourse.mybir as mybir
from concourse.bass_utils import run_bass_kernel_spmd

F32 = mybir.dt.float32
BF16 = mybir.dt.bfloat16
AF = mybir.ActivationFunctionType
ALU = mybir.AluOpType
NPBF = ml_dtypes.bfloat16

D = 1024
T = 2048
CT = 256
DEPTH = 2
DIN = 2304
DFF = 4096
NCORES = 8
NEG = -30000.0
NBLK = 21
EOFF = {"int": 0, 0: 5, 1: 9, 14: 13, 15: 17}
RMS_EPS = 1e-6
LN_EPS = 1e-5
CELL = 512
_CACHE = {}
DBG = {"pairs": 4, "units": None, "parts": 5, "pv": 0}


class Builder:
    def __init__(self, nc):
        self.nc = nc
        self.ops = []
        self.cw = {}
        self.cr = {}

    @staticmethod
    def _cells(ap):
        sz = 4 if ap.dtype == F32 else 2
        pairs = ap.ap
        pstride, pcount = pairs[0]
        off = ap.offset
        p0 = off // pstride
        f0 = off % pstride
        ext = 0
        for step, cnt in pairs[1:]:
            ext += (cnt - 1) * step
        lo = f0 * sz
        hi = (f0 + ext + 1) * sz
        name = ap.name
        cell = 2048 if name.startswith("pp") else CELL
        out = []
        for q in range(p0 // 32, (p0 + pcount - 1) // 32 + 1):
            for ci in range(lo // cell, (hi - 1) // cell + 1):
                out.append((name, q, ci))
        return out

    def add(self, eng, fn, reads=(), writes=(), dma=False):
        oid = len(self.ops)
        deps = set()
        rc = [c for ap in reads for c in self._cells(ap)]
        wc = [c for ap in writes for c in self._cells(ap)]
        wc += [c for c in rc if c[0].startswith("pp")]
        for c in rc:
            w = self.cw.get(c)
            if w is not None:
                deps.add(w)
        for c in wc:
            w = self.cw.get(c)
            if w is not None:
                deps.add(w)
            r = self.cr.get(c)
            if r:
                deps.update(r.values())
        key = ("d", oid) if dma else eng
        for c in wc:
            self.cw[c] = oid
            self.cr[c] = {}
        for c in rc:
            self.cr.setdefault(c, {})[key] = oid
        self.ops.append(dict(eng=eng, fn=fn, deps=deps, dma=dma, sig=False, cnt=0))
        return oid

    def mm(self, out, lhsT, rhs, start, stop):
        self.add("pe", lambda e: e.matmul(out, lhsT, rhs, start=start, stop=stop), reads=[lhsT, rhs], writes=[out])

    def act(self, out, in_, func, bias=None, scale=None):
        reads = [in_]
        kw = {}
        if bias is not None:
            kw["bias"] = bias
            if not isinstance(bias, float):
                reads.append(bias)
        if scale is not None:
            kw["scale"] = scale
            if not isinstance(scale, float):
                reads.append(scale)
        self.add("act", lambda e: e.activation(out=out, in_=in_, func=func, **kw), reads=reads, writes=[out])

    def tt(self, eng, out, in0, in1, op):
        self.add(eng, lambda e: e.tensor_tensor(out=out, in0=in0, in1=in1, op=op), reads=[in0, in1], writes=[out])

    def ts(self, eng, out, in0, s1, op0, s2=None, op1=None):
        reads = [in0]
        if not isinstance(s1, float):
            reads.append(s1)
        if s2 is not None and not isinstance(s2, float):
            reads.append(s2)
        if op1 is None:
            self.add(eng, lambda e: e.tensor_scalar(out=out, in0=in0, scalar1=s1, scalar2=None, op0=op0), reads=reads, writes=[out])
        else:
            self.add(eng, lambda e: e.tensor_scalar(out=out, in0=in0, scalar1=s1, scalar2=s2, op0=op0, op1=op1), reads=reads, writes=[out])

    def stt(self, out, in0, scalar, in1, op0, op1):
        reads = [in0, in1]
        if not isinstance(scalar, float):
            reads.append(scalar)
        self.add("dve", lambda e: e.scalar_tensor_tensor(out=out, in0=in0, scalar=scalar, in1=in1, op0=op0, op1=op1),
                 reads=reads, writes=[out])

    def copy(self, eng, out, in_):
        if eng == "act":
            self.add("act", lambda e: e.copy(out=out, in_=in_), reads=[in_], writes=[out])
        else:
            self.add(eng, lambda e: e.tensor_copy(out=out, in_=in_), reads=[in_], writes=[out])

    def recip(self, out, in_):
        self.add("dve", lambda e: e.reciprocal(out=out, in_=in_), reads=[in_], writes=[out])

    def memset(self, eng, ap, val):
        self.add(eng, lambda e: e.memset(ap, val), writes=[ap])

    def dma(self, q, out, in_, reads=(), writes=()):
        self.add(q, lambda e: e.dma_start(out=out, in_=in_), reads=reads, writes=writes, dma=True)

    def emit(self, es):
        nc = self.nc
        ops = self.ops
        engs = ["pe", "act", "dve", "pool", "sp"]
        for op in ops:
            best = {}
            keep = set()
            for d in op["deps"]:
                a = ops[d]
                if a["dma"]:
                    keep.add(d)
                    continue
                if a["eng"] == "pe" and op["eng"] == "pe" and not op["dma"]:
                    continue
                if d > best.get(a["eng"], -1):
                    best[a["eng"]] = d
            for d in best.values():
                ops[d]["sig"] = True
                keep.add(d)
            op["deps"] = keep
        cnt = {e: 0 for e in engs}
        for op in ops:
            if not op["dma"] and op["sig"]:
                cnt[op["eng"]] += 1
                op["cnt"] = cnt[op["eng"]]
        NDS = 8
        esem = {e: es.enter_context(nc.semaphore("s_" + e)) for e in engs}
        dsem = {q: [es.enter_context(nc.semaphore(f"d_{q}{i}")) for i in range(NDS)] for q in ("sp", "pool", "act")}
        dk = {q: 0 for q in dsem}
        for op in ops:
            if op["dma"]:
                q = op["eng"]
                k = dk[q]
                dk[q] += 1
                op["dsem"] = dsem[q][k % NDS]
                op["dval"] = 16 * (k // NDS + 1)
                op["dprev"] = 16 * (k // NDS)
        streams = {e: [op for op in ops if op["eng"] == e] for e in engs}
        self.trace = []
        semname = {id(v): k for k, v in esem.items()}
        for q in dsem:
            for i, sm in enumerate(dsem[q]):
                semname[id(sm)] = f"d_{q}{i}"

        def run(e, name):
            waited = {}

            def wait(sem, val):
                key = id(sem)
                if waited.get(key, 0) >= val:
                    return
                waited[key] = val
                self.trace.append((name, "wait", semname.get(key, "?"), val))
                e.wait_ge(sem, val)

            for op in streams[name]:
                for d in sorted(op["deps"]):
                    a = ops[d]
                    if a["dma"]:
                        wait(a["dsem"], a["dval"])
                    else:
                        if a["eng"] == "pe" and name == "pe" and not op["dma"]:
                            continue
                        wait(esem[a["eng"]], a["cnt"])
                if op["dma"]:
                    if op["dprev"] > 0:
                        wait(op["dsem"], op["dprev"])
                    ins = op["fn"](e)
                    ins.then_inc(op["dsem"], 16)
                    self.trace.append((name, "dma", semname[id(op["dsem"])], op["dval"]))
                else:
                    ins = op["fn"](e)
                    self.trace.append((name, "op", type(ins).__name__, op["cnt"] if op["sig"] else 0))
                    if op["sig"]:
                        ins.then_inc(esem[name], 1)
            if name in dsem:
                for i in range(NDS):
                    k = dk[name]
                    n_i = (k - i + NDS - 1) // NDS if k > i else 0
                    if n_i > 0:
                        wait(dsem[name][i], 16 * n_i)

        with nc.Block() as block:
            @block.tensor
            def _(e):
                run(e, "pe")

            @block.scalar
            def _(e):
                run(e, "act")

            @block.vector
            def _(e):
                run(e, "dve")

            @block.gpsimd
            def _(e):
                run(e, "pool")

            @block.sync
            def _(e):
                run(e, "sp")


def tile_chunks(m):
    if m <= 1:
        return list(range(0, 4)), m
    if m >= 14:
        return list(range(12, 16)), m
    return list(range(m - 2, m + 3)), "int"


def build_program(stop_after=None, dbg_names=()):
    nc = bass.Bass("TRN2", target_bir_lowering=False)

    def din(name, shape, dt=F32):
        return nc.dram_tensor(name, list(shape), dt, kind="ExternalInput").ap()

    xT = din("xT", [D, T])
    cxT = din("cxT", [D, CT])
    cvec_d = din("cvec", [128, 16])
    ada_w = din("ada_w", [DEPTH, D, 6 * D])
    adab_d = din("adab", [128, DEPTH * 96])
    ng_d = din("ng", [128, DEPTH * 2 * 16])
    fng_d = din("fng", [128, 8])
    cp_d = din("cp", [128, DEPTH * 2 * 35])
    w_in = din("w_in", [DEPTH, D, DIN])
    w_inFT = din("w_inFT", [DEPTH, 256, D])
    w_f = din("w_f", [DEPTH, 256, 256])
    w_pw = din("w_pw", [DEPTH, 256, 256])
    w_out = din("w_out", [DEPTH, D, D])
    w1 = din("w1", [DEPTH, D, DFF])
    w2 = din("w2", [DEPTH, DFF, D])
    csblk_d = din("csblk", [256, 512])
    dft_d = din("dft", [8, 128, 16 * 2 * 256], BF16)
    dft256_d = din("dft256", [128, 2 * 2 * 256], BF16)
    ident_d = din("ident", [128, 128], BF16)
    bias_d = din("biasT", [DEPTH, 8, 128, NBLK * 128])
    outT = nc.dram_tensor("outT", [D, T], F32, kind="ExternalOutput").ap()
    dbg_out = {}

    es = ExitStack()
    with es:
        def sb(name, n, dt):
            return es.enter_context(nc.sbuf_tensor(name, [128, n], dt))

        H_t = sb("H", 8 * T, F32)
        Hc_t = sb("Hc", 8 * CT, F32)
        hn_t = sb("hn", 8 * T, BF16)
        hnc_t = sb("hnc", 8 * CT, BF16)
        mixc_t = sb("mixc", 8 * CT, BF16)
        AR = sb("arena", 40960, BF16)
        ident = sb("identS", 128, BF16)
        ones_rms = sb("ones_rms", 128, BF16)
        ones_ln = sb("ones_ln", 128, BF16)
        csblk = sb("csblkS", 2 * 512, BF16)
        dft256 = sb("dft256S", 2 * 2 * 256, BF16)
        wf_t = sb("wfS", 2 * 256, BF16)
        pw_t = sb("pwS", 2 * 256, BF16)
        cvec = sb("cvecS", 16, F32)
        csil = sb("csil", 16, BF16)
        adab = sb("adabS", DEPTH * 96, F32)
        ngs = sb("ngS", DEPTH * 32, F32)
        fng = sb("fngS", 8, F32)
        cps = sb("cpS", DEPTH * 70, F32)
        mod_t = sb("mod", DEPTH * 96, F32)
        coef_t = sb("coef", DEPTH * 96, F32)
        pp = [es.enter_context(nc.psum_tensor(f"pp{i}", [128, 1024], F32)) for i in range(4)]

        B = Builder(nc)

        def bank(i):
            i = i % 8
            return pp[i // 2][:, (i % 2) * 512:(i % 2) * 512 + 512]

        bank_ctr = [0]

        def nbank():
            b = bank(bank_ctr[0])
            bank_ctr[0] += 1
            return b

        H = H_t[:].rearrange("p (c t) -> p c t", c=8)
        Hc = Hc_t[:].rearrange("p (c t) -> p c t", c=8)
        hn = hn_t[:].rearrange("p (c t) -> p c t", c=8)
        hnc = hnc_t[:].rearrange("p (c t) -> p c t", c=8)
        mixc = mixc_t[:].rearrange("p (c t) -> p c t", c=8)

        def av(off_kib, n):
            o = int(round(off_kib * 512))
            return AR[:, o:o + n]

        mix = av(0, 8 * T).rearrange("p (c t) -> p c t", c=8)
        WA = av(32, 4096)
        WB = av(40, 4096)
        sqb = [av(48, 512), av(49, 512)]
        rstd_b = av(50, 1024).bitcast(F32)
        t32 = [av(52, 1024).bitcast(F32), av(54, 1024).bitcast(F32)]
        SCR = 56.0

        class Stream:
            pass

        LAT = Stream()
        LAT.T, LAT.H, LAT.hn, LAT.mix, LAT.col, LAT.name = T, H, hn, mix, 0, "lat"
        CTX = Stream()
        CTX.T, CTX.H, CTX.hn, CTX.mix, CTX.col, CTX.name = CT, Hc, hnc, mixc, 1, "ctx"

        def blocks(S, n=512):
            return [(o, min(n, S.T - o)) for o in range(0, S.T, n)]

        def coef(l, k, c, col):
            o = l * 96 + k * 16 + c * 2 + col
            return coef_t[:, o:o + 1]

        B.dma("sp", ident[:], ident_d, writes=[ident[:]])
        B.dma("sp", cvec[:], cvec_d, writes=[cvec[:]])
        B.dma("sp", adab[:], adab_d, writes=[adab[:]])
        B.dma("sp", ngs[:], ng_d, writes=[ngs[:]])
        B.dma("sp", fng[:], fng_d, writes=[fng[:]])
        B.dma("sp", cps[:], cp_d, writes=[cps[:]])
        B.dma("sp", dft256[:], dft256_d, writes=[dft256[:]])
        B.dma("pool", csblk[:].rearrange("p (c n) -> p c n", c=2), csblk_d.rearrange("(c p) n -> p c n", p=128),
              writes=[csblk[:]])
        B.memset("pool", ones_rms[:], 1.0 / D)
        B.memset("pool", ones_ln[:], 1.0 / 256)
        B.dma("sp", Hc, cxT.rearrange("(c p) t -> p c t", p=128), writes=[Hc_t[:]])
        for tb in range(4):
            B.dma("sp", H[:, :, tb * 512:(tb + 1) * 512], xT.rearrange("(c p) t -> p c t", p=128)[:, :, tb * 512:(tb + 1) * 512],
                  writes=[H[:, c, tb * 512:(tb + 1) * 512] for c in range(8)])
        B.act(csil[:], cvec[:], AF.Silu)

        def ada_slice(l, sidx, slot, ps):
            csv = csil[:].rearrange("p (c k) -> p c k", c=8)
            sv = slot.rearrange("p (c n) -> p c n", c=8)
            B.dma("pool", sv, ada_w[l].rearrange("(c p) n -> p c n", p=128)[:, :, sidx * 512:(sidx + 1) * 512], writes=[slot])
            for jj in range(4):
                for dc in range(8):
                    B.mm(ps[:, 2 * jj:2 * jj + 2], sv[:, dc, jj * 128:(jj + 1) * 128], csv[:, dc, :], dc == 0, dc == 7)
            o = l * 96 + sidx * 8
            B.tt("dve", mod_t[:, o:o + 8], ps[:, 0:8], adab[:, o:o + 8], ALU.add)

        def ada_coefs(l, ks):
            m3 = mod_t[:, l * 96:(l + 1) * 96].rearrange("p (k x) -> p k x", k=6)
            c3 = coef_t[:, l * 96:(l + 1) * 96].rearrange("p (k x) -> p k x", k=6)
            n3 = ngs[:, l * 32:(l + 1) * 32].rearrange("p (k x) -> p k x", k=2)
            for k in ks:
                if k == 0:
                    B.stt(c3[:, 0, :], m3[:, 1, :], 1.0, n3[:, 0, :], ALU.add, ALU.mult)
                elif k == 1:
                    B.copy("dve", c3[:, 1, :], m3[:, 0, :])
                elif k == 2:
                    B.copy("dve", c3[:, 2, :], m3[:, 2, :])
                elif k == 3:
                    B.stt(c3[:, 3, :], m3[:, 4, :], 1.0, n3[:, 1, :], ALU.add, ALU.mult)
                elif k == 4:
                    B.copy("dve", c3[:, 4, :], m3[:, 3, :])
                elif k == 5:
                    B.copy("dve", c3[:, 5, :], m3[:, 5, :])

        def norm(S, l, kA, kB, final=False):
            for (o, n) in blocks(S):
                ps = nbank()
                for c in range(8):
                    sq = sqb[c % 2]
                    B.act(sq[:, :n], S.H[:, c, o:o + n], AF.Square)
                    B.mm(ps[:, :n], ones_rms[:], sq[:, :n], c == 0, c == 7)
                B.act(rstd_b[:, :n], ps[:, :n], AF.Sqrt, bias=RMS_EPS_AP[0], scale=1.0)
                B.recip(rstd_b[:, :n], rstd_b[:, :n])
                for c in range(8):
                    t = t32[c % 2]
                    B.tt("dve", t[:, :n], S.H[:, c, o:o + n], rstd_b[:, :n], ALU.mult)
                    if final:
                        B.act(S.H[:, c, o:o + n], t[:, :n], AF.Identity, scale=fng[:, c:c + 1])
                    else:
                        B.act(S.hn[:, c, o:o + n], t[:, :n], AF.Identity, bias=coef(l, kB, c, S.col), scale=coef(l, kA, c, S.col))

        eps_t = sb("epsS", 2, F32)
        B.memset("pool", eps_t[:, 0:1], RMS_EPS)
        B.memset("pool", eps_t[:, 1:2], LN_EPS)
        RMS_EPS_AP = [eps_t[:, 0:1]]
        LN_EPS_AP = eps_t[:, 1:2]

        evac_rr = [0]

        def evac(out, in_):
            eng = ("act", "dve")[evac_rr[0] % 2]
            evac_rr[0] += 1
            B.copy(eng, out, in_)

        def proj_cm(S, wv, col0, evac_fn, nblk=512):
            for (o, n) in blocks(S, nblk):
                ps = nbank()
                for dc in range(8):
                    B.mm(ps[:, :n], wv[:, dc, col0:col0 + 128], S.hn[:, dc, o:o + n], dc == 0, dc == 7)
                evac_fn(ps[:, :n], o, n)

        def attention(l, last, bg=()):
            QTA = av(SCR + 0, T)
            QTB = av(SCR + 4, T)
            KT = av(SCR + 8, T)
            Vp = av(SCR + 12, 16 * 256).rearrange("p (t x) -> p t x", t=16)
            KcT = av(SCR + 20, CT)
            Vc = av(SCR + 20.5, 2 * 256).rearrange("p (t x) -> p t x", t=2)
            QcA = av(48, CT)
            QcB = av(48.5, CT)
            rec = [av(52, 256).bitcast(F32), av(52.5, 256).bitcast(F32)]
            Eb = [av(0, NBLK * 128), av(5.25, NBLK * 128)]
            PT = [av(10.5 + 1.75 * i, 896) for i in range(3)]
            B.memset("pool", Vp[:, :, 64:128], 1.0)
            B.memset("pool", Vc[:, :, 64:128], 1.0)
            B.memset("pool", QTA[64:128, :], 0.0)
            B.memset("pool", QTB[0:64, :], 0.0)
            B.memset("pool", QcA[64:128, :], 0.0)
            B.memset("pool", QcB[0:64, :], 0.0)
            Sps = [pp[0], pp[1], pp[2]]
            Obank = pp[3][:, 0:512]
            Mbanks = [pp[0][:, 0:512], pp[0][:, 512:1024], pp[1][:, 0:512], pp[1][:, 512:1024],
                      pp[2][:, 0:512], pp[2][:, 512:1024], pp[3][:, 512:1024]]
            mctr = [0]

            def nM():
                b = Mbanks[mctr[0] % len(Mbanks)]
                mctr[0] += 1
                return b
            uctr = [0]
            bgq = list(bg)

            for p in range(DBG["pairs"]):
                wv = WB.rearrange("p (c n) -> p c n", c=8)[:, :, 0:384]
                c0 = 768 + p * 384
                B.dma("pool", wv, w_in[l].rearrange("(c p) n -> p c n", p=128)[:, :, c0:c0 + 384], writes=[WB])
                for hh in range(2):
                    B.dma("pool", Eb[hh], bias_d[l, 2 * p + hh], writes=[Eb[hh]])
                    B.act(Eb[hh], Eb[hh], AF.Exp)

                def ev_kc(ps, o, n):
                    evac(KcT[:, o:o + n], ps)
                proj_cm_fixed(CTX, wv, 128, ev_kc, nM)
                if not last:
                    def ev_qc(ps, o, n):
                        B.copy("act", QcA[0:64, o:o + n], ps[0:64, :])
                        B.copy("dve", QcB[64:128, o:o + n], ps[64:128, :])
                    proj_cm_fixed(CTX, wv, 0, ev_qc, nM)
                for t in range(2):
                    mb = nM()
                    for dc in range(8):
                        B.mm(mb[:, 0:128], hnc[:, dc, t * 128:(t + 1) * 128], wv[:, dc, 256:384], dc == 0, dc == 7)
                    B.copy("dve", Vc[:, t, :].rearrange("p (a b) -> p a b", a=2)[:, :, 0:64],
                           mb[:, 0:128].rearrange("p (a b) -> p a b", a=2))

                def ev_q(ps, o, n):
                    B.copy("act", QTA[0:64, o:o + n], ps[0:64, :])
                    B.copy("dve", QTB[64:128, o:o + n], ps[64:128, :])

                def ev_k(ps, o, n):
                    evac(KT[:, o:o + n], ps)
                proj_cm_fixed(LAT, wv, 0, ev_q, nM)
                proj_cm_fixed(LAT, wv, 128, ev_k, nM)
                for t4 in range(4):
                    mb = nM()
                    for tt_ in range(4):
                        t = t4 * 4 + tt_
                        for dc in range(8):
                            B.mm(mb[:, tt_ * 128:(tt_ + 1) * 128], hn[:, dc, t * 128:(t + 1) * 128], wv[:, dc, 256:384], dc == 0, dc == 7)
                    for tt_ in range(4):
                        t = t4 * 4 + tt_
                        B.copy(("dve", "act")[tt_ % 2], Vp[:, t, :].rearrange("p (a b) -> p a b", a=2)[:, :, 0:64],
                               mb[:, tt_ * 128:(tt_ + 1) * 128].rearrange("p (a b) -> p a b", a=2))

                units = []
                if not last:
                    for hh in range(2):
                        for m in range(2):
                            units.append(("ctx", hh, m))
                for hh in range(2):
                    for m in range(16):
                        units.append(("lat", hh, m))
                if DBG["units"] is not None:
                    units = units[:DBG["units"]]

                def unit_S(u):
                    kind, hh, m = units[u]
                    k = uctr[0] + u
                    Sp = Sps[k % 3]
                    if kind == "lat":
                        chunks, _ = tile_chunks(m)
                        q = (QTA, QTB)[hh][:, m * 128:(m + 1) * 128]
                        for i, j in enumerate(chunks):
                            B.mm(Sp[:, i * 128:(i + 1) * 128], KT[:, j * 128:(j + 1) * 128], q, True, True)
                        nl = len(chunks)
                    else:
                        q = (QcA, QcB)[hh][:, m * 128:(m + 1) * 128]
                        nl = 0
                    for cc in range(2):
                        B.mm(Sp[:, (nl + cc) * 128:(nl + cc + 1) * 128], KcT[:, cc * 128:(cc + 1) * 128], q, True, True)

                def unit_rest(u):
                    kind, hh, m = units[u]
                    k = uctr[0] + u
                    Sp = Sps[k % 3]
                    P = PT[k % 3]
                    if kind == "lat":
                        chunks, typ = tile_chunks(m)
                    else:
                        chunks, typ = [], None
                    nl = len(chunks)
                    ns = nl + 2
                    B.act(P[:, 0:ns * 128], Sp[:, 0:ns * 128], AF.Exp, scale=0.125)
                    if nl:
                        eo = EOFF[typ] * 128
                        B.tt("dve", P[:, 0:nl * 128], P[:, 0:nl * 128], Eb[hh][:, eo:eo + nl * 128], ALU.mult)
                    if DBG["parts"] < 3:
                        return
                    Op = Obank[:, (k % 4) * 128:(k % 4 + 1) * 128]
                    vo = 64 * hh
                    for i, j in enumerate(chunks):
                        B.mm(Op, Vp[:, j, vo:vo + 128], P[:, i * 128:(i + 1) * 128], i == 0, False)
                    for cc in range(2):
                        B.mm(Op, Vc[:, cc, vo:vo + 128], P[:, (nl + cc) * 128:(nl + cc + 1) * 128], (nl == 0 and cc == 0), cc == 1)
                    if DBG["parts"] < 4:
                        B.copy("act", rec[0][:, :], Op)
                        return
                    r = rec[k % 2]
                    olo, ohi = (0, 64) if hh == 0 else (64, 128)
                    dlo, dhi = (64, 128) if hh == 0 else (0, 64)
                    B.recip(r[dlo:dhi, :], Op[dlo:dhi, :])
                    dst = (mix if kind == "lat" else mixc)[olo:ohi, 4 + p, m * 128:(m + 1) * 128]
                    B.tt("dve", dst, Op[olo:ohi, :], r[dlo:dhi, :], ALU.mult)

                LOOK = 2
                for u in range(min(LOOK, len(units))):
                    unit_S(u)
                for u in range(len(units)):
                    if u + LOOK < len(units):
                        unit_S(u + LOOK)
                    unit_rest(u)
                    if bgq and (uctr[0] + u) % 6 == 5:
                        bgq.pop(0)(WA, pp[3][:, 512:1024])
                uctr[0] += len(units)
            while bgq:
                bgq.pop(0)(WA, pp[3][:, 512:1024])

        def proj_cm_fixed(S, wv, col0, evac_fn, nb):
            for (o, n) in blocks(S):
                psb = nb()
                for dc in range(8):
                    B.mm(psb[:, :n], wv[:, dc, col0:col0 + 128], S.hn[:, dc, o:o + n], dc == 0, dc == 7)
                evac_fn(psb[:, :n], o, n)

        def conv_phase(l, streams):
            wv = WA.rearrange("p (c n) -> p c n", c=8)
            B.dma("pool", wv, w_in[l].rearrange("(c p) n -> p c n", p=128)[:, :, 256:768], writes=[WA])
            B.dma("pool", pw_t[:].rearrange("p (c n) -> p c n", c=2), w_pw[l].rearrange("(c p) n -> p c n", p=128), writes=[pw_t[:]])
            pwv = pw_t[:].rearrange("p (c n) -> p c n", c=2)
            vpad = av(SCR, 2 * (T + 30)).rearrange("p (c t) -> p c t", c=2)
            vpadc = av(0, 2 * (CT + 30)).rearrange("p (c t) -> p c t", c=2)
            doff = SCR + (2 * (T + 30) * 2) / 1024.0
            doff = math.ceil(doff * 16) / 16.0
            Dg = av(doff, 62 * 128).rearrange("p (c k n) -> p c k n", c=2, k=31)
            cpl = cps[:, l * 70:(l + 1) * 70].rearrange("p (c k) -> p c k", c=2)
            sig = av(2, 512)
            ybf = [av(3, 512), av(4, 512)]
            ysq = [av(5, 512), av(6, 512)]
            y32 = [t32[0], t32[1]]
            z32 = rstd_b
            sil = [sqb[0], sqb[1]]
            for ch in range(2):
                for k in range(31):
                    B.ts("dve", Dg[:, ch, k, :], ident[:], cpl[:, ch, k:k + 1], ALU.mult)
            for S in streams:
                vp = vpad if S is LAT else vpadc
                B.memset("pool", vp[:, :, 0:15], 0.0)
                B.memset("pool", vp[:, :, 15 + S.T:30 + S.T], 0.0)
                for (o, n) in blocks(S):
                    for ch in range(2):
                        pa = nbank()
                        pg = nbank()
                        for dc in range(8):
                            B.mm(pa[:, :n], wv[:, dc, ch * 128:(ch + 1) * 128], S.hn[:, dc, o:o + n], dc == 0, dc == 7)
                        for dc in range(8):
                            B.mm(pg[:, :n], wv[:, dc, 256 + ch * 128:256 + (ch + 1) * 128], S.hn[:, dc, o:o + n], dc == 0, dc == 7)
                        B.act(sig[:, :n], pg[:, :n], AF.Sigmoid)
                        B.tt("dve", vp[:, ch, 15 + o:15 + o + n], pa[:, :n], sig[:, :n], ALU.mult)
                for (o, n) in blocks(S):
                    pm = nbank()
                    pq = nbank()
                    for ch in range(2):
                        pc = nbank()
                        for k in range(31):
                            B.mm(pc[:, :n], Dg[:, ch, k, :], vp[:, ch, o + k:o + k + n], k == 0, k == 30)
                        B.act(ybf[ch][:, :n], pc[:, :n], AF.Identity, bias=cpl[:, ch, 31:32], scale=1.0)
                        B.act(ysq[ch][:, :n], pc[:, :n], AF.Square, bias=cpl[:, ch, 31:32], scale=1.0)
                        B.ts("dve", y32[ch][:, :n], pc[:, :n], cpl[:, ch, 31:32], ALU.add)
                    for ch in range(2):
                        B.mm(pm[:, :n], ones_ln[:], ybf[ch][:, :n], ch == 0, ch == 1)
                    for ch in range(2):
                        B.mm(pq[:, :n], ones_ln[:], ysq[ch][:, :n], ch == 0, ch == 1)
                    B.act(z32[:, :n], pm[:, :n], AF.Square)
                    B.tt("dve", z32[:, :n], pq[:, :n], z32[:, :n], ALU.subtract)
                    B.act(z32[:, :n], z32[:, :n], AF.Sqrt, bias=LN_EPS_AP, scale=1.0)
                    B.recip(z32[:, :n], z32[:, :n])
                    for ch in range(2):
                        B.tt("dve", y32[ch][:, :n], y32[ch][:, :n], pm[:, :n], ALU.subtract)
                        B.tt("dve", y32[ch][:, :n], y32[ch][:, :n], z32[:, :n], ALU.mult)
                        B.act(sil[ch][:, :n], y32[ch][:, :n], AF.Silu, bias=cpl[:, ch, 33:34], scale=cpl[:, ch, 32:33])
                    for oc in range(2):
                        po = nbank()
                        for ch in range(2):
                            B.mm(po[:, :n], pwv[:, ch, oc * 128:(oc + 1) * 128], sil[ch][:, :n], ch == 0, ch == 1)
                        B.act(S.mix[:, 2 + oc, o:o + n], po[:, :n], AF.Identity, bias=cpl[:, oc, 34:35], scale=1.0)

        def fourier_phase(l, streams):
            ftv = WB[:, 0:2048].rearrange("p (c n) -> p c n", c=2)
            B.dma("pool", ftv, w_inFT[l].rearrange("(c p) n -> p c n", p=128), writes=[WB[:, 0:2048]])
            B.dma("pool", wf_t[:].rearrange("p (c n) -> p c n", c=2), w_f[l].rearrange("(c p) n -> p c n", p=128), writes=[wf_t[:]])
            wfv = wf_t[:].rearrange("p (c n) -> p c n", c=2)
            csv = csblk[:].rearrange("p (c n) -> p c n", c=2)
            Wp = WA.rearrange("p (c n) -> p c n", c=8)
            for dc in range(8):
                ps = nbank()
                for cc in range(2):
                    B.mm(ps, ftv[:, cc, dc * 128:(dc + 1) * 128], csv[:, cc, :], cc == 0, cc == 1)
                evac(Wp[:, dc, :], ps)
            UU = av(SCR, 16 * 512).rearrange("p (t n) -> p t n", t=16)
            UUc = av(SCR + 16, 2 * 512).rearrange("p (t n) -> p t n", t=2)
            FT = av(SCR + 18, 2 * 256).rearrange("p (c n) -> p c n", c=2)
            for S in streams:
                U = UU if S is LAT else UUc
                nt = S.T // 128
                for t in range(nt):
                    ps = nbank()
                    for dc in range(8):
                        B.mm(ps, S.hn[:, dc, t * 128:(t + 1) * 128], Wp[:, dc, :], dc == 0, dc == 7)
                    evac(U[:, t, :], ps)
            for S in streams:
                U = UU if S is LAT else UUc
                nt = S.T // 128
                njb = S.T // 256
                for jb in range(njb):
                    if S is LAT:
                        dbuf = hn_t[:, (jb % 2) * 8192:(jb % 2 + 1) * 8192]
                        B.dma("sp", dbuf, dft_d[jb], writes=[dbuf])
                        dv = dbuf.rearrange("p (k s j) -> p k s j", k=16, s=2)
                    else:
                        dv = dft256[:].rearrange("p (k s j) -> p k s j", k=2, s=2)
                    for cg in range(2):
                        ps = nbank()
                        first = True
                        for kc in range(nt):
                            for s in range(2):
                                B.mm(ps[:, 0:256], U[:, kc, s * 256 + cg * 128:s * 256 + (cg + 1) * 128], dv[:, kc, s, :],
                                     first, (kc == nt - 1 and s == 1))
                                first = False
                        evac(FT[:, cg, :], ps[:, 0:256])
                    for oc in range(2):
                        ps = nbank()
                        for cg in range(2):
                            B.mm(ps[:, 0:256], wfv[:, cg, oc * 128:(oc + 1) * 128], FT[:, cg, :], cg == 0, cg == 1)
                        evac(S.mix[:, oc, jb * 256:(jb + 1) * 256], ps[:, 0:256])

        def outproj(l, streams):
            wo = AR[:, 32 * 512:48 * 512].rearrange("p (c n) -> p c n", c=8)
            B.dma("pool", wo[:, 0:4, :], w_out[l].rearrange("(c p) n -> p c n", p=128)[:, 0:4, :], writes=[WA])
            B.dma("pool", wo[:, 4:8, :], w_out[l].rearrange("(c p) n -> p c n", p=128)[:, 4:8, :], writes=[WB])
            for S in streams:
                for (o, n) in blocks(S):
                    for oc in range(8):
                        ps = nbank()
                        for kc in range(8):
                            B.mm(ps[:, :n], wo[:, kc, oc * 128:(oc + 1) * 128], S.mix[:, kc, o:o + n], kc == 0, kc == 7)
                        B.stt(S.H[:, oc, o:o + n], ps[:, :n], coef(l, 2, oc, S.col), S.H[:, oc, o:o + n], ALU.mult, ALU.add)

        def mlp(l, streams, after_first_norm):
            slot_off = [8, 16, 24, 32, 40, 56, 64, 72]
            sets = [[5, 6, 7, 0], [1, 2, 3, 4]]
            hid = [av(0, 2048).rearrange("p (f t) -> p f t", f=8), av(4, 2048).rearrange("p (f t) -> p f t", f=8)]
            rl = [sqb[0][:, 0:256], sqb[1][:, 0:256]]
            hctr = 0
            for fb in range(4):
                st = sets[fb % 2]
                w1s = [av(slot_off[st[0]], 4096).rearrange("p (c n) -> p c n", c=8),
                       av(slot_off[st[1]], 4096).rearrange("p (c n) -> p c n", c=8)]
                w2s = [av(slot_off[st[2]], 4096).rearrange("p (c n) -> p c n", c=4),
                       av(slot_off[st[3]], 4096).rearrange("p (c n) -> p c n", c=4)]
                for hhalf in range(2):
                    c0 = fb * 1024 + hhalf * 512
                    B.dma("pool", w1s[hhalf], w1[l].rearrange("(c p) n -> p c n", p=128)[:, :, c0:c0 + 512],
                          writes=[av(slot_off[st[hhalf]], 4096)])
                for hhalf in range(2):
                    r0 = fb * 8 + hhalf * 4
                    B.dma("pool", w2s[hhalf], w2[l].rearrange("(c p) n -> p c n", p=128)[:, r0:r0 + 4, :],
                          writes=[av(slot_off[st[2 + hhalf]], 4096)])
                if fb == 0 and after_first_norm is not None:
                    after_first_norm()
                for S in streams:
                    for (o, n) in blocks(S, 256):
                        hb = hid[hctr % 2]
                        hctr += 1
                        for fc in range(8):
                            ps = nbank()
                            wv = w1s[fc // 4]
                            cc = (fc % 4) * 128
                            for dc in range(8):
                                B.mm(ps[:, :n], wv[:, dc, cc:cc + 128], S.hn[:, dc, o:o + n], dc == 0, dc == 7)
                            r = rl[fc % 2]
                            B.act(r[:, :n], ps[:, :n], AF.Relu)
                            B.tt("dve", hb[:, fc, :n], r[:, :n], r[:, :n], ALU.mult)
                        for oc in range(8):
                            ps = nbank()
                            for fc in range(8):
                                B.mm(ps[:, :n], w2s[fc // 4][:, fc % 4, oc * 128:(oc + 1) * 128], hb[:, fc, :n], fc == 0, fc == 7)
                            B.stt(S.H[:, oc, o:o + n], ps[:, :n], coef(l, 5, oc, S.col), S.H[:, oc, o:o + n], ALU.mult, ALU.add)

        def dump(name, ap_sb, shape):
            d = nc.dram_tensor("dbg_" + name, list(shape), ap_sb.dtype, kind="ExternalOutput").ap()
            B.dma("sp", d, ap_sb, reads=[ap_sb])
            dbg_out[name] = True

        done = False
        for l in range(DEPTH):
            last = l == DEPTH - 1
            streams = [LAT] if last else [CTX, LAT]
            bg = []
            if l == 0:
                for sidx in range(4):
                    ada_slice(0, sidx, (WA, WB)[sidx % 2], nbank())
                ada_coefs(0, [0, 1])
                for sidx in range(4, 12):
                    bg.append(lambda slot, ps, sidx=sidx: ada_slice(0, sidx, slot, ps))
                bg.append(lambda slot, ps: ada_coefs(0, [2, 3, 4, 5]))
                for sidx in range(12):
                    bg.append(lambda slot, ps, sidx=sidx: ada_slice(1, sidx, slot, ps))
                bg.append(lambda slot, ps: ada_coefs(1, [0, 1, 2, 3, 4, 5]))
            if stop_after == f"ada{l}":
                dump("mod", mod_t[:], [128, DEPTH * 96])
                dump("coef", coef_t[:], [128, DEPTH * 96])
                done = True
                break
            norm(CTX, l, 0, 1)
            norm(LAT, l, 0, 1)
            if stop_after == f"norm{l}":
                dump("hn", hn_t[:], [128, 8 * T])
                dump("hnc", hnc_t[:], [128, 8 * CT])
                done = True
                break
            attention(l, last, bg)
            if stop_after == f"attn{l}":
                dump("mix", AR[:, 0:8 * T], [128, 8 * T])
                dump("mixc", mixc_t[:], [128, 8 * CT])
                done = True
                break
            conv_phase(l, streams)
            if stop_after == f"conv{l}":
                dump("mix", AR[:, 0:8 * T], [128, 8 * T])
                dump("mixc", mixc_t[:], [128, 8 * CT])
                done = True
                break
            fourier_phase(l, streams)
            if stop_after == f"four{l}":
                dump("mix", AR[:, 0:8 * T], [128, 8 * T])
                dump("mixc", mixc_t[:], [128, 8 * CT])
                done = True
                break
            outproj(l, streams)
            if stop_after == f"oproj{l}":
                dump("H", H_t[:], [128, 8 * T])
                dump("Hc", Hc_t[:], [128, 8 * CT])
                done = True
                break
            for S in streams:
                norm(S, l, 3, 4)
            mlp(l, streams, None)
            if stop_after == f"mlp{l}":
                dump("H", H_t[:], [128, 8 * T])
                dump("Hc", Hc_t[:], [128, 8 * CT])
                done = True
                break
        if not done:
            norm(LAT, 0, 0, 0, final=True)
        for c in range(8):
            B.dma("sp", outT.rearrange("(c p) t -> p c t", p=128)[:, c, :], H[:, c, :], reads=[H[:, c, :]])
        B.emit(es)
    _CACHE['trace'] = B.trace
    return nc, sorted(dbg_out.keys()), len(B.ops)


def _bias_tables(rpb):
    kc = np.arange(64)
    qc = np.arange(64)
    cs = np.clip(qc - 8, 0, 48)
    colvalid = (kc[:, None] >= cs[None, :]) & (kc[:, None] < cs[None, :] + 16)
    coloff = np.clip(kc[:, None] - qc[None, :] + 15, 0, 30)
    blocks = []
    specs = [(5, j) for j in range(3, 8)] + [(0, j) for j in range(4)] + [(1, j) for j in range(4)] + \
            [(14, j) for j in range(12, 16)] + [(15, j) for j in range(12, 16)]
    out = np.full((DEPTH, 8, 128, NBLK, 128), NEG, np.float32)
    for bi, (m, j) in enumerate(specs):
        for a in range(2):
            for b in range(2):
                krow = 2 * j + a
                qrow = 2 * m + b
                rs = min(max(qrow - 4, 0), 24)
                if not (rs <= krow < rs + 8):
                    continue
                dr = krow - qrow + 7
                vals = rpb[:, :, dr, :][:, :, coloff]
                vals = np.where(colvalid[None, None], vals, np.float32(NEG))
                out[:, :, a * 64:(a + 1) * 64, bi, b * 64:(b + 1) * 64] = vals
    return np.ascontiguousarray(out.reshape(DEPTH, 8, 128, NBLK * 128))


def _dft_tables():
    k = np.arange(T, dtype=np.int64)
    ang = 2.0 * np.pi * ((k[:, None] * k[None, :]) % T).astype(np.float64) / T
    Cm = (np.cos(ang) / math.sqrt(T)).astype(np.float32)
    Sm = (-np.sin(ang) / math.sqrt(T)).astype(np.float32)
    CS = np.stack([Cm, Sm], axis=0)
    CS = CS.reshape(2, 16, 128, 8, 256)
    dft = np.ascontiguousarray(CS.transpose(3, 2, 1, 0, 4)).reshape(8, 128, 16 * 2 * 256).astype(NPBF)
    k2 = np.arange(CT, dtype=np.int64)
    ang2 = 2.0 * np.pi * ((k2[:, None] * k2[None, :]) % CT).astype(np.float64) / CT
    C2 = (np.cos(ang2) / math.sqrt(CT)).astype(np.float32)
    S2 = (-np.sin(ang2) / math.sqrt(CT)).astype(np.float32)
    CS2 = np.stack([C2, S2], axis=0).reshape(2, 2, 128, 256)
    dft256 = np.ascontiguousarray(CS2.transpose(2, 1, 0, 3)).reshape(128, 2 * 2 * 256).astype(NPBF)
    a = np.arange(64)
    ang3 = 2.0 * np.pi * ((a[:, None] * a[None, :]) % 64) / 64.0
    cb = np.zeros((256, 256), np.float32)
    sbk = np.zeros((256, 256), np.float32)
    for g in range(4):
        cb[g * 64:(g + 1) * 64, g * 64:(g + 1) * 64] = np.cos(ang3) / 8.0
        sbk[g * 64:(g + 1) * 64, g * 64:(g + 1) * 64] = np.sin(ang3) / 8.0
    csblk = np.ascontiguousarray(np.concatenate([cb, sbk], axis=1)).astype(np.float32)
    return dft, dft256, csblk


def host_prep(inp):
    f = lambda a: np.ascontiguousarray(np.asarray(a, dtype=np.float32))
    x, c, ctx, c_ctx = f(inp["x"]), f(inp["c"]), f(inp["ctx"]), f(inp["c_ctx"])
    shared = {}
    shared["ada_w"] = f(inp["ada_w"])
    ab = f(inp["ada_b"]).reshape(DEPTH, 48, 128).transpose(2, 0, 1)
    shared["adab"] = np.ascontiguousarray(np.repeat(ab[:, :, :, None], 2, axis=3)).reshape(128, DEPTH * 96)
    n1 = f(inp["norm1_g"]).reshape(DEPTH, 8, 128).transpose(2, 0, 1)
    n2 = f(inp["norm2_g"]).reshape(DEPTH, 8, 128).transpose(2, 0, 1)
    ng = np.stack([n1, n2], axis=2)
    shared["ng"] = np.ascontiguousarray(np.repeat(ng[..., None], 2, axis=4)).reshape(128, DEPTH * 32)
    shared["fng"] = np.ascontiguousarray(f(inp["final_norm_g"]).reshape(8, 128).T)
    dw = f(inp["conv_dw_w"]).reshape(DEPTH, 31, 2, 128).transpose(3, 0, 2, 1)
    def pv(name):
        return f(inp[name]).reshape(DEPTH, 2, 128).transpose(2, 0, 1)[..., None]
    cp = np.concatenate([dw, pv("conv_dw_b"), pv("conv_norm_g"), pv("conv_norm_b"), pv("conv_pw_b")], axis=3)
    shared["cp"] = np.ascontiguousarray(cp).reshape(128, DEPTH * 70)
    w_in = f(inp["w_in"])
    order = list(range(0, 768))
    for p in range(4):
        for part in range(3):
            base = 768 + part * 512 + p * 128
            order += list(range(base, base + 128))
    shared["w_in"] = np.ascontiguousarray(w_in[:, :, order])
    shared["w_inFT"] = np.ascontiguousarray(w_in[:, :, :256].transpose(0, 2, 1))
    shared["w_f"] = f(inp["w_fourier"])
    shared["w_pw"] = f(inp["conv_pw_w"])
    shared["w_out"] = f(inp["w_out"])
    shared["w1"] = f(inp["mlp_w1"])
    shared["w2"] = f(inp["mlp_w2"])
    dft, dft256, csblk = _dft_tables()
    shared["dft"] = dft
    shared["dft256"] = dft256
    shared["csblk"] = csblk
    shared["ident"] = np.eye(128, dtype=np.float32).astype(NPBF)
    shared["biasT"] = _bias_tables(f(inp["na_rpb"]))
    maps = []
    cc = c_ctx.reshape(8, 128).T
    for b in range(NCORES):
        m = dict(shared)
        m["xT"] = np.ascontiguousarray(x[b].T)
        m["cxT"] = np.ascontiguousarray(ctx[b].T)
        cb = c[b].reshape(8, 128).T
        m["cvec"] = np.ascontiguousarray(np.stack([cb, cc], axis=2)).reshape(128, 16)
        maps.append(m)
    return maps


def kernel(**inputs):
    maps = host_prep(inputs)
    if "nc" not in _CACHE:
        _CACHE["nc"] = build_program()[0]
    nc = _CACHE["nc"]
    res = run_bass_kernel_spmd(nc, maps, core_ids=list(range(NCORES)))
    out = np.stack([np.ascontiguousarray(res.results[b]["outT"].T) for b in range(NCORES)], axis=0)
    return out.astype(np.float32)
```

```python
import math
from contextlib import ExitStack

import numpy as np
import ml_dtypes
import concourse.bass as bass
import concourse.mybir as mybir
from concourse.bass_utils import run_bass_kernel_spmd

F32 = mybir.dt.float32
BF16 = mybir.dt.bfloat16
AF = mybir.ActivationFunctionType
ALU = mybir.AluOpType
NPBF = ml_dtypes.bfloat16

D = 1024
T = 2048
CT = 256
DEPTH = 2
DIN = 2304
DFF = 4096
NCORES = 8
NEG = -30000.0
NBLK = 21
EOFF = {"int": 0, 0: 5, 1: 9, 14: 13, 15: 17}
RMS_EPS = 1e-6
LN_EPS = 1e-5
CELL = 512
_CACHE = {}
DBG = {"pairs": 4, "units": None, "parts": 5, "pv": 0}


class Builder:
    def __init__(self, nc):
        self.nc = nc
        self.ops = []
        self.cw = {}
        self.cr = {}

    @staticmethod
    def _cells(ap):
        sz = 4 if ap.dtype == F32 else 2
        pairs = ap.ap
        pstride, pcount = pairs[0]
        off = ap.offset
        p0 = off // pstride
        f0 = off % pstride
        ext = 0
        for step, cnt in pairs[1:]:
            ext += (cnt - 1) * step
        lo = f0 * sz
        hi = (f0 + ext + 1) * sz
        name = ap.name
        cell = 2048 if name.startswith("pp") else CELL
        out = []
        for q in range(p0 // 32, (p0 + pcount - 1) // 32 + 1):
            for ci in range(lo // cell, (hi - 1) // cell + 1):
                out.append((name, q, ci))
        return out

    def add(self, eng, fn, reads=(), writes=(), dma=False):
        oid = len(self.ops)
        deps = set()
        rc = [c for ap in reads for c in self._cells(ap)]
        wc = [c for ap in writes for c in self._cells(ap)]
        wc += [c for c in rc if c[0].startswith("pp")]
        for c in rc:
            w = self.cw.get(c)
            if w is not None:
                deps.add(w)
        for c in wc:
            w = self.cw.get(c)
            if w is not None:
                deps.add(w)
            r = self.cr.get(c)
            if r:
                deps.update(r.values())
        key = ("d", oid) if dma else eng
        for c in wc:
            self.cw[c] = oid
            self.cr[c] = {}
        for c in rc:
            self.cr.setdefault(c, {})[key] = oid
        self.ops.append(dict(eng=eng, fn=fn, deps=deps, dma=dma, sig=False, cnt=0))
        return oid

    def mm(self, out, lhsT, rhs, start, stop):
        self.add("pe", lambda e: e.matmul(out, lhsT, rhs, start=start, stop=stop), reads=[lhsT, rhs], writes=[out])

    def act(self, out, in_, func, bias=None, scale=None):
        reads = [in_]
        kw = {}
        if bias is not None:
            kw["bias"] = bias
            if not isinstance(bias, float):
                reads.append(bias)
        if scale is not None:
            kw["scale"] = scale
            if not isinstance(scale, float):
                reads.append(scale)
        self.add("act", lambda e: e.activation(out=out, in_=in_, func=func, **kw), reads=reads, writes=[out])

    def tt(self, eng, out, in0, in1, op):
        self.add(eng, lambda e: e.tensor_tensor(out=out, in0=in0, in1=in1, op=op), reads=[in0, in1], writes=[out])

    def ts(self, eng, out, in0, s1, op0, s2=None, op1=None):
        reads = [in0]
        if not isinstance(s1, float):
            reads.append(s1)
        if s2 is not None and not isinstance(s2, float):
            reads.append(s2)
        if op1 is None:
            self.add(eng, lambda e: e.tensor_scalar(out=out, in0=in0, scalar1=s1, scalar2=None, op0=op0), reads=reads, writes=[out])
        else:
            self.add(eng, lambda e: e.tensor_scalar(out=out, in0=in0, scalar1=s1, scalar2=s2, op0=op0, op1=op1), reads=reads, writes=[out])

    def stt(self, out, in0, scalar, in1, op0, op1):
        reads = [in0, in1]
        if not isinstance(scalar, float):
            reads.append(scalar)
        self.add("dve", lambda e: e.scalar_tensor_tensor(out=out, in0=in0, scalar=scalar, in1=in1, op0=op0, op1=op1),
                 reads=reads, writes=[out])

    def copy(self, eng, out, in_):
        if eng == "act":
            self.add("act", lambda e: e.copy(out=out, in_=in_), reads=[in_], writes=[out])
        else:
            self.add(eng, lambda e: e.tensor_copy(out=out, in_=in_), reads=[in_], writes=[out])

    def recip(self, out, in_):
        self.add("dve", lambda e: e.reciprocal(out=out, in_=in_), reads=[in_], writes=[out])

    def memset(self, eng, ap, val):
        self.add(eng, lambda e: e.memset(ap, val), writes=[ap])

    def dma(self, q, out, in_, reads=(), writes=()):
        self.add(q, lambda e: e.dma_start(out=out, in_=in_), reads=reads, writes=writes, dma=True)

    def emit(self, es):
        nc = self.nc
        ops = self.ops
        engs = ["pe", "act", "dve", "pool", "sp"]
        for op in ops:
            best = {}
            keep = set()
            for d in op["deps"]:
                a = ops[d]
                if a["dma"]:
                    keep.add(d)
                    continue
                if a["eng"] == "pe" and op["eng"] == "pe" and not op["dma"]:
                    continue
                if d > best.get(a["eng"], -1):
                    best[a["eng"]] = d
            for d in best.values():
                ops[d]["sig"] = True
                keep.add(d)
            op["deps"] = keep
        cnt = {e: 0 for e in engs}
        for op in ops:
            if not op["dma"] and op["sig"]:
                cnt[op["eng"]] += 1
                op["cnt"] = cnt[op["eng"]]
        NDS = 8
        esem = {e: es.enter_context(nc.semaphore("s_" + e)) for e in engs}
        dsem = {q: [es.enter_context(nc.semaphore(f"d_{q}{i}")) for i in range(NDS)] for q in ("sp", "pool", "act")}
        dk = {q: 0 for q in dsem}
        for op in ops:
            if op["dma"]:
                q = op["eng"]
                k = dk[q]
                dk[q] += 1
                op["dsem"] = dsem[q][k % NDS]
                op["dval"] = 16 * (k // NDS + 1)
                op["dprev"] = 16 * (k // NDS)
        streams = {e: [op for op in ops if op["eng"] == e] for e in engs}
        self.trace = []
        semname = {id(v): k for k, v in esem.items()}
        for q in dsem:
            for i, sm in enumerate(dsem[q]):
                semname[id(sm)] = f"d_{q}{i}"

        def run(e, name):
            waited = {}

            def wait(sem, val):
                key = id(sem)
                if waited.get(key, 0) >= val:
                    return
                waited[key] = val
                self.trace.append((name, "wait", semname.get(key, "?"), val))
                e.wait_ge(sem, val)

            for op in streams[name]:
                for d in sorted(op["deps"]):
                    a = ops[d]
                    if a["dma"]:
                        wait(a["dsem"], a["dval"])
                    else:
                        if a["eng"] == "pe" and name == "pe" and not op["dma"]:
                            continue
                        wait(esem[a["eng"]], a["cnt"])
                if op["dma"]:
                    if op["dprev"] > 0:
                        wait(op["dsem"], op["dprev"])
                    ins = op["fn"](e)
                    ins.then_inc(op["dsem"], 16)
                    self.trace.append((name, "dma", semname[id(op["dsem"])], op["dval"]))
                else:
                    ins = op["fn"](e)
                    self.trace.append((name, "op", type(ins).__name__, op["cnt"] if op["sig"] else 0))
                    if op["sig"]:
                        ins.then_inc(esem[name], 1)
            if name in dsem:
                for i in range(NDS):
                    k = dk[name]
                    n_i = (k - i + NDS - 1) // NDS if k > i else 0
                    if n_i > 0:
                        wait(dsem[name][i], 16 * n_i)

        with nc.Block() as block:
            @block.tensor
            def _(e):
                run(e, "pe")

            @block.scalar
            def _(e):
                run(e, "act")

            @block.vector
            def _(e):
                run(e, "dve")

            @block.gpsimd
            def _(e):
                run(e, "pool")

            @block.sync
            def _(e):
                run(e, "sp")


def tile_chunks(m):
    if m <= 1:
        return list(range(0, 4)), m
    if m >= 14:
        return list(range(12, 16)), m
    return list(range(m - 2, m + 3)), "int"


def build_program(stop_after=None, dbg_names=()):
    nc = bass.Bass("TRN2", target_bir_lowering=False)

    def din(name, shape, dt=F32):
        return nc.dram_tensor(name, list(shape), dt, kind="ExternalInput").ap()

    xT = din("xT", [D, T])
    cxT = din("cxT", [D, CT])
    cvec_d = din("cvec", [128, 16])
    ada_w = din("ada_w", [DEPTH, D, 6 * D])
    adab_d = din("adab", [128, DEPTH * 96])
    ng_d = din("ng", [128, DEPTH * 2 * 16])
    fng_d = din("fng", [128, 8])
    cp_d = din("cp", [128, DEPTH * 2 * 35])
    w_in = din("w_in", [DEPTH, D, DIN])
    w_inFT = din("w_inFT", [DEPTH, 256, D])
    w_f = din("w_f", [DEPTH, 256, 256])
    w_pw = din("w_pw", [DEPTH, 256, 256])
    w_out = din("w_out", [DEPTH, D, D])
    w1 = din("w1", [DEPTH, D, DFF])
    w2 = din("w2", [DEPTH, DFF, D])
    csblk_d = din("csblk", [256, 512])
    dft_d = din("dft", [8, 128, 16 * 2 * 256], BF16)
    dft256_d = din("dft256", [128, 2 * 2 * 256], BF16)
    ident_d = din("ident", [128, 128], BF16)
    bias_d = din("biasT", [DEPTH, 8, 128, NBLK * 128])
    outT = nc.dram_tensor("outT", [D, T], F32, kind="ExternalOutput").ap()
    dbg_out = {}

    es = ExitStack()
    with es:
        def sb(name, n, dt):
            return es.enter_context(nc.sbuf_tensor(name, [128, n], dt))

        H_t = sb("H", 8 * T, F32)
        Hc_t = sb("Hc", 8 * CT, F32)
        hn_t = sb("hn", 8 * T, BF16)
        hnc_t = sb("hnc", 8 * CT, BF16)
        mixc_t = sb("mixc", 8 * CT, BF16)
        AR = sb("arena", 40960, BF16)
        ident = sb("identS", 128, BF16)
        ones_rms = sb("ones_rms", 128, BF16)
        ones_ln = sb("ones_ln", 128, BF16)
        csblk = sb("csblkS", 2 * 512, BF16)
        dft256 = sb("dft256S", 2 * 2 * 256, BF16)
        wf_t = sb("wfS", 2 * 256, BF16)
        pw_t = sb("pwS", 2 * 256, BF16)
        cvec = sb("cvecS", 16, F32)
        csil = sb("csil", 16, BF16)
        adab = sb("adabS", DEPTH * 96, F32)
        ngs = sb("ngS", DEPTH * 32, F32)
        fng = sb("fngS", 8, F32)
        cps = sb("cpS", DEPTH * 70, F32)
        mod_t = sb("mod", DEPTH * 96, F32)
        coef_t = sb("coef", DEPTH * 96, F32)
        pp = [es.enter_context(nc.psum_tensor(f"pp{i}", [128, 1024], F32)) for i in range(4)]

        B = Builder(nc)

        def bank(i):
            i = i % 8
            return pp[i // 2][:, (i % 2) * 512:(i % 2) * 512 + 512]

        bank_ctr = [0]

        def nbank():
            b = bank(bank_ctr[0])
            bank_ctr[0] += 1
            return b

        H = H_t[:].rearrange("p (c t) -> p c t", c=8)
        Hc = Hc_t[:].rearrange("p (c t) -> p c t", c=8)
        hn = hn_t[:].rearrange("p (c t) -> p c t", c=8)
        hnc = hnc_t[:].rearrange("p (c t) -> p c t", c=8)
        mixc = mixc_t[:].rearrange("p (c t) -> p c t", c=8)

        def av(off_kib, n):
            o = int(round(off_kib * 512))
            return AR[:, o:o + n]

        mix = av(0, 8 * T).rearrange("p (c t) -> p c t", c=8)
        WA = av(32, 4096)
        WB = av(40, 4096)
        sqb = [av(48, 512), av(49, 512)]
        rstd_b = av(50, 1024).bitcast(F32)
        t32 = [av(52, 1024).bitcast(F32), av(54, 1024).bitcast(F32)]
        SCR = 56.0

        class Stream:
            pass

        LAT = Stream()
        LAT.T, LAT.H, LAT.hn, LAT.mix, LAT.col, LAT.name = T, H, hn, mix, 0, "lat"
        CTX = Stream()
        CTX.T, CTX.H, CTX.hn, CTX.mix, CTX.col, CTX.name = CT, Hc, hnc, mixc, 1, "ctx"

        def blocks(S, n=512):
            return [(o, min(n, S.T - o)) for o in range(0, S.T, n)]

        def coef(l, k, c, col):
            o = l * 96 + k * 16 + c * 2 + col
            return coef_t[:, o:o + 1]

        B.dma("sp", ident[:], ident_d, writes=[ident[:]])
        B.dma("sp", cvec[:], cvec_d, writes=[cvec[:]])
        B.dma("sp", adab[:], adab_d, writes=[adab[:]])
        B.dma("sp", ngs[:], ng_d, writes=[ngs[:]])
        B.dma("sp", fng[:], fng_d, writes=[fng[:]])
        B.dma("sp", cps[:], cp_d, writes=[cps[:]])
        B.dma("sp", dft256[:], dft256_d, writes=[dft256[:]])
        B.dma("pool", csblk[:].rearrange("p (c n) -> p c n", c=2), csblk_d.rearrange("(c p) n -> p c n", p=128),
              writes=[csblk[:]])
        B.memset("pool", ones_rms[:], 1.0 / D)
        B.memset("pool", ones_ln[:], 1.0 / 256)
        B.dma("sp", Hc, cxT.rearrange("(c p) t -> p c t", p=128), writes=[Hc_t[:]])
        for tb in range(4):
            B.dma("sp", H[:, :, tb * 512:(tb + 1) * 512], xT.rearrange("(c p) t -> p c t", p=128)[:, :, tb * 512:(tb + 1) * 512],
                  writes=[H[:, c, tb * 512:(tb + 1) * 512] for c in range(8)])
        B.act(csil[:], cvec[:], AF.Silu)

        def ada_slice(l, sidx, slot, ps):
            csv = csil[:].rearrange("p (c k) -> p c k", c=8)
            sv = slot.rearrange("p (c n) -> p c n", c=8)
            B.dma("pool", sv, ada_w[l].rearrange("(c p) n -> p c n", p=128)[:, :, sidx * 512:(sidx + 1) * 512], writes=[slot])
            for jj in range(4):
                for dc in range(8):
                    B.mm(ps[:, 2 * jj:2 * jj + 2], sv[:, dc, jj * 128:(jj + 1) * 128], csv[:, dc, :], dc == 0, dc == 7)
            o = l * 96 + sidx * 8
            B.tt("dve", mod_t[:, o:o + 8], ps[:, 0:8], adab[:, o:o + 8], ALU.add)

        def ada_coefs(l, ks):
            m3 = mod_t[:, l * 96:(l + 1) * 96].rearrange("p (k x) -> p k x", k=6)
            c3 = coef_t[:, l * 96:(l + 1) * 96].rearrange("p (k x) -> p k x", k=6)
            n3 = ngs[:, l * 32:(l + 1) * 32].rearrange("p (k x) -> p k x", k=2)
            for k in ks:
                if k == 0:
                    B.stt(c3[:, 0, :], m3[:, 1, :], 1.0, n3[:, 0, :], ALU.add, ALU.mult)
                elif k == 1:
                    B.copy("dve", c3[:, 1, :], m3[:, 0, :])
                elif k == 2:
                    B.copy("dve", c3[:, 2, :], m3[:, 2, :])
                elif k == 3:
                    B.stt(c3[:, 3, :], m3[:, 4, :], 1.0, n3[:, 1, :], ALU.add, ALU.mult)
                elif k == 4:
                    B.copy("dve", c3[:, 4, :], m3[:, 3, :])
                elif k == 5:
                    B.copy("dve", c3[:, 5, :], m3[:, 5, :])

        def norm(S, l, kA, kB, final=False):
            for (o, n) in blocks(S):
                ps = nbank()
                for c in range(8):
                    sq = sqb[c % 2]
                    B.act(sq[:, :n], S.H[:, c, o:o + n], AF.Square)
                    B.mm(ps[:, :n], ones_rms[:], sq[:, :n], c == 0, c == 7)
                B.act(rstd_b[:, :n], ps[:, :n], AF.Sqrt, bias=RMS_EPS_AP[0], scale=1.0)
                B.recip(rstd_b[:, :n], rstd_b[:, :n])
                for c in range(8):
                    t = t32[c % 2]
                    B.tt("dve", t[:, :n], S.H[:, c, o:o + n], rstd_b[:, :n], ALU.mult)
                    if final:
                        B.act(S.H[:, c, o:o + n], t[:, :n], AF.Identity, scale=fng[:, c:c + 1])
                        if c == 7:
                            B.dma("sp", outT.rearrange("(c p) t -> p c t", p=128)[:, :, o:o + n], S.H[:, :, o:o + n],
                                  reads=[S.H[:, cc, o:o + n] for cc in range(8)])
                    else:
                        B.act(S.hn[:, c, o:o + n], t[:, :n], AF.Identity, bias=coef(l, kB, c, S.col), scale=coef(l, kA, c, S.col))

        eps_t = sb("epsS", 2, F32)
        B.memset("pool", eps_t[:, 0:1], RMS_EPS)
        B.memset("pool", eps_t[:, 1:2], LN_EPS)
        RMS_EPS_AP = [eps_t[:, 0:1]]
        LN_EPS_AP = eps_t[:, 1:2]

        evac_rr = [0]

        def evac(out, in_):
            eng = ("act", "dve")[evac_rr[0] % 2]
            evac_rr[0] += 1
            B.copy(eng, out, in_)

        def proj_cm(S, wv, col0, evac_fn, nblk=512):
            for (o, n) in blocks(S, nblk):
                ps = nbank()
                for dc in range(8):
                    B.mm(ps[:, :n], wv[:, dc, col0:col0 + 128], S.hn[:, dc, o:o + n], dc == 0, dc == 7)
                evac_fn(ps[:, :n], o, n)

        def attention(l, last, bg=()):
            QTA = av(SCR + 0, T)
            QTB = av(SCR + 4, T)
            KT = av(SCR + 8, T)
            Vp = av(SCR + 12, 16 * 256).rearrange("p (t x) -> p t x", t=16)
            KcT = av(SCR + 20, CT)
            Vc = av(SCR + 20.5, 2 * 256).rearrange("p (t x) -> p t x", t=2)
            QcA = av(48, CT)
            QcB = av(48.5, CT)
            rec = [av(52, 256).bitcast(F32), av(52.5, 256).bitcast(F32)]
            Eb = [av(0, NBLK * 128), av(5.25, NBLK * 128)]
            PT = [av(10.5 + 1.75 * i, 896) for i in range(3)]
            B.memset("pool", Vp[:, :, 64:128], 1.0)
            B.memset("pool", Vc[:, :, 64:128], 1.0)
            B.memset("pool", QTA[64:128, :], 0.0)
            B.memset("pool", QTB[0:64, :], 0.0)
            B.memset("pool", QcA[64:128, :], 0.0)
            B.memset("pool", QcB[0:64, :], 0.0)
            Sps = [pp[0], pp[1], pp[2]]
            Obank = pp[3][:, 0:512]
            Mbanks = [pp[0][:, 0:512], pp[0][:, 512:1024], pp[1][:, 0:512], pp[1][:, 512:1024],
                      pp[2][:, 0:512], pp[2][:, 512:1024], pp[3][:, 512:1024]]
            mctr = [0]

            def nM():
                b = Mbanks[mctr[0] % len(Mbanks)]
                mctr[0] += 1
                return b
            uctr = [0]
            bgq = list(bg)

            for p in range(DBG["pairs"]):
                wv = WB.rearrange("p (c n) -> p c n", c=8)[:, :, 0:384]
                c0 = 768 + p * 384
                B.dma("pool", wv, w_in[l].rearrange("(c p) n -> p c n", p=128)[:, :, c0:c0 + 384], writes=[WB])
                for hh in range(2):
                    B.dma("pool", Eb[hh], bias_d[l, 2 * p + hh], writes=[Eb[hh]])
                    B.act(Eb[hh], Eb[hh], AF.Exp)

                def ev_kc(ps, o, n):
                    evac(KcT[:, o:o + n], ps)
                proj_cm_fixed(CTX, wv, 128, ev_kc, nM)
                if not last:
                    def ev_qc(ps, o, n):
                        B.copy("act", QcA[0:64, o:o + n], ps[0:64, :])
                        B.copy("dve", QcB[64:128, o:o + n], ps[64:128, :])
                    proj_cm_fixed(CTX, wv, 0, ev_qc, nM)
                for t in range(2):
                    mb = nM()
                    for dc in range(8):
                        B.mm(mb[:, 0:128], hnc[:, dc, t * 128:(t + 1) * 128], wv[:, dc, 256:384], dc == 0, dc == 7)
                    B.copy("dve", Vc[:, t, :].rearrange("p (a b) -> p a b", a=2)[:, :, 0:64],
                           mb[:, 0:128].rearrange("p (a b) -> p a b", a=2))

                def ev_q(ps, o, n):
                    B.copy("act", QTA[0:64, o:o + n], ps[0:64, :])
                    B.copy("dve", QTB[64:128, o:o + n], ps[64:128, :])

                def ev_k(ps, o, n):
                    evac(KT[:, o:o + n], ps)
                proj_cm_fixed(LAT, wv, 0, ev_q, nM)
                proj_cm_fixed(LAT, wv, 128, ev_k, nM)
                for t4 in range(4):
                    mb = nM()
                    for tt_ in range(4):
                        t = t4 * 4 + tt_
                        for dc in range(8):
                            B.mm(mb[:, tt_ * 128:(tt_ + 1) * 128], hn[:, dc, t * 128:(t + 1) * 128], wv[:, dc, 256:384], dc == 0, dc == 7)
                    for tt_ in range(4):
                        t = t4 * 4 + tt_
                        B.copy(("dve", "act")[tt_ % 2], Vp[:, t, :].rearrange("p (a b) -> p a b", a=2)[:, :, 0:64],
                               mb[:, tt_ * 128:(tt_ + 1) * 128].rearrange("p (a b) -> p a b", a=2))

                units = []
                if not last:
                    for hh in range(2):
                        for m in range(2):
                            units.append(("ctx", hh, m))
                for hh in range(2):
                    for m in range(16):
                        units.append(("lat", hh, m))
                if DBG["units"] is not None:
                    units = units[:DBG["units"]]

                def unit_S(u):
                    kind, hh, m = units[u]
                    k = uctr[0] + u
                    Sp = Sps[k % 3]
                    if kind == "lat":
                        chunks, _ = tile_chunks(m)
                        q = (QTA, QTB)[hh][:, m * 128:(m + 1) * 128]
                        for i, j in enumerate(chunks):
                            B.mm(Sp[:, i * 128:(i + 1) * 128], KT[:, j * 128:(j + 1) * 128], q, True, True)
                        nl = len(chunks)
                    else:
                        q = (QcA, QcB)[hh][:, m * 128:(m + 1) * 128]
                        nl = 0
                    for cc in range(2):
                        B.mm(Sp[:, (nl + cc) * 128:(nl + cc + 1) * 128], KcT[:, cc * 128:(cc + 1) * 128], q, True, True)

                def unit_rest(u):
                    kind, hh, m = units[u]
                    k = uctr[0] + u
                    Sp = Sps[k % 3]
                    P = PT[k % 3]
                    if kind == "lat":
                        chunks, typ = tile_chunks(m)
                    else:
                        chunks, typ = [], None
                    nl = len(chunks)
                    ns = nl + 2
                    B.act(P[:, 0:ns * 128], Sp[:, 0:ns * 128], AF.Exp, scale=0.125)
                    if nl:
                        eo = EOFF[typ] * 128
                        B.tt("dve", P[:, 0:nl * 128], P[:, 0:nl * 128], Eb[hh][:, eo:eo + nl * 128], ALU.mult)
                    if DBG["parts"] < 3:
                        return
                    Op = Obank[:, (k % 4) * 128:(k % 4 + 1) * 128]
                    vo = 64 * hh
                    for i, j in enumerate(chunks):
                        B.mm(Op, Vp[:, j, vo:vo + 128], P[:, i * 128:(i + 1) * 128], i == 0, False)
                    for cc in range(2):
                        B.mm(Op, Vc[:, cc, vo:vo + 128], P[:, (nl + cc) * 128:(nl + cc + 1) * 128], (nl == 0 and cc == 0), cc == 1)
                    if DBG["parts"] < 4:
                        B.copy("act", rec[0][:, :], Op)
                        return
                    r = rec[k % 2]
                    olo, ohi = (0, 64) if hh == 0 else (64, 128)
                    dlo, dhi = (64, 128) if hh == 0 else (0, 64)
                    B.recip(r[dlo:dhi, :], Op[dlo:dhi, :])
                    dst = (mix if kind == "lat" else mixc)[olo:ohi, 4 + p, m * 128:(m + 1) * 128]
                    B.tt("dve", dst, Op[olo:ohi, :], r[dlo:dhi, :], ALU.mult)

                LOOK = 2
                for u in range(min(LOOK, len(units))):
                    unit_S(u)
                for u in range(len(units)):
                    if u + LOOK < len(units):
                        unit_S(u + LOOK)
                    unit_rest(u)
                    if bgq and (uctr[0] + u) % 6 == 5:
                        bgq.pop(0)(WA, pp[3][:, 512:1024])
                uctr[0] += len(units)
            while bgq:
                bgq.pop(0)(WA, pp[3][:, 512:1024])

        def proj_cm_fixed(S, wv, col0, evac_fn, nb):
            for (o, n) in blocks(S):
                psb = nb()
                for dc in range(8):
                    B.mm(psb[:, :n], wv[:, dc, col0:col0 + 128], S.hn[:, dc, o:o + n], dc == 0, dc == 7)
                evac_fn(psb[:, :n], o, n)

        def conv_phase(l, streams):
            wv = WA.rearrange("p (c n) -> p c n", c=8)
            B.dma("pool", wv, w_in[l].rearrange("(c p) n -> p c n", p=128)[:, :, 256:768], writes=[WA])
            B.dma("pool", pw_t[:].rearrange("p (c n) -> p c n", c=2), w_pw[l].rearrange("(c p) n -> p c n", p=128), writes=[pw_t[:]])
            pwv = pw_t[:].rearrange("p (c n) -> p c n", c=2)
            vpad = av(SCR, 2 * (T + 30)).rearrange("p (c t) -> p c t", c=2)
            vpadc = av(0, 2 * (CT + 30)).rearrange("p (c t) -> p c t", c=2)
            doff = SCR + (2 * (T + 30) * 2) / 1024.0
            doff = math.ceil(doff * 16) / 16.0
            Dg = av(doff, 62 * 128).rearrange("p (c k n) -> p c k n", c=2, k=31)
            cpl = cps[:, l * 70:(l + 1) * 70].rearrange("p (c k) -> p c k", c=2)
            sig = av(2, 512)
            ybf = [av(3, 512), av(4, 512)]
            ysq = [av(5, 512), av(6, 512)]
            y32 = [t32[0], t32[1]]
            z32 = rstd_b
            sil = [sqb[0], sqb[1]]
            for ch in range(2):
                for k in range(31):
                    B.ts("dve", Dg[:, ch, k, :], ident[:], cpl[:, ch, k:k + 1], ALU.mult)
            for S in streams:
                vp = vpad if S is LAT else vpadc
                B.memset("pool", vp[:, :, 0:15], 0.0)
                B.memset("pool", vp[:, :, 15 + S.T:30 + S.T], 0.0)
                for (o, n) in blocks(S):
                    for ch in range(2):
                        pa = nbank()
                        pg = nbank()
                        for dc in range(8):
                            B.mm(pa[:, :n], wv[:, dc, ch * 128:(ch + 1) * 128], S.hn[:, dc, o:o + n], dc == 0, dc == 7)
                        for dc in range(8):
                            B.mm(pg[:, :n], wv[:, dc, 256 + ch * 128:256 + (ch + 1) * 128], S.hn[:, dc, o:o + n], dc == 0, dc == 7)
                        B.act(sig[:, :n], pg[:, :n], AF.Sigmoid)
                        B.tt("dve", vp[:, ch, 15 + o:15 + o + n], pa[:, :n], sig[:, :n], ALU.mult)
            tsets = [dict(ybf=ybf, ysq=ysq, y32=y32),
                     dict(ybf=[av(40, 512), av(41, 512)], ysq=[av(42, 512), av(43, 512)],
                          y32=[av(44, 1024).bitcast(F32), av(46, 1024).bitcast(F32)])]
            work = []
            for S in streams:
                vp = vpad if S is LAT else vpadc
                for (o, n) in blocks(S):
                    work.append((S, vp, o, n))

            def stA(i):
                S, vp, o, n = work[i]
                ts_ = tsets[i % 2]
                for ch in range(2):
                    pc = nbank()
                    for k in range(31):
                        B.mm(pc[:, :n], Dg[:, ch, k, :], vp[:, ch, o + k:o + k + n], k == 0, k == 30)
                    B.act(ts_["ybf"][ch][:, :n], pc[:, :n], AF.Identity, bias=cpl[:, ch, 31:32], scale=1.0)
                    B.act(ts_["ysq"][ch][:, :n], pc[:, :n], AF.Square, bias=cpl[:, ch, 31:32], scale=1.0)
                    B.ts("dve", ts_["y32"][ch][:, :n], pc[:, :n], cpl[:, ch, 31:32], ALU.add)

            def stB(i):
                S, vp, o, n = work[i]
                ts_ = tsets[i % 2]
                pm = nbank()
                pq = nbank()
                for ch in range(2):
                    B.mm(pm[:, :n], ones_ln[:], ts_["ybf"][ch][:, :n], ch == 0, ch == 1)
                for ch in range(2):
                    B.mm(pq[:, :n], ones_ln[:], ts_["ysq"][ch][:, :n], ch == 0, ch == 1)
                B.act(z32[:, :n], pm[:, :n], AF.Square)
                B.tt("dve", z32[:, :n], pq[:, :n], z32[:, :n], ALU.subtract)
                B.act(z32[:, :n], z32[:, :n], AF.Sqrt, bias=LN_EPS_AP, scale=1.0)
                B.recip(z32[:, :n], z32[:, :n])
                for ch in range(2):
                    yv = ts_["y32"][ch]
                    B.tt("dve", yv[:, :n], yv[:, :n], pm[:, :n], ALU.subtract)
                    B.tt("dve", yv[:, :n], yv[:, :n], z32[:, :n], ALU.mult)
                    B.act(sil[ch][:, :n], yv[:, :n], AF.Silu, bias=cpl[:, ch, 33:34], scale=cpl[:, ch, 32:33])

            def stC(i):
                S, vp, o, n = work[i]
                for oc in range(2):
                    po = nbank()
                    for ch in range(2):
                        B.mm(po[:, :n], pwv[:, ch, oc * 128:(oc + 1) * 128], sil[ch][:, :n], ch == 0, ch == 1)
                    B.act(S.mix[:, 2 + oc, o:o + n], po[:, :n], AF.Identity, bias=cpl[:, oc, 34:35], scale=1.0)

            stA(0)
            for i in range(len(work)):
                stB(i)
                if i + 1 < len(work):
                    stA(i + 1)
                stC(i)

        def fourier_phase(l, streams):
            ftv = WB[:, 0:2048].rearrange("p (c n) -> p c n", c=2)
            B.dma("pool", ftv, w_inFT[l].rearrange("(c p) n -> p c n", p=128), writes=[WB[:, 0:2048]])
            B.dma("pool", wf_t[:].rearrange("p (c n) -> p c n", c=2), w_f[l].rearrange("(c p) n -> p c n", p=128), writes=[wf_t[:]])
            wfv = wf_t[:].rearrange("p (c n) -> p c n", c=2)
            csv = csblk[:].rearrange("p (c n) -> p c n", c=2)
            Wp = WA.rearrange("p (c n) -> p c n", c=8)
            for dc in range(8):
                ps = nbank()
                for cc in range(2):
                    B.mm(ps, ftv[:, cc, dc * 128:(dc + 1) * 128], csv[:, cc, :], cc == 0, cc == 1)
                evac(Wp[:, dc, :], ps)
            UU = av(SCR, 16 * 512).rearrange("p (t n) -> p t n", t=16)
            UUc = av(SCR + 16, 2 * 512).rearrange("p (t n) -> p t n", t=2)
            FT = av(SCR + 18, 2 * 256).rearrange("p (c n) -> p c n", c=2)
            for S in streams:
                U = UU if S is LAT else UUc
                nt = S.T // 128
                for t in range(nt):
                    ps = nbank()
                    for dc in range(8):
                        B.mm(ps, S.hn[:, dc, t * 128:(t + 1) * 128], Wp[:, dc, :], dc == 0, dc == 7)
                    evac(U[:, t, :], ps)
            for S in streams:
                U = UU if S is LAT else UUc
                nt = S.T // 128
                njb = S.T // 256
                for jb in range(njb):
                    if S is LAT:
                        dbuf = hn_t[:, (jb % 2) * 8192:(jb % 2 + 1) * 8192]
                        B.dma("sp", dbuf, dft_d[jb], writes=[dbuf])
                        dv = dbuf.rearrange("p (k s j) -> p k s j", k=16, s=2)
                    else:
                        dv = dft256[:].rearrange("p (k s j) -> p k s j", k=2, s=2)
                    for cg in range(2):
                        ps = nbank()
                        first = True
                        for kc in range(nt):
                            for s in range(2):
                                B.mm(ps[:, 0:256], U[:, kc, s * 256 + cg * 128:s * 256 + (cg + 1) * 128], dv[:, kc, s, :],
                                     first, (kc == nt - 1 and s == 1))
                                first = False
                        evac(FT[:, cg, :], ps[:, 0:256])
                    for oc in range(2):
                        ps = nbank()
                        for cg in range(2):
                            B.mm(ps[:, 0:256], wfv[:, cg, oc * 128:(oc + 1) * 128], FT[:, cg, :], cg == 0, cg == 1)
                        evac(S.mix[:, oc, jb * 256:(jb + 1) * 256], ps[:, 0:256])

        def outproj(l, streams):
            wo = AR[:, 32 * 512:48 * 512].rearrange("p (c n) -> p c n", c=8)
            B.dma("pool", wo[:, 0:4, :], w_out[l].rearrange("(c p) n -> p c n", p=128)[:, 0:4, :], writes=[WA])
            B.dma("pool", wo[:, 4:8, :], w_out[l].rearrange("(c p) n -> p c n", p=128)[:, 4:8, :], writes=[WB])
            for S in streams:
                for (o, n) in blocks(S):
                    for oc in range(8):
                        ps = nbank()
                        for kc in range(8):
                            B.mm(ps[:, :n], wo[:, kc, oc * 128:(oc + 1) * 128], S.mix[:, kc, o:o + n], kc == 0, kc == 7)
                        B.stt(S.H[:, oc, o:o + n], ps[:, :n], coef(l, 2, oc, S.col), S.H[:, oc, o:o + n], ALU.mult, ALU.add)

        def mlp(l, streams, after_first_norm):
            slot_off = [8, 16, 24, 32, 40, 56, 64, 72]
            sets = [[5, 6, 7, 0], [1, 2, 3, 4]]
            hid = [av(0, 2048).rearrange("p (f t) -> p f t", f=8), av(4, 2048).rearrange("p (f t) -> p f t", f=8)]
            rl = [sqb[0][:, 0:256], sqb[1][:, 0:256]]
            hctr = 0
            for fb in range(4):
                st = sets[fb % 2]
                w1s = [av(slot_off[st[0]], 4096).rearrange("p (c n) -> p c n", c=8),
                       av(slot_off[st[1]], 4096).rearrange("p (c n) -> p c n", c=8)]
                w2s = [av(slot_off[st[2]], 4096).rearrange("p (c n) -> p c n", c=4),
                       av(slot_off[st[3]], 4096).rearrange("p (c n) -> p c n", c=4)]
                for hhalf in range(2):
                    c0 = fb * 1024 + hhalf * 512
                    B.dma("pool", w1s[hhalf], w1[l].rearrange("(c p) n -> p c n", p=128)[:, :, c0:c0 + 512],
                          writes=[av(slot_off[st[hhalf]], 4096)])
                for hhalf in range(2):
                    r0 = fb * 8 + hhalf * 4
                    B.dma("pool", w2s[hhalf], w2[l].rearrange("(c p) n -> p c n", p=128)[:, r0:r0 + 4, :],
                          writes=[av(slot_off[st[2 + hhalf]], 4096)])
                if fb == 0 and after_first_norm is not None:
                    after_first_norm()
                for S in streams:
                    for (o, n) in blocks(S, 256):
                        hb = hid[hctr % 2]
                        hctr += 1
                        for fc in range(8):
                            ps = nbank()
                            wv = w1s[fc // 4]
                            cc = (fc % 4) * 128
                            for dc in range(8):
                                B.mm(ps[:, :n], wv[:, dc, cc:cc + 128], S.hn[:, dc, o:o + n], dc == 0, dc == 7)
                            r = rl[fc % 2]
                            B.act(r[:, :n], ps[:, :n], AF.Relu)
                            B.tt("dve", hb[:, fc, :n], r[:, :n], r[:, :n], ALU.mult)
                        for oc in range(8):
                            ps = nbank()
                            for fc in range(8):
                                B.mm(ps[:, :n], w2s[fc // 4][:, fc % 4, oc * 128:(oc + 1) * 128], hb[:, fc, :n], fc == 0, fc == 7)
                            B.stt(S.H[:, oc, o:o + n], ps[:, :n], coef(l, 5, oc, S.col), S.H[:, oc, o:o + n], ALU.mult, ALU.add)

        def dump(name, ap_sb, shape):
            d = nc.dram_tensor("dbg_" + name, list(shape), ap_sb.dtype, kind="ExternalOutput").ap()
            B.dma("sp", d, ap_sb, reads=[ap_sb])
            dbg_out[name] = True

        done = False
        for l in range(DEPTH):
            last = l == DEPTH - 1
            streams = [LAT] if last else [CTX, LAT]
            bg = []
            if l == 0:
                for sidx in range(4):
                    ada_slice(0, sidx, (WA, WB)[sidx % 2], nbank())
                ada_coefs(0, [0, 1])
                for sidx in range(4, 12):
                    bg.append(lambda slot, ps, sidx=sidx: ada_slice(0, sidx, slot, ps))
                bg.append(lambda slot, ps: ada_coefs(0, [2, 3, 4, 5]))
                for sidx in range(12):
                    bg.append(lambda slot, ps, sidx=sidx: ada_slice(1, sidx, slot, ps))
                bg.append(lambda slot, ps: ada_coefs(1, [0, 1, 2, 3, 4, 5]))
            if stop_after == f"ada{l}":
                dump("mod", mod_t[:], [128, DEPTH * 96])
                dump("coef", coef_t[:], [128, DEPTH * 96])
                done = True
                break
            norm(CTX, l, 0, 1)
            norm(LAT, l, 0, 1)
            if stop_after == f"norm{l}":
                dump("hn", hn_t[:], [128, 8 * T])
                dump("hnc", hnc_t[:], [128, 8 * CT])
                done = True
                break
            attention(l, last, bg)
            if stop_after == f"attn{l}":
                dump("mix", AR[:, 0:8 * T], [128, 8 * T])
                dump("mixc", mixc_t[:], [128, 8 * CT])
                done = True
                break
            conv_phase(l, streams)
            if stop_after == f"conv{l}":
                dump("mix", AR[:, 0:8 * T], [128, 8 * T])
                dump("mixc", mixc_t[:], [128, 8 * CT])
                done = True
                break
            fourier_phase(l, streams)
            if stop_after == f"four{l}":
                dump("mix", AR[:, 0:8 * T], [128, 8 * T])
                dump("mixc", mixc_t[:], [128, 8 * CT])
                done = True
                break
            outproj(l, streams)
            if stop_after == f"oproj{l}":
                dump("H", H_t[:], [128, 8 * T])
                dump("Hc", Hc_t[:], [128, 8 * CT])
                done = True
                break
            for S in streams:
                norm(S, l, 3, 4)
            mlp(l, streams, None)
            if stop_after == f"mlp{l}":
                dump("H", H_t[:], [128, 8 * T])
                dump("Hc", Hc_t[:], [128, 8 * CT])
                done = True
                break
        if not done:
            norm(LAT, 0, 0, 0, final=True)
        else:
            for c in range(8):
                B.dma("sp", outT.rearrange("(c p) t -> p c t", p=128)[:, c, :], H[:, c, :], reads=[H[:, c, :]])
        B.emit(es)
    _CACHE['trace'] = B.trace
    return nc, sorted(dbg_out.keys()), len(B.ops)


def _bias_tables(rpb):
    kc = np.arange(64)
    qc = np.arange(64)
    cs = np.clip(qc - 8, 0, 48)
    colvalid = (kc[:, None] >= cs[None, :]) & (kc[:, None] < cs[None, :] + 16)
    coloff = np.clip(kc[:, None] - qc[None, :] + 15, 0, 30)
    blocks = []
    specs = [(5, j) for j in range(3, 8)] + [(0, j) for j in range(4)] + [(1, j) for j in range(4)] + \
            [(14, j) for j in range(12, 16)] + [(15, j) for j in range(12, 16)]
    out = np.full((DEPTH, 8, 128, NBLK, 128), NEG, np.float32)
    for bi, (m, j) in enumerate(specs):
        for a in range(2):
            for b in range(2):
                krow = 2 * j + a
                qrow = 2 * m + b
                rs = min(max(qrow - 4, 0), 24)
                if not (rs <= krow < rs + 8):
                    continue
                dr = krow - qrow + 7
                vals = rpb[:, :, dr, :][:, :, coloff]
                vals = np.where(colvalid[None, None], vals, np.float32(NEG))
                out[:, :, a * 64:(a + 1) * 64, bi, b * 64:(b + 1) * 64] = vals
    return np.ascontiguousarray(out.reshape(DEPTH, 8, 128, NBLK * 128))


def _dft_tables():
    k = np.arange(T, dtype=np.int64)
    ang = 2.0 * np.pi * ((k[:, None] * k[None, :]) % T).astype(np.float64) / T
    Cm = (np.cos(ang) / math.sqrt(T)).astype(np.float32)
    Sm = (-np.sin(ang) / math.sqrt(T)).astype(np.float32)
    CS = np.stack([Cm, Sm], axis=0)
    CS = CS.reshape(2, 16, 128, 8, 256)
    dft = np.ascontiguousarray(CS.transpose(3, 2, 1, 0, 4)).reshape(8, 128, 16 * 2 * 256).astype(NPBF)
    k2 = np.arange(CT, dtype=np.int64)
    ang2 = 2.0 * np.pi * ((k2[:, None] * k2[None, :]) % CT).astype(np.float64) / CT
    C2 = (np.cos(ang2) / math.sqrt(CT)).astype(np.float32)
    S2 = (-np.sin(ang2) / math.sqrt(CT)).astype(np.float32)
    CS2 = np.stack([C2, S2], axis=0).reshape(2, 2, 128, 256)
    dft256 = np.ascontiguousarray(CS2.transpose(2, 1, 0, 3)).reshape(128, 2 * 2 * 256).astype(NPBF)
    a = np.arange(64)
    ang3 = 2.0 * np.pi * ((a[:, None] * a[None, :]) % 64) / 64.0
    cb = np.zeros((256, 256), np.float32)
    sbk = np.zeros((256, 256), np.float32)
    for g in range(4):
        cb[g * 64:(g + 1) * 64, g * 64:(g + 1) * 64] = np.cos(ang3) / 8.0
        sbk[g * 64:(g + 1) * 64, g * 64:(g + 1) * 64] = np.sin(ang3) / 8.0
    csblk = np.ascontiguousarray(np.concatenate([cb, sbk], axis=1)).astype(np.float32)
    return dft, dft256, csblk


def host_prep(inp):
    f = lambda a: np.ascontiguousarray(np.asarray(a, dtype=np.float32))
    x, c, ctx, c_ctx = f(inp["x"]), f(inp["c"]), f(inp["ctx"]), f(inp["c_ctx"])
    shared = {}
    shared["ada_w"] = f(inp["ada_w"])
    ab = f(inp["ada_b"]).reshape(DEPTH, 48, 128).transpose(2, 0, 1)
    shared["adab"] = np.ascontiguousarray(np.repeat(ab[:, :, :, None], 2, axis=3)).reshape(128, DEPTH * 96)
    n1 = f(inp["norm1_g"]).reshape(DEPTH, 8, 128).transpose(2, 0, 1)
    n2 = f(inp["norm2_g"]).reshape(DEPTH, 8, 128).transpose(2, 0, 1)
    ng = np.stack([n1, n2], axis=2)
    shared["ng"] = np.ascontiguousarray(np.repeat(ng[..., None], 2, axis=4)).reshape(128, DEPTH * 32)
    shared["fng"] = np.ascontiguousarray(f(inp["final_norm_g"]).reshape(8, 128).T)
    dw = f(inp["conv_dw_w"]).reshape(DEPTH, 31, 2, 128).transpose(3, 0, 2, 1)
    def pv(name):
        return f(inp[name]).reshape(DEPTH, 2, 128).transpose(2, 0, 1)[..., None]
    cp = np.concatenate([dw, pv("conv_dw_b"), pv("conv_norm_g"), pv("conv_norm_b"), pv("conv_pw_b")], axis=3)
    shared["cp"] = np.ascontiguousarray(cp).reshape(128, DEPTH * 70)
    w_in = f(inp["w_in"])
    order = list(range(0, 768))
    for p in range(4):
        for part in range(3):
            base = 768 + part * 512 + p * 128
            order += list(range(base, base + 128))
    shared["w_in"] = np.ascontiguousarray(w_in[:, :, order])
    shared["w_inFT"] = np.ascontiguousarray(w_in[:, :, :256].transpose(0, 2, 1))
    shared["w_f"] = f(inp["w_fourier"])
    shared["w_pw"] = f(inp["conv_pw_w"])
    shared["w_out"] = f(inp["w_out"])
    shared["w1"] = f(inp["mlp_w1"])
    shared["w2"] = f(inp["mlp_w2"])
    dft, dft256, csblk = _dft_tables()
    shared["dft"] = dft
    shared["dft256"] = dft256
    shared["csblk"] = csblk
    shared["ident"] = np.eye(128, dtype=np.float32).astype(NPBF)
    shared["biasT"] = _bias_tables(f(inp["na_rpb"]))
    maps = []
    cc = c_ctx.reshape(8, 128).T
    for b in range(NCORES):
        m = dict(shared)
        m["xT"] = np.ascontiguousarray(x[b].T)
        m["cxT"] = np.ascontiguousarray(ctx[b].T)
        cb = c[b].reshape(8, 128).T
        m["cvec"] = np.ascontiguousarray(np.stack([cb, cc], axis=2)).reshape(128, 16)
        maps.append(m)
    return maps


def kernel(**inputs):
    maps = host_prep(inputs)
    if "nc" not in _CACHE:
        _CACHE["nc"] = build_program()[0]
    nc = _CACHE["nc"]
    res = run_bass_kernel_spmd(nc, maps, core_ids=list(range(NCORES)))
    out = np.stack([np.ascontiguousarray(res.results[b]["outT"].T) for b in range(NCORES)], axis=0)
    return out.astype(np.float32)
```

```python
import math
from contextlib import ExitStack

import numpy as np
import ml_dtypes
import concourse.bass as bass
import concourse.mybir as mybir
from concourse.bass_utils import run_bass_kernel_spmd

F32 = mybir.dt.float32
BF16 = mybir.dt.bfloat16
AF = mybir.ActivationFunctionType
ALU = mybir.AluOpType
NPBF = ml_dtypes.bfloat16

D = 1024
T = 2048
CT = 256
DEPTH = 2
DIN = 2304
DFF = 4096
NCORES = 8
NEG = -30000.0
NBLK = 21
EOFF = {"int": 0, 0: 5, 1: 9, 14: 13, 15: 17}
RMS_EPS = 1e-6
LN_EPS = 1e-5
CELL = 512
_CACHE = {}
DBG = {"pairs": 4, "units": None, "parts": 5, "pv": 0}


class Builder:
    def __init__(self, nc):
        self.nc = nc
        self.ops = []
        self.cw = {}
        self.cr = {}

    @staticmethod
    def _cells(ap):
        sz = 4 if ap.dtype == F32 else 2
        pairs = ap.ap
        pstride, pcount = pairs[0]
        off = ap.offset
        p0 = off // pstride
        f0 = off % pstride
        ext = 0
        for step, cnt in pairs[1:]:
            ext += (cnt - 1) * step
        lo = f0 * sz
        hi = (f0 + ext + 1) * sz
        name = ap.name
        cell = 2048 if name.startswith("pp") else CELL
        out = []
        for q in range(p0 // 32, (p0 + pcount - 1) // 32 + 1):
            for ci in range(lo // cell, (hi - 1) // cell + 1):
                out.append((name, q, ci))
        return out

    def add(self, eng, fn, reads=(), writes=(), dma=False):
        oid = len(self.ops)
        deps = set()
        rc = [c for ap in reads for c in self._cells(ap)]
        wc = [c for ap in writes for c in self._cells(ap)]
        wc += [c for c in rc if c[0].startswith("pp")]
        for c in rc:
            w = self.cw.get(c)
            if w is not None:
                deps.add(w)
        for c in wc:
            w = self.cw.get(c)
            if w is not None:
                deps.add(w)
            r = self.cr.get(c)
            if r:
                deps.update(r.values())
        key = ("d", oid) if dma else eng
        for c in wc:
            self.cw[c] = oid
            self.cr[c] = {}
        for c in rc:
            self.cr.setdefault(c, {})[key] = oid
        self.ops.append(dict(eng=eng, fn=fn, deps=deps, dma=dma, sig=False, cnt=0))
        return oid

    def mm(self, out, lhsT, rhs, start, stop):
        self.add("pe", lambda e: e.matmul(out, lhsT, rhs, start=start, stop=stop), reads=[lhsT, rhs], writes=[out])

    def act(self, out, in_, func, bias=None, scale=None):
        reads = [in_]
        kw = {}
        if bias is not None:
            kw["bias"] = bias
            if not isinstance(bias, float):
                reads.append(bias)
        if scale is not None:
            kw["scale"] = scale
            if not isinstance(scale, float):
                reads.append(scale)
        self.add("act", lambda e: e.activation(out=out, in_=in_, func=func, **kw), reads=reads, writes=[out])

    def tt(self, eng, out, in0, in1, op):
        self.add(eng, lambda e: e.tensor_tensor(out=out, in0=in0, in1=in1, op=op), reads=[in0, in1], writes=[out])

    def ts(self, eng, out, in0, s1, op0, s2=None, op1=None):
        reads = [in0]
        if not isinstance(s1, float):
            reads.append(s1)
        if s2 is not None and not isinstance(s2, float):
            reads.append(s2)
        if op1 is None:
            self.add(eng, lambda e: e.tensor_scalar(out=out, in0=in0, scalar1=s1, scalar2=None, op0=op0), reads=reads, writes=[out])
        else:
            self.add(eng, lambda e: e.tensor_scalar(out=out, in0=in0, scalar1=s1, scalar2=s2, op0=op0, op1=op1), reads=reads, writes=[out])

    def stt(self, out, in0, scalar, in1, op0, op1):
        reads = [in0, in1]
        if not isinstance(scalar, float):
            reads.append(scalar)
        self.add("dve", lambda e: e.scalar_tensor_tensor(out=out, in0=in0, scalar=scalar, in1=in1, op0=op0, op1=op1),
                 reads=reads, writes=[out])

    def copy(self, eng, out, in_):
        if eng == "act":
            self.add("act", lambda e: e.copy(out=out, in_=in_), reads=[in_], writes=[out])
        else:
            self.add(eng, lambda e: e.tensor_copy(out=out, in_=in_), reads=[in_], writes=[out])

    def recip(self, out, in_):
        self.add("dve", lambda e: e.reciprocal(out=out, in_=in_), reads=[in_], writes=[out])

    def memset(self, eng, ap, val):
        self.add(eng, lambda e: e.memset(ap, val), writes=[ap])

    def dma(self, q, out, in_, reads=(), writes=()):
        self.add(q, lambda e: e.dma_start(out=out, in_=in_), reads=reads, writes=writes, dma=True)

    def emit(self, es):
        nc = self.nc
        ops = self.ops
        engs = ["pe", "act", "dve", "pool", "sp"]
        for op in ops:
            best = {}
            keep = set()
            for d in op["deps"]:
                a = ops[d]
                if a["dma"]:
                    keep.add(d)
                    continue
                if a["eng"] == "pe" and op["eng"] == "pe" and not op["dma"]:
                    continue
                if d > best.get(a["eng"], -1):
                    best[a["eng"]] = d
            for d in best.values():
                ops[d]["sig"] = True
                keep.add(d)
            op["deps"] = keep
        cnt = {e: 0 for e in engs}
        for op in ops:
            if not op["dma"] and op["sig"]:
                cnt[op["eng"]] += 1
                op["cnt"] = cnt[op["eng"]]
        NDS = 8
        esem = {e: es.enter_context(nc.semaphore("s_" + e)) for e in engs}
        dsem = {q: [es.enter_context(nc.semaphore(f"d_{q}{i}")) for i in range(NDS)] for q in ("sp", "pool", "act")}
        dk = {q: 0 for q in dsem}
        for op in ops:
            if op["dma"]:
                q = op["eng"]
                k = dk[q]
                dk[q] += 1
                op["dsem"] = dsem[q][k % NDS]
                op["dval"] = 16 * (k // NDS + 1)
                op["dprev"] = 16 * (k // NDS)
        streams = {e: [op for op in ops if op["eng"] == e] for e in engs}
        self.trace = []
        semname = {id(v): k for k, v in esem.items()}
        for q in dsem:
            for i, sm in enumerate(dsem[q]):
                semname[id(sm)] = f"d_{q}{i}"

        def run(e, name):
            waited = {}

            def wait(sem, val):
                key = id(sem)
                if waited.get(key, 0) >= val:
                    return
                waited[key] = val
                self.trace.append((name, "wait", semname.get(key, "?"), val))
                e.wait_ge(sem, val)

            for op in streams[name]:
                for d in sorted(op["deps"]):
                    a = ops[d]
                    if a["dma"]:
                        wait(a["dsem"], a["dval"])
                    else:
                        if a["eng"] == "pe" and name == "pe" and not op["dma"]:
                            continue
                        wait(esem[a["eng"]], a["cnt"])
                if op["dma"]:
                    if op["dprev"] > 0:
                        wait(op["dsem"], op["dprev"])
                    ins = op["fn"](e)
                    ins.then_inc(op["dsem"], 16)
                    self.trace.append((name, "dma", semname[id(op["dsem"])], op["dval"]))
                else:
                    ins = op["fn"](e)
                    self.trace.append((name, "op", type(ins).__name__, op["cnt"] if op["sig"] else 0))
                    if op["sig"]:
                        ins.then_inc(esem[name], 1)
            if name in dsem:
                for i in range(NDS):
                    k = dk[name]
                    n_i = (k - i + NDS - 1) // NDS if k > i else 0
                    if n_i > 0:
                        wait(dsem[name][i], 16 * n_i)

        with nc.Block() as block:
            @block.tensor
            def _(e):
                run(e, "pe")

            @block.scalar
            def _(e):
                run(e, "act")

            @block.vector
            def _(e):
                run(e, "dve")

            @block.gpsimd
            def _(e):
                run(e, "pool")

            @block.sync
            def _(e):
                run(e, "sp")


def tile_chunks(m):
    if m <= 1:
        return list(range(0, 4)), m
    if m >= 14:
        return list(range(12, 16)), m
    return list(range(m - 2, m + 3)), "int"


def build_program(stop_after=None, dbg_names=()):
    nc = bass.Bass("TRN2", target_bir_lowering=False)

    def din(name, shape, dt=F32):
        return nc.dram_tensor(name, list(shape), dt, kind="ExternalInput").ap()

    xT = din("xT", [D, T])
    cxT = din("cxT", [D, CT])
    cvec_d = din("cvec", [128, 16])
    ada_w = din("ada_w", [DEPTH, D, 6 * D])
    adab_d = din("adab", [128, DEPTH * 96])
    ng_d = din("ng", [128, DEPTH * 2 * 16])
    fng_d = din("fng", [128, 8])
    cp_d = din("cp", [128, DEPTH * 2 * 35])
    w_in = din("w_in", [DEPTH, D, DIN])
    w_inFT = din("w_inFT", [DEPTH, 256, D])
    w_f = din("w_f", [DEPTH, 256, 256])
    w_pw = din("w_pw", [DEPTH, 256, 256])
    w_out = din("w_out", [DEPTH, D, D])
    w1 = din("w1", [DEPTH, D, DFF])
    w2 = din("w2", [DEPTH, DFF, D])
    csblk_d = din("csblk", [256, 512])
    dft_d = din("dft", [8, 128, 16 * 2 * 256], BF16)
    dft256_d = din("dft256", [128, 2 * 2 * 256], BF16)
    ident_d = din("ident", [128, 128], BF16)
    bias_d = din("biasT", [DEPTH, 8, 128, NBLK * 128])
    outT = nc.dram_tensor("outT", [D, T], F32, kind="ExternalOutput").ap()
    dbg_out = {}

    es = ExitStack()
    with es:
        def sb(name, n, dt):
            return es.enter_context(nc.sbuf_tensor(name, [128, n], dt))

        H_t = sb("H", 8 * T, F32)
        Hc_t = sb("Hc", 8 * CT, F32)
        hn_t = sb("hn", 8 * T, BF16)
        hnc_t = sb("hnc", 8 * CT, BF16)
        mixc_t = sb("mixc", 8 * CT, BF16)
        AR = sb("arena", 40960, BF16)
        ident = sb("identS", 128, BF16)
        ones_rms = sb("ones_rms", 128, BF16)
        ones_ln = sb("ones_ln", 128, BF16)
        csblk = sb("csblkS", 2 * 512, BF16)
        dft256 = sb("dft256S", 2 * 2 * 256, BF16)
        wf_t = sb("wfS", 2 * 256, BF16)
        pw_t = sb("pwS", 2 * 256, BF16)
        cvec = sb("cvecS", 16, F32)
        csil = sb("csil", 16, BF16)
        adab = sb("adabS", DEPTH * 96, F32)
        ngs = sb("ngS", DEPTH * 32, F32)
        fng = sb("fngS", 8, F32)
        cps = sb("cpS", DEPTH * 70, F32)
        mod_t = sb("mod", DEPTH * 96, F32)
        coef_t = sb("coef", DEPTH * 96, F32)
        pp = [es.enter_context(nc.psum_tensor(f"pp{i}", [128, 1024], F32)) for i in range(4)]

        B = Builder(nc)

        def bank(i):
            i = i % 8
            return pp[i // 2][:, (i % 2) * 512:(i % 2) * 512 + 512]

        bank_ctr = [0]

        def nbank():
            b = bank(bank_ctr[0])
            bank_ctr[0] += 1
            return b

        H = H_t[:].rearrange("p (c t) -> p c t", c=8)
        Hc = Hc_t[:].rearrange("p (c t) -> p c t", c=8)
        hn = hn_t[:].rearrange("p (c t) -> p c t", c=8)
        hnc = hnc_t[:].rearrange("p (c t) -> p c t", c=8)
        mixc = mixc_t[:].rearrange("p (c t) -> p c t", c=8)

        def av(off_kib, n):
            o = int(round(off_kib * 512))
            return AR[:, o:o + n]

        mix = av(0, 8 * T).rearrange("p (c t) -> p c t", c=8)
        WA = av(32, 4096)
        WB = av(40, 4096)
        sqb = [av(48, 512), av(49, 512)]
        rstd_b = av(50, 1024).bitcast(F32)
        t32 = [av(52, 1024).bitcast(F32), av(54, 1024).bitcast(F32)]
        SCR = 56.0

        class Stream:
            pass

        LAT = Stream()
        LAT.T, LAT.H, LAT.hn, LAT.mix, LAT.col, LAT.name = T, H, hn, mix, 0, "lat"
        CTX = Stream()
        CTX.T, CTX.H, CTX.hn, CTX.mix, CTX.col, CTX.name = CT, Hc, hnc, mixc, 1, "ctx"

        def blocks(S, n=512):
            return [(o, min(n, S.T - o)) for o in range(0, S.T, n)]

        def coef(l, k, c, col):
            o = l * 96 + k * 16 + c * 2 + col
            return coef_t[:, o:o + 1]

        B.dma("sp", ident[:], ident_d, writes=[ident[:]])
        B.dma("sp", cvec[:], cvec_d, writes=[cvec[:]])
        B.dma("sp", adab[:], adab_d, writes=[adab[:]])
        B.dma("sp", ngs[:], ng_d, writes=[ngs[:]])
        B.dma("sp", fng[:], fng_d, writes=[fng[:]])
        B.dma("sp", cps[:], cp_d, writes=[cps[:]])
        B.dma("sp", dft256[:], dft256_d, writes=[dft256[:]])
        B.dma("pool", csblk[:].rearrange("p (c n) -> p c n", c=2), csblk_d.rearrange("(c p) n -> p c n", p=128),
              writes=[csblk[:]])
        B.memset("pool", ones_rms[:], 1.0 / D)
        B.memset("pool", ones_ln[:], 1.0 / 256)
        B.dma("sp", Hc, cxT.rearrange("(c p) t -> p c t", p=128), writes=[Hc_t[:]])
        for tb in range(4):
            B.dma("sp", H[:, :, tb * 512:(tb + 1) * 512], xT.rearrange("(c p) t -> p c t", p=128)[:, :, tb * 512:(tb + 1) * 512],
                  writes=[H[:, c, tb * 512:(tb + 1) * 512] for c in range(8)])
        B.act(csil[:], cvec[:], AF.Silu)

        def ada_slice(l, sidx, slot, ps):
            csv = csil[:].rearrange("p (c k) -> p c k", c=8)
            sv = slot.rearrange("p (c n) -> p c n", c=8)
            B.dma("pool", sv, ada_w[l].rearrange("(c p) n -> p c n", p=128)[:, :, sidx * 512:(sidx + 1) * 512], writes=[slot])
            for jj in range(4):
                for dc in range(8):
                    B.mm(ps[:, 2 * jj:2 * jj + 2], sv[:, dc, jj * 128:(jj + 1) * 128], csv[:, dc, :], dc == 0, dc == 7)
            o = l * 96 + sidx * 8
            B.tt("dve", mod_t[:, o:o + 8], ps[:, 0:8], adab[:, o:o + 8], ALU.add)

        def ada_coefs(l, ks):
            m3 = mod_t[:, l * 96:(l + 1) * 96].rearrange("p (k x) -> p k x", k=6)
            c3 = coef_t[:, l * 96:(l + 1) * 96].rearrange("p (k x) -> p k x", k=6)
            n3 = ngs[:, l * 32:(l + 1) * 32].rearrange("p (k x) -> p k x", k=2)
            for k in ks:
                if k == 0:
                    B.stt(c3[:, 0, :], m3[:, 1, :], 1.0, n3[:, 0, :], ALU.add, ALU.mult)
                elif k == 1:
                    B.copy("dve", c3[:, 1, :], m3[:, 0, :])
                elif k == 2:
                    B.copy("dve", c3[:, 2, :], m3[:, 2, :])
                elif k == 3:
                    B.stt(c3[:, 3, :], m3[:, 4, :], 1.0, n3[:, 1, :], ALU.add, ALU.mult)
                elif k == 4:
                    B.copy("dve", c3[:, 4, :], m3[:, 3, :])
                elif k == 5:
                    B.copy("dve", c3[:, 5, :], m3[:, 5, :])

        def norm(S, l, kA, kB, final=False):
            blks = blocks(S)
            pss = {}

            def s1(i):
                o, n = blks[i]
                ps = nbank()
                pss[i] = ps
                for c in range(8):
                    sq = sqb[c % 2]
                    B.act(sq[:, :n], S.H[:, c, o:o + n], AF.Square)
                    B.mm(ps[:, :n], ones_rms[:], sq[:, :n], c == 0, c == 7)

            def s2(i):
                o, n = blks[i]
                B.act(rstd_b[:, :n], pss[i][:, :n], AF.Sqrt, bias=RMS_EPS_AP[0], scale=1.0)
                B.recip(rstd_b[:, :n], rstd_b[:, :n])

            def s3(i):
                o, n = blks[i]
                for c in range(8):
                    t = t32[c % 2]
                    B.tt("dve", t[:, :n], S.H[:, c, o:o + n], rstd_b[:, :n], ALU.mult)
                    if final:
                        if c % 2 == 0:
                            B.act(S.H[:, c, o:o + n], t[:, :n], AF.Identity, scale=fng[:, c:c + 1])
                        else:
                            B.ts("dve", S.H[:, c, o:o + n], t[:, :n], fng[:, c:c + 1], ALU.mult)
                        if c == 7:
                            B.dma("sp", outT.rearrange("(c p) t -> p c t", p=128)[:, :, o:o + n], S.H[:, :, o:o + n],
                                  reads=[S.H[:, cc, o:o + n] for cc in range(8)])
                    else:
                        if c % 2 == 0:
                            B.act(S.hn[:, c, o:o + n], t[:, :n], AF.Identity, bias=coef(l, kB, c, S.col), scale=coef(l, kA, c, S.col))
                        else:
                            B.ts("dve", S.hn[:, c, o:o + n], t[:, :n], coef(l, kA, c, S.col), ALU.mult,
                                 s2=coef(l, kB, c, S.col), op1=ALU.add)

            s1(0)
            for i in range(len(blks)):
                s2(i)
                if i + 1 < len(blks):
                    s1(i + 1)
                s3(i)

        eps_t = sb("epsS", 2, F32)
        B.memset("pool", eps_t[:, 0:1], RMS_EPS)
        B.memset("pool", eps_t[:, 1:2], LN_EPS)
        RMS_EPS_AP = [eps_t[:, 0:1]]
        LN_EPS_AP = eps_t[:, 1:2]

        evac_rr = [0]

        def evac(out, in_):
            eng = ("act", "dve")[evac_rr[0] % 2]
            evac_rr[0] += 1
            B.copy(eng, out, in_)

        def proj_cm(S, wv, col0, evac_fn, nblk=512):
            for (o, n) in blocks(S, nblk):
                ps = nbank()
                for dc in range(8):
                    B.mm(ps[:, :n], wv[:, dc, col0:col0 + 128], S.hn[:, dc, o:o + n], dc == 0, dc == 7)
                evac_fn(ps[:, :n], o, n)

        def attention(l, last, bg=()):
            QTA = av(SCR + 0, T)
            QTB = av(SCR + 4, T)
            KT = av(SCR + 8, T)
            Vp = av(SCR + 12, 16 * 256).rearrange("p (t x) -> p t x", t=16)
            KcT = av(SCR + 20, CT)
            Vc = av(SCR + 20.5, 2 * 256).rearrange("p (t x) -> p t x", t=2)
            QcA = av(48, CT)
            QcB = av(48.5, CT)
            rec = [av(52, 256).bitcast(F32), av(52.5, 256).bitcast(F32)]
            Eb = [av(0, NBLK * 128), av(5.25, NBLK * 128)]
            PT = [av(10.5 + 1.75 * i, 896) for i in range(3)]
            B.memset("pool", Vp[:, :, 64:128], 1.0)
            B.memset("pool", Vc[:, :, 64:128], 1.0)
            B.memset("pool", QTA[64:128, :], 0.0)
            B.memset("pool", QTB[0:64, :], 0.0)
            B.memset("pool", QcA[64:128, :], 0.0)
            B.memset("pool", QcB[0:64, :], 0.0)
            Sps = [pp[0], pp[1], pp[2]]
            Obank = pp[3][:, 0:512]
            Mbanks = [pp[0][:, 0:512], pp[0][:, 512:1024], pp[1][:, 0:512], pp[1][:, 512:1024],
                      pp[2][:, 0:512], pp[2][:, 512:1024], pp[3][:, 512:1024]]
            mctr = [0]

            def nM():
                b = Mbanks[mctr[0] % len(Mbanks)]
                mctr[0] += 1
                return b
            uctr = [0]
            bgq = list(bg)

            for p in range(DBG["pairs"]):
                wv = WB.rearrange("p (c n) -> p c n", c=8)[:, :, 0:384]
                c0 = 768 + p * 384
                B.dma("pool", wv, w_in[l].rearrange("(c p) n -> p c n", p=128)[:, :, c0:c0 + 384], writes=[WB])
                for hh in range(2):
                    B.dma("pool", Eb[hh], bias_d[l, 2 * p + hh], writes=[Eb[hh]])
                    B.act(Eb[hh], Eb[hh], AF.Exp)

                def ev_kc(ps, o, n):
                    evac(KcT[:, o:o + n], ps)
                proj_cm_fixed(CTX, wv, 128, ev_kc, nM)
                if not last:
                    def ev_qc(ps, o, n):
                        B.copy("act", QcA[0:64, o:o + n], ps[0:64, :])
                        B.copy("dve", QcB[64:128, o:o + n], ps[64:128, :])
                    proj_cm_fixed(CTX, wv, 0, ev_qc, nM)
                for t in range(2):
                    mb = nM()
                    for dc in range(8):
                        B.mm(mb[:, 0:128], hnc[:, dc, t * 128:(t + 1) * 128], wv[:, dc, 256:384], dc == 0, dc == 7)
                    B.copy("dve", Vc[:, t, :].rearrange("p (a b) -> p a b", a=2)[:, :, 0:64],
                           mb[:, 0:128].rearrange("p (a b) -> p a b", a=2))

                def ev_q(ps, o, n):
                    B.copy("act", QTA[0:64, o:o + n], ps[0:64, :])
                    B.copy("dve", QTB[64:128, o:o + n], ps[64:128, :])

                def ev_k(ps, o, n):
                    evac(KT[:, o:o + n], ps)
                proj_cm_fixed(LAT, wv, 0, ev_q, nM)
                proj_cm_fixed(LAT, wv, 128, ev_k, nM)
                for t4 in range(4):
                    mb = nM()
                    for tt_ in range(4):
                        t = t4 * 4 + tt_
                        for dc in range(8):
                            B.mm(mb[:, tt_ * 128:(tt_ + 1) * 128], hn[:, dc, t * 128:(t + 1) * 128], wv[:, dc, 256:384], dc == 0, dc == 7)
                    for tt_ in range(4):
                        t = t4 * 4 + tt_
                        B.copy(("dve", "act")[tt_ % 2], Vp[:, t, :].rearrange("p (a b) -> p a b", a=2)[:, :, 0:64],
                               mb[:, tt_ * 128:(tt_ + 1) * 128].rearrange("p (a b) -> p a b", a=2))

                units = []
                if not last:
                    for hh in range(2):
                        for m in range(2):
                            units.append(("ctx", hh, m))
                for hh in range(2):
                    for m in range(16):
                        units.append(("lat", hh, m))
                if DBG["units"] is not None:
                    units = units[:DBG["units"]]

                def unit_S(u):
                    kind, hh, m = units[u]
                    k = uctr[0] + u
                    Sp = Sps[k % 3]
                    if kind == "lat":
                        chunks, _ = tile_chunks(m)
                        q = (QTA, QTB)[hh][:, m * 128:(m + 1) * 128]
                        for i, j in enumerate(chunks):
                            B.mm(Sp[:, i * 128:(i + 1) * 128], KT[:, j * 128:(j + 1) * 128], q, True, True)
                        nl = len(chunks)
                    else:
                        q = (QcA, QcB)[hh][:, m * 128:(m + 1) * 128]
                        nl = 0
                    for cc in range(2):
                        B.mm(Sp[:, (nl + cc) * 128:(nl + cc + 1) * 128], KcT[:, cc * 128:(cc + 1) * 128], q, True, True)

                def unit_rest(u):
                    kind, hh, m = units[u]
                    k = uctr[0] + u
                    Sp = Sps[k % 3]
                    P = PT[k % 3]
                    if kind == "lat":
                        chunks, typ = tile_chunks(m)
                    else:
                        chunks, typ = [], None
                    nl = len(chunks)
                    ns = nl + 2
                    B.act(P[:, 0:ns * 128], Sp[:, 0:ns * 128], AF.Exp, scale=0.125)
                    if nl:
                        eo = EOFF[typ] * 128
                        B.tt("dve", P[:, 0:nl * 128], P[:, 0:nl * 128], Eb[hh][:, eo:eo + nl * 128], ALU.mult)
                    if DBG["parts"] < 3:
                        return
                    Op = Obank[:, (k % 4) * 128:(k % 4 + 1) * 128]
                    vo = 64 * hh
                    for i, j in enumerate(chunks):
                        B.mm(Op, Vp[:, j, vo:vo + 128], P[:, i * 128:(i + 1) * 128], i == 0, False)
                    for cc in range(2):
                        B.mm(Op, Vc[:, cc, vo:vo + 128], P[:, (nl + cc) * 128:(nl + cc + 1) * 128], (nl == 0 and cc == 0), cc == 1)
                    if DBG["parts"] < 4:
                        B.copy("act", rec[0][:, :], Op)
                        return
                    r = rec[k % 2]
                    olo, ohi = (0, 64) if hh == 0 else (64, 128)
                    dlo, dhi = (64, 128) if hh == 0 else (0, 64)
                    B.recip(r[dlo:dhi, :], Op[dlo:dhi, :])
                    dst = (mix if kind == "lat" else mixc)[olo:ohi, 4 + p, m * 128:(m + 1) * 128]
                    B.tt("dve", dst, Op[olo:ohi, :], r[dlo:dhi, :], ALU.mult)

                LOOK = 2
                for u in range(min(LOOK, len(units))):
                    unit_S(u)
                for u in range(len(units)):
                    if u + LOOK < len(units):
                        unit_S(u + LOOK)
                    unit_rest(u)
                    if bgq and (uctr[0] + u) % 6 == 5:
                        bgq.pop(0)(WA, pp[3][:, 512:1024])
                uctr[0] += len(units)
            while bgq:
                bgq.pop(0)(WA, pp[3][:, 512:1024])

        def proj_cm_fixed(S, wv, col0, evac_fn, nb):
            for (o, n) in blocks(S):
                psb = nb()
                for dc in range(8):
                    B.mm(psb[:, :n], wv[:, dc, col0:col0 + 128], S.hn[:, dc, o:o + n], dc == 0, dc == 7)
                evac_fn(psb[:, :n], o, n)

        def conv_phase(l, streams):
            wv = WA.rearrange("p (c n) -> p c n", c=8)
            B.dma("pool", wv, w_in[l].rearrange("(c p) n -> p c n", p=128)[:, :, 256:768], writes=[WA])
            B.dma("pool", pw_t[:].rearrange("p (c n) -> p c n", c=2), w_pw[l].rearrange("(c p) n -> p c n", p=128), writes=[pw_t[:]])
            pwv = pw_t[:].rearrange("p (c n) -> p c n", c=2)
            vpad = av(SCR, 2 * (T + 30)).rearrange("p (c t) -> p c t", c=2)
            vpadc = av(0, 2 * (CT + 30)).rearrange("p (c t) -> p c t", c=2)
            doff = SCR + (2 * (T + 30) * 2) / 1024.0
            doff = math.ceil(doff * 16) / 16.0
            Dg = av(doff, 62 * 128).rearrange("p (c k n) -> p c k n", c=2, k=31)
            cpl = cps[:, l * 70:(l + 1) * 70].rearrange("p (c k) -> p c k", c=2)
            sig = av(2, 512)
            ybf = [av(3, 512), av(4, 512)]
            ysq = [av(5, 512), av(6, 512)]
            y32 = [t32[0], t32[1]]
            z32 = rstd_b
            sil = [sqb[0], sqb[1]]
            for ch in range(2):
                for k in range(31):
                    B.ts("dve", Dg[:, ch, k, :], ident[:], cpl[:, ch, k:k + 1], ALU.mult)
            for S in streams:
                vp = vpad if S is LAT else vpadc
                B.memset("pool", vp[:, :, 0:15], 0.0)
                B.memset("pool", vp[:, :, 15 + S.T:30 + S.T], 0.0)
                for (o, n) in blocks(S):
                    for ch in range(2):
                        pa = nbank()
                        pg = nbank()
                        for dc in range(8):
                            B.mm(pa[:, :n], wv[:, dc, ch * 128:(ch + 1) * 128], S.hn[:, dc, o:o + n], dc == 0, dc == 7)
                        for dc in range(8):
                            B.mm(pg[:, :n], wv[:, dc, 256 + ch * 128:256 + (ch + 1) * 128], S.hn[:, dc, o:o + n], dc == 0, dc == 7)
                        B.act(sig[:, :n], pg[:, :n], AF.Sigmoid)
                        B.tt("dve", vp[:, ch, 15 + o:15 + o + n], pa[:, :n], sig[:, :n], ALU.mult)
            tsets = [dict(ybf=ybf, ysq=ysq, y32=y32),
                     dict(ybf=[av(40, 512), av(41, 512)], ysq=[av(42, 512), av(43, 512)],
                          y32=[av(44, 1024).bitcast(F32), av(46, 1024).bitcast(F32)])]
            work = []
            for S in streams:
                vp = vpad if S is LAT else vpadc
                for (o, n) in blocks(S):
                    work.append((S, vp, o, n))

            def stA(i):
                S, vp, o, n = work[i]
                ts_ = tsets[i % 2]
                for ch in range(2):
                    pc = nbank()
                    for k in range(31):
                        B.mm(pc[:, :n], Dg[:, ch, k, :], vp[:, ch, o + k:o + k + n], k == 0, k == 30)
                    B.act(ts_["ybf"][ch][:, :n], pc[:, :n], AF.Identity, bias=cpl[:, ch, 31:32], scale=1.0)
                    B.act(ts_["ysq"][ch][:, :n], pc[:, :n], AF.Square, bias=cpl[:, ch, 31:32], scale=1.0)
                    B.ts("dve", ts_["y32"][ch][:, :n], pc[:, :n], cpl[:, ch, 31:32], ALU.add)

            def stB(i):
                S, vp, o, n = work[i]
                ts_ = tsets[i % 2]
                pm = nbank()
                pq = nbank()
                for ch in range(2):
                    B.mm(pm[:, :n], ones_ln[:], ts_["ybf"][ch][:, :n], ch == 0, ch == 1)
                for ch in range(2):
                    B.mm(pq[:, :n], ones_ln[:], ts_["ysq"][ch][:, :n], ch == 0, ch == 1)
                B.act(z32[:, :n], pm[:, :n], AF.Square)
                B.tt("dve", z32[:, :n], pq[:, :n], z32[:, :n], ALU.subtract)
                B.act(z32[:, :n], z32[:, :n], AF.Sqrt, bias=LN_EPS_AP, scale=1.0)
                B.recip(z32[:, :n], z32[:, :n])
                for ch in range(2):
                    yv = ts_["y32"][ch]
                    B.tt("dve", yv[:, :n], yv[:, :n], pm[:, :n], ALU.subtract)
                    B.tt("dve", yv[:, :n], yv[:, :n], z32[:, :n], ALU.mult)
                    B.act(sil[ch][:, :n], yv[:, :n], AF.Silu, bias=cpl[:, ch, 33:34], scale=cpl[:, ch, 32:33])

            def stC(i):
                S, vp, o, n = work[i]
                for oc in range(2):
                    po = nbank()
                    for ch in range(2):
                        B.mm(po[:, :n], pwv[:, ch, oc * 128:(oc + 1) * 128], sil[ch][:, :n], ch == 0, ch == 1)
                    B.act(S.mix[:, 2 + oc, o:o + n], po[:, :n], AF.Identity, bias=cpl[:, oc, 34:35], scale=1.0)

            stA(0)
            for i in range(len(work)):
                stB(i)
                if i + 1 < len(work):
                    stA(i + 1)
                stC(i)

        def fourier_phase(l, streams):
            ftv = WB[:, 0:2048].rearrange("p (c n) -> p c n", c=2)
            B.dma("pool", ftv, w_inFT[l].rearrange("(c p) n -> p c n", p=128), writes=[WB[:, 0:2048]])
            B.dma("pool", wf_t[:].rearrange("p (c n) -> p c n", c=2), w_f[l].rearrange("(c p) n -> p c n", p=128), writes=[wf_t[:]])
            wfv = wf_t[:].rearrange("p (c n) -> p c n", c=2)
            csv = csblk[:].rearrange("p (c n) -> p c n", c=2)
            Wp = WA.rearrange("p (c n) -> p c n", c=8)
            for dc in range(8):
                ps = nbank()
                for cc in range(2):
                    B.mm(ps, ftv[:, cc, dc * 128:(dc + 1) * 128], csv[:, cc, :], cc == 0, cc == 1)
                evac(Wp[:, dc, :], ps)
            UU = av(SCR, 16 * 512).rearrange("p (t n) -> p t n", t=16)
            UUc = av(SCR + 16, 2 * 512).rearrange("p (t n) -> p t n", t=2)
            FT = av(SCR + 18, 2 * 256).rearrange("p (c n) -> p c n", c=2)
            for S in streams:
                U = UU if S is LAT else UUc
                nt = S.T // 128
                for t in range(nt):
                    ps = nbank()
                    for dc in range(8):
                        B.mm(ps, S.hn[:, dc, t * 128:(t + 1) * 128], Wp[:, dc, :], dc == 0, dc == 7)
                    evac(U[:, t, :], ps)
            for S in streams:
                U = UU if S is LAT else UUc
                nt = S.T // 128
                njb = S.T // 256
                for jb in range(njb):
                    if S is LAT:
                        dbuf = hn_t[:, (jb % 2) * 8192:(jb % 2 + 1) * 8192]
                        B.dma("sp", dbuf, dft_d[jb], writes=[dbuf])
                        dv = dbuf.rearrange("p (k s j) -> p k s j", k=16, s=2)
                    else:
                        dv = dft256[:].rearrange("p (k s j) -> p k s j", k=2, s=2)
                    for cg in range(2):
                        ps = nbank()
                        first = True
                        for kc in range(nt):
                            for s in range(2):
                                B.mm(ps[:, 0:256], U[:, kc, s * 256 + cg * 128:s * 256 + (cg + 1) * 128], dv[:, kc, s, :],
                                     first, (kc == nt - 1 and s == 1))
                                first = False
                        evac(FT[:, cg, :], ps[:, 0:256])
                    for oc in range(2):
                        ps = nbank()
                        for cg in range(2):
                            B.mm(ps[:, 0:256], wfv[:, cg, oc * 128:(oc + 1) * 128], FT[:, cg, :], cg == 0, cg == 1)
                        evac(S.mix[:, oc, jb * 256:(jb + 1) * 256], ps[:, 0:256])

        def outproj(l, streams):
            wo = AR[:, 32 * 512:48 * 512].rearrange("p (c n) -> p c n", c=8)
            B.dma("pool", wo[:, 0:4, :], w_out[l].rearrange("(c p) n -> p c n", p=128)[:, 0:4, :], writes=[WA])
            B.dma("pool", wo[:, 4:8, :], w_out[l].rearrange("(c p) n -> p c n", p=128)[:, 4:8, :], writes=[WB])
            for S in streams:
                for (o, n) in blocks(S):
                    for oc in range(8):
                        ps = nbank()
                        for kc in range(8):
                            B.mm(ps[:, :n], wo[:, kc, oc * 128:(oc + 1) * 128], S.mix[:, kc, o:o + n], kc == 0, kc == 7)
                        B.stt(S.H[:, oc, o:o + n], ps[:, :n], coef(l, 2, oc, S.col), S.H[:, oc, o:o + n], ALU.mult, ALU.add)

        def mlp(l, streams, after_first_norm):
            slot_off = [8, 16, 24, 32, 40, 56, 64, 72]
            sets = [[5, 6, 7, 0], [1, 2, 3, 4]]
            hid = [av(0, 2048).rearrange("p (f t) -> p f t", f=8), av(4, 2048).rearrange("p (f t) -> p f t", f=8)]
            rl = [sqb[0][:, 0:256], sqb[1][:, 0:256]]
            hctr = 0
            for fb in range(4):
                st = sets[fb % 2]
                w1s = [av(slot_off[st[0]], 4096).rearrange("p (c n) -> p c n", c=8),
                       av(slot_off[st[1]], 4096).rearrange("p (c n) -> p c n", c=8)]
                w2s = [av(slot_off[st[2]], 4096).rearrange("p (c n) -> p c n", c=4),
                       av(slot_off[st[3]], 4096).rearrange("p (c n) -> p c n", c=4)]
                for hhalf in range(2):
                    c0 = fb * 1024 + hhalf * 512
                    B.dma("pool", w1s[hhalf], w1[l].rearrange("(c p) n -> p c n", p=128)[:, :, c0:c0 + 512],
                          writes=[av(slot_off[st[hhalf]], 4096)])
                for hhalf in range(2):
                    r0 = fb * 8 + hhalf * 4
                    B.dma("pool", w2s[hhalf], w2[l].rearrange("(c p) n -> p c n", p=128)[:, r0:r0 + 4, :],
                          writes=[av(slot_off[st[2 + hhalf]], 4096)])
                if fb == 0 and after_first_norm is not None:
                    after_first_norm()
                for S in streams:
                    for (o, n) in blocks(S, 256):
                        hb = hid[hctr % 2]
                        hctr += 1
                        for fc in range(8):
                            ps = nbank()
                            wv = w1s[fc // 4]
                            cc = (fc % 4) * 128
                            for dc in range(8):
                                B.mm(ps[:, :n], wv[:, dc, cc:cc + 128], S.hn[:, dc, o:o + n], dc == 0, dc == 7)
                            r = rl[fc % 2]
                            B.act(r[:, :n], ps[:, :n], AF.Relu)
                            B.tt("dve", hb[:, fc, :n], r[:, :n], r[:, :n], ALU.mult)
                        for oc in range(8):
                            ps = nbank()
                            for fc in range(8):
                                B.mm(ps[:, :n], w2s[fc // 4][:, fc % 4, oc * 128:(oc + 1) * 128], hb[:, fc, :n], fc == 0, fc == 7)
                            B.stt(S.H[:, oc, o:o + n], ps[:, :n], coef(l, 5, oc, S.col), S.H[:, oc, o:o + n], ALU.mult, ALU.add)

        def dump(name, ap_sb, shape):
            d = nc.dram_tensor("dbg_" + name, list(shape), ap_sb.dtype, kind="ExternalOutput").ap()
            B.dma("sp", d, ap_sb, reads=[ap_sb])
            dbg_out[name] = True

        done = False
        for l in range(DEPTH):
            last = l == DEPTH - 1
            streams = [LAT] if last else [CTX, LAT]
            bg = []
            if l == 0:
                for sidx in range(4):
                    ada_slice(0, sidx, (WA, WB)[sidx % 2], nbank())
                ada_coefs(0, [0, 1])
                for sidx in range(4, 12):
                    bg.append(lambda slot, ps, sidx=sidx: ada_slice(0, sidx, slot, ps))
                bg.append(lambda slot, ps: ada_coefs(0, [2, 3, 4, 5]))
                for sidx in range(12):
                    bg.append(lambda slot, ps, sidx=sidx: ada_slice(1, sidx, slot, ps))
                bg.append(lambda slot, ps: ada_coefs(1, [0, 1, 2, 3, 4, 5]))
            if stop_after == f"ada{l}":
                dump("mod", mod_t[:], [128, DEPTH * 96])
                dump("coef", coef_t[:], [128, DEPTH * 96])
                done = True
                break
            norm(CTX, l, 0, 1)
            norm(LAT, l, 0, 1)
            if stop_after == f"norm{l}":
                dump("hn", hn_t[:], [128, 8 * T])
                dump("hnc", hnc_t[:], [128, 8 * CT])
                done = True
                break
            attention(l, last, bg)
            if stop_after == f"attn{l}":
                dump("mix", AR[:, 0:8 * T], [128, 8 * T])
                dump("mixc", mixc_t[:], [128, 8 * CT])
                done = True
                break
            conv_phase(l, streams)
            if stop_after == f"conv{l}":
                dump("mix", AR[:, 0:8 * T], [128, 8 * T])
                dump("mixc", mixc_t[:], [128, 8 * CT])
                done = True
                break
            fourier_phase(l, streams)
            if stop_after == f"four{l}":
                dump("mix", AR[:, 0:8 * T], [128, 8 * T])
                dump("mixc", mixc_t[:], [128, 8 * CT])
                done = True
                break
            outproj(l, streams)
            if stop_after == f"oproj{l}":
                dump("H", H_t[:], [128, 8 * T])
                dump("Hc", Hc_t[:], [128, 8 * CT])
                done = True
                break
            for S in streams:
                norm(S, l, 3, 4)
            mlp(l, streams, None)
            if stop_after == f"mlp{l}":
                dump("H", H_t[:], [128, 8 * T])
                dump("Hc", Hc_t[:], [128, 8 * CT])
                done = True
                break
        if not done:
            norm(LAT, 0, 0, 0, final=True)
        else:
            for c in range(8):
                B.dma("sp", outT.rearrange("(c p) t -> p c t", p=128)[:, c, :], H[:, c, :], reads=[H[:, c, :]])
        B.emit(es)
    _CACHE['trace'] = B.trace
    return nc, sorted(dbg_out.keys()), len(B.ops)


def _bias_tables(rpb):
    kc = np.arange(64)
    qc = np.arange(64)
    cs = np.clip(qc - 8, 0, 48)
    colvalid = (kc[:, None] >= cs[None, :]) & (kc[:, None] < cs[None, :] + 16)
    coloff = np.clip(kc[:, None] - qc[None, :] + 15, 0, 30)
    blocks = []
    specs = [(5, j) for j in range(3, 8)] + [(0, j) for j in range(4)] + [(1, j) for j in range(4)] + \
            [(14, j) for j in range(12, 16)] + [(15, j) for j in range(12, 16)]
    out = np.full((DEPTH, 8, 128, NBLK, 128), NEG, np.float32)
    for bi, (m, j) in enumerate(specs):
        for a in range(2):
            for b in range(2):
                krow = 2 * j + a
                qrow = 2 * m + b
                rs = min(max(qrow - 4, 0), 24)
                if not (rs <= krow < rs + 8):
                    continue
                dr = krow - qrow + 7
                vals = rpb[:, :, dr, :][:, :, coloff]
                vals = np.where(colvalid[None, None], vals, np.float32(NEG))
                out[:, :, a * 64:(a + 1) * 64, bi, b * 64:(b + 1) * 64] = vals
    return np.ascontiguousarray(out.reshape(DEPTH, 8, 128, NBLK * 128))


def _dft_tables():
    k = np.arange(T, dtype=np.int64)
    ang = 2.0 * np.pi * ((k[:, None] * k[None, :]) % T).astype(np.float64) / T
    Cm = (np.cos(ang) / math.sqrt(T)).astype(np.float32)
    Sm = (-np.sin(ang) / math.sqrt(T)).astype(np.float32)
    CS = np.stack([Cm, Sm], axis=0)
    CS = CS.reshape(2, 16, 128, 8, 256)
    dft = np.ascontiguousarray(CS.transpose(3, 2, 1, 0, 4)).reshape(8, 128, 16 * 2 * 256).astype(NPBF)
    k2 = np.arange(CT, dtype=np.int64)
    ang2 = 2.0 * np.pi * ((k2[:, None] * k2[None, :]) % CT).astype(np.float64) / CT
    C2 = (np.cos(ang2) / math.sqrt(CT)).astype(np.float32)
    S2 = (-np.sin(ang2) / math.sqrt(CT)).astype(np.float32)
    CS2 = np.stack([C2, S2], axis=0).reshape(2, 2, 128, 256)
    dft256 = np.ascontiguousarray(CS2.transpose(2, 1, 0, 3)).reshape(128, 2 * 2 * 256).astype(NPBF)
    a = np.arange(64)
    ang3 = 2.0 * np.pi * ((a[:, None] * a[None, :]) % 64) / 64.0
    cb = np.zeros((256, 256), np.float32)
    sbk = np.zeros((256, 256), np.float32)
    for g in range(4):
        cb[g * 64:(g + 1) * 64, g * 64:(g + 1) * 64] = np.cos(ang3) / 8.0
        sbk[g * 64:(g + 1) * 64, g * 64:(g + 1) * 64] = np.sin(ang3) / 8.0
    csblk = np.ascontiguousarray(np.concatenate([cb, sbk], axis=1)).astype(np.float32)
    return dft, dft256, csblk


def host_prep(inp):
    f = lambda a: np.ascontiguousarray(np.asarray(a, dtype=np.float32))
    x, c, ctx, c_ctx = f(inp["x"]), f(inp["c"]), f(inp["ctx"]), f(inp["c_ctx"])
    shared = {}
    shared["ada_w"] = f(inp["ada_w"])
    ab = f(inp["ada_b"]).reshape(DEPTH, 48, 128).transpose(2, 0, 1)
    shared["adab"] = np.ascontiguousarray(np.repeat(ab[:, :, :, None], 2, axis=3)).reshape(128, DEPTH * 96)
    n1 = f(inp["norm1_g"]).reshape(DEPTH, 8, 128).transpose(2, 0, 1)
    n2 = f(inp["norm2_g"]).reshape(DEPTH, 8, 128).transpose(2, 0, 1)
    ng = np.stack([n1, n2], axis=2)
    shared["ng"] = np.ascontiguousarray(np.repeat(ng[..., None], 2, axis=4)).reshape(128, DEPTH * 32)
    shared["fng"] = np.ascontiguousarray(f(inp["final_norm_g"]).reshape(8, 128).T)
    dw = f(inp["conv_dw_w"]).reshape(DEPTH, 31, 2, 128).transpose(3, 0, 2, 1)
    def pv(name):
        return f(inp[name]).reshape(DEPTH, 2, 128).transpose(2, 0, 1)[..., None]
    cp = np.concatenate([dw, pv("conv_dw_b"), pv("conv_norm_g"), pv("conv_norm_b"), pv("conv_pw_b")], axis=3)
    shared["cp"] = np.ascontiguousarray(cp).reshape(128, DEPTH * 70)
    w_in = f(inp["w_in"])
    order = list(range(0, 768))
    for p in range(4):
        for part in range(3):
            base = 768 + part * 512 + p * 128
            order += list(range(base, base + 128))
    shared["w_in"] = np.ascontiguousarray(w_in[:, :, order])
    shared["w_inFT"] = np.ascontiguousarray(w_in[:, :, :256].transpose(0, 2, 1))
    shared["w_f"] = f(inp["w_fourier"])
    shared["w_pw"] = f(inp["conv_pw_w"])
    shared["w_out"] = f(inp["w_out"])
    shared["w1"] = f(inp["mlp_w1"])
    shared["w2"] = f(inp["mlp_w2"])
    dft, dft256, csblk = _dft_tables()
    shared["dft"] = dft
    shared["dft256"] = dft256
    shared["csblk"] = csblk
    shared["ident"] = np.eye(128, dtype=np.float32).astype(NPBF)
    shared["biasT"] = _bias_tables(f(inp["na_rpb"]))
    maps = []
    cc = c_ctx.reshape(8, 128).T
    for b in range(NCORES):
        m = dict(shared)
        m["xT"] = np.ascontiguousarray(x[b].T)
        m["cxT"] = np.ascontiguousarray(ctx[b].T)
        cb = c[b].reshape(8, 128).T
        m["cvec"] = np.ascontiguousarray(np.stack([cb, cc], axis=2)).reshape(128, 16)
        maps.append(m)
    return maps


def kernel(**inputs):
    maps = host_prep(inputs)
    if "nc" not in _CACHE:
        _CACHE["nc"] = build_program()[0]
    nc = _CACHE["nc"]
    res = run_bass_kernel_spmd(nc, maps, core_ids=list(range(NCORES)))
    out = np.stack([np.ascontiguousarray(res.results[b]["outT"].T) for b in range(NCORES)], axis=0)
    return out.astype(np.float32)
```
